# Optimizing a Trainium2 kernel written in Bass

```python
import math
import jax, jax.numpy as jnp
from jax import lax
import numpy as np

D_MODEL = 1024
BATCH = 8
SEQ = 4096
DEPTH = 1

GRID_W = 64
CTX_LEN = 256
D_HYENA = 512
D_GLA = D_MODEL - D_HYENA
GLA_HEADS = 4
GLA_DK = D_GLA // 2
GLA_DV = D_GLA
GLA_HEAD_DK = GLA_DK // GLA_HEADS
GLA_HEAD_DV = GLA_DV // GLA_HEADS
GATE_RANK = 16
GATE_NORMALIZER = 16.0
GLA_CHUNK = 64
SHORT_CONV = 3
FILTER_EMB = 33
FILTER_BANDS = (FILTER_EMB - 1) // 2
FILTER_HIDDEN = 64
FAST_DECAY_PCT = 0.3
SLOW_DECAY_PCT = 1.5
DECAY_TARGET = 1e-2
D_FF = 4 * D_MODEL
EPS = 1e-6
PROJ_WIDTHS = (3 * D_HYENA, GLA_DK, GLA_DK, GLA_DV, GLA_DV, GATE_RANK, GATE_RANK)
PROJ_SPLITS = tuple(int(s) for s in np.cumsum(PROJ_WIDTHS)[:-1])
N_IN = sum(PROJ_WIDTHS)

kernel_name = 'hybrid_hyena_gla_prefix_dit'


def _rms_norm(x, g):
    xf = x.astype(jnp.float32)
    y = xf * lax.rsqrt(jnp.mean(xf * xf, axis=-1, keepdims=True) + EPS)
    return (y * g.astype(jnp.float32)).astype(x.dtype)


def _sq_relu_mlp(h, w1, w2):
    return jnp.square(jax.nn.relu(h @ w1)) @ w2


def _short_conv(u, w, b):
    t = u.shape[-2]
    pad = SHORT_CONV // 2
    up = jnp.pad(u, [(0, 0)] * (u.ndim - 2) + [(pad, pad), (0, 0)])
    out = b
    for j in range(SHORT_CONV):
        out = out + w[j] * up[..., j:j + t, :]
    return out


def _hyena_spectrum(length, w1, b1, w2, b2, w3, b3, freq, wout):
    f32 = jnp.float32
    t = jnp.linspace(0.0, 1.0, length, dtype=f32)[:, None]
    w = 2.0 * math.pi * jnp.arange(length, dtype=f32) / length
    f = jnp.linspace(1e-4, FILTER_BANDS - 1, FILTER_BANDS, dtype=f32)
    ang = w[:, None] * f[None, :]
    z = jnp.concatenate([t, jnp.cos(ang), -jnp.sin(ang)], axis=-1)
    fr = freq.astype(f32)
    hdn = jnp.sin(fr * (z @ w1.astype(f32) + b1.astype(f32)))
    hdn = jnp.sin(fr * (hdn @ w2.astype(f32) + b2.astype(f32)))
    hdn = jnp.sin(fr * (hdn @ w3.astype(f32) + b3.astype(f32)))
    h = (hdn @ wout.astype(f32)).reshape(length, 2, D_HYENA)
    min_decay = math.log(DECAY_TARGET) / SLOW_DECAY_PCT
    max_decay = math.log(DECAY_TARGET) / FAST_DECAY_PCT
    deltas = jnp.abs(jnp.linspace(min_decay, max_decay, D_HYENA, dtype=f32))
    h = h * jnp.exp(-t * deltas)[:, None, :]
    kern = jnp.concatenate([h[:, 0], jnp.zeros((1, D_HYENA), f32), h[:0:-1, 1]], axis=0)
    kern = kern * lax.rsqrt(jnp.sum(kern * kern, axis=0, keepdims=True) + EPS)
    return jnp.fft.rfft(kern, axis=0)


def _long_conv(u, spec, bias):
    length = u.shape[1]
    uf = u.astype(jnp.float32)
    y = jnp.fft.irfft(jnp.fft.rfft(uf, n=2 * length, axis=1) * spec, n=2 * length, axis=1)[:, :length]
    return (y + uf * bias.astype(jnp.float32)).astype(u.dtype)


def _hyena_gate_conv(u, spec, bias):
    x0, x1, v = jnp.split(u, 3, axis=-1)
    return x0 * _long_conv(x1 * v, spec, bias)


def _to_heads(z, n_heads):
    b, t, w = z.shape
    return z.reshape(b, t, n_heads, w // n_heads).transpose(0, 2, 1, 3)


def _log_gate(gr, w, b):
    lg = jax.nn.log_sigmoid(gr.astype(jnp.float32) @ w.astype(jnp.float32) + b.astype(jnp.float32))
    return _to_heads(lg / GATE_NORMALIZER, GLA_HEADS)


def _gla_chunked(q, k, v, lg, s0):
    bsz, nh, t, dk = q.shape
    dv = v.shape[-1]
    n = t // GLA_CHUNK
    q = q.reshape(bsz, nh, n, GLA_CHUNK, dk)
    k = k.reshape(bsz, nh, n, GLA_CHUNK, dk)
    lg = lg.reshape(bsz, nh, n, GLA_CHUNK, dk)
    v = v.reshape(bsz, nh, n, GLA_CHUNK, dv)
    b = jnp.cumsum(lg, axis=3)
    b_last = b[:, :, :, -1:, :]
    q_in = q * jnp.exp(b)
    k_in = k * jnp.exp(-b)
    k_end = k * jnp.exp(b_last - b)
    mask = jnp.tril(jnp.ones((GLA_CHUNK, GLA_CHUNK), dtype=bool))
    att = jnp.where(mask, jnp.einsum('bhncd,bhnsd->bhncs', q_in, k_in), 0.0)
    o_intra = jnp.einsum('bhncs,bhnse->bhnce', att, v)
    ds = jnp.einsum('bhnsd,bhnse->nbhde', k_end, v)
    decay = jnp.moveaxis(jnp.exp(b_last[:, :, :, 0, :]), 2, 0)

    def step(s, inp):
        d, dsn = inp
        return d[..., None] * s + dsn, s

    s_final, s_start = lax.scan(step, s0, (decay, ds))
    o_inter = jnp.einsum('bhncd,nbhde->bhnce', q_in, s_start)
    return (o_intra + o_inter).reshape(bsz, nh, t, dv), s_final


def _gla_final_state(k, v, lg):
    b = jnp.cumsum(lg, axis=2)
    return jnp.einsum('bhtd,bhte->bhde', k * jnp.exp(b[:, :, -1:] - b), v)


def _rev(a):
    return jnp.flip(a, axis=2)


def _gla_bidir(q, k, v, lg_f, lg_b, s_f0, s_b0):
    o_f, s_f = _gla_chunked(q, k, v, lg_f, s_f0)
    o_b, s_b = _gla_chunked(_rev(q), _rev(k), _rev(v), _rev(lg_b), s_b0)
    return o_f + _rev(o_b), s_f, s_b


def _gla_heads(q, k, v):
    f32 = jnp.float32
    qh = _to_heads(q.astype(f32), GLA_HEADS) * (GLA_HEAD_DK ** -0.5)
    return qh, _to_heads(k.astype(f32), GLA_HEADS), _to_heads(v.astype(f32), GLA_HEADS)


def _gla_output(o, og, g_norm):
    on = o * lax.rsqrt(jnp.mean(o * o, axis=-1, keepdims=True) + EPS) * g_norm.astype(jnp.float32)
    bsz, _, t, _ = o.shape
    y = on.transpose(0, 2, 1, 3).reshape(bsz, t, GLA_DV)
    return (y * jax.nn.silu(og.astype(jnp.float32))).astype(og.dtype)


def setup_inputs(seed: int = 0) -> dict:
    key = jax.random.key(seed)
    ks = jax.random.split(key, 29)
    D = D_MODEL
    L = DEPTH

    def nrm(i, shape, scale):
        return scale * jax.random.normal(ks[i], shape, jnp.float32)

    return {
        'x': nrm(0, (BATCH, SEQ, D), 1.0),
        'c': nrm(1, (BATCH, D), 1.0),
        'ctx': nrm(2, (BATCH, CTX_LEN, D), 1.0),
        'c_ctx': nrm(3, (D,), 1.0),
        'w_ada': nrm(4, (L, D, 6 * D), 0.5 * D ** -0.5),
        'b_ada': nrm(5, (L, 6 * D), 0.02),
        'norm_mix': 1.0 + nrm(6, (L, D), 0.05),
        'norm_mlp': 1.0 + nrm(7, (L, D), 0.05),
        'w_in': nrm(8, (L, D, N_IN), D ** -0.5),
        'conv_w': nrm(9, (L, SHORT_CONV, 3 * D_HYENA), SHORT_CONV ** -0.5),
        'conv_b': nrm(10, (L, 3 * D_HYENA), 0.02),
        'filt_w1': nrm(11, (L, FILTER_EMB, FILTER_HIDDEN), FILTER_EMB ** -0.5),
        'filt_b1': nrm(12, (L, FILTER_HIDDEN), 0.1),
        'filt_w2': nrm(13, (L, FILTER_HIDDEN, FILTER_HIDDEN), FILTER_HIDDEN ** -0.5),
        'filt_b2': nrm(14, (L, FILTER_HIDDEN), 0.1),
        'filt_w3': nrm(15, (L, FILTER_HIDDEN, FILTER_HIDDEN), FILTER_HIDDEN ** -0.5),
        'filt_b3': nrm(16, (L, FILTER_HIDDEN), 0.1),
        'filt_freq': 1.0 + nrm(17, (L, FILTER_HIDDEN), 0.1),
        'filt_wout': nrm(18, (L, FILTER_HIDDEN, 2 * D_HYENA), FILTER_HIDDEN ** -0.5),
        'hyena_bias': nrm(19, (L, D_HYENA), 0.5),
        'gk_w_fwd': nrm(20, (L, GATE_RANK, GLA_DK), GATE_RANK ** -0.5),
        'gk_b_fwd': nrm(21, (L, GLA_DK), 0.1),
        'gk_w_bwd': nrm(22, (L, GATE_RANK, GLA_DK), GATE_RANK ** -0.5),
        'gk_b_bwd': nrm(23, (L, GLA_DK), 0.1),
        'gla_norm': 1.0 + nrm(24, (L, GLA_HEAD_DV), 0.05),
        'w_out': nrm(25, (L, D, D), D ** -0.5),
        'w_mlp1': nrm(26, (L, D, D_FF), D ** -0.5),
        'w_mlp2': nrm(27, (L, D_FF, D), D_FF ** -0.5),
        'norm_final': 1.0 + nrm(28, (D,), 0.05),
    }


def reference(x, c, ctx, c_ctx, w_ada, b_ada, norm_mix, norm_mlp, w_in, conv_w, conv_b,
              filt_w1, filt_b1, filt_w2, filt_b2, filt_w3, filt_b3, filt_freq, filt_wout,
              hyena_bias, gk_w_fwd, gk_b_fwd, gk_w_bwd, gk_b_bwd, gla_norm, w_out,
              w_mlp1, w_mlp2, norm_final):
    bsz, seq, _ = x.shape
    rows = seq // GRID_W
    ctx_len = ctx.shape[1]
    for l in range(DEPTH):
        last = l == DEPTH - 1
        filt = (filt_w1[l], filt_b1[l], filt_w2[l], filt_b2[l], filt_w3[l], filt_b3[l], filt_freq[l], filt_wout[l])
        mod = jax.nn.silu(c) @ w_ada[l] + b_ada[l]
        mod_c = jax.nn.silu(c_ctx) @ w_ada[l] + b_ada[l]
        sh1, sc1, g1, sh2, sc2, g2 = jnp.split(mod[:, None, :], 6, axis=-1)
        csh1, csc1, cg1, csh2, csc2, cg2 = jnp.split(mod_c, 6)

        h_lat = _rms_norm(x, norm_mix[l]) * (1.0 + sc1) + sh1
        h_ctx = _rms_norm(ctx, norm_mix[l]) * (1.0 + csc1) + csh1
        hy_l, q_l, k_l, v_l, og_l, gf_l, gb_l = jnp.split(h_lat @ w_in[l], PROJ_SPLITS, axis=-1)
        hy_c, q_c, k_c, v_c, og_c, gf_c, gb_c = jnp.split(h_ctx @ w_in[l], PROJ_SPLITS, axis=-1)

        qh_c, kh_c, vh_c = _gla_heads(q_c, k_c, v_c)
        lgf_c = _log_gate(gf_c, gk_w_fwd[l], gk_b_fwd[l])
        lgb_c = _log_gate(gb_c, gk_w_bwd[l], gk_b_bwd[l])
        if last:
            s_f = _gla_final_state(kh_c, vh_c, lgf_c)
            s_b = _gla_final_state(_rev(kh_c), _rev(vh_c), _rev(lgb_c))
        else:
            zeros = jnp.zeros((bsz, GLA_HEADS, GLA_HEAD_DK, GLA_HEAD_DV), jnp.float32)
            o_c, s_f, s_b = _gla_bidir(qh_c, kh_c, vh_c, lgf_c, lgb_c, zeros, zeros)

        qh_l, kh_l, vh_l = _gla_heads(q_l, k_l, v_l)
        lgf_l = _log_gate(gf_l, gk_w_fwd[l], gk_b_fwd[l])
        lgb_l = _log_gate(gb_l, gk_w_bwd[l], gk_b_bwd[l])
        o_l, _, _ = _gla_bidir(qh_l, kh_l, vh_l, lgf_l, lgb_l, s_f, s_b)
        y_gla_l = _gla_output(o_l, og_l, gla_norm[l])

        u_l = _short_conv(hy_l.reshape(bsz, rows, GRID_W, 3 * D_HYENA), conv_w[l], conv_b[l])
        u_l = u_l.reshape(bsz, seq, 3 * D_HYENA)
        y_hy_l = _hyena_gate_conv(u_l, _hyena_spectrum(seq, *filt), hyena_bias[l])

        x = x + g1 * (jnp.concatenate([y_hy_l, y_gla_l], axis=-1) @ w_out[l])
        h2 = _rms_norm(x, norm_mlp[l]) * (1.0 + sc2) + sh2
        x = x + g2 * _sq_relu_mlp(h2, w_mlp1[l], w_mlp2[l])

        if not last:
            u_c = _short_conv(hy_c, conv_w[l], conv_b[l])
            y_hy_c = _hyena_gate_conv(u_c, _hyena_spectrum(ctx_len, *filt), hyena_bias[l])
            y_gla_c = _gla_output(o_c, og_c, gla_norm[l])
            ctx = ctx + cg1 * (jnp.concatenate([y_hy_c, y_gla_c], axis=-1) @ w_out[l])
            h2c = _rms_norm(ctx, norm_mlp[l]) * (1.0 + csc2) + csh2
            ctx = ctx + cg2 * _sq_relu_mlp(h2c, w_mlp1[l], w_mlp2[l])
    return _rms_norm(x, norm_final)
```

```python
import math
import os
GSTOP = int(os.environ.get('GLA_STOP', '99'))
SAME_ENGINE_RELAX = os.environ.get('RELAX_SAME') is not None
from contextlib import ExitStack
import numpy as np
import ml_dtypes
import concourse.bass as bass
import concourse.mybir as mybir
from concourse.bass_utils import run_bass_kernel_spmd

F32 = mybir.dt.float32
BF16 = mybir.dt.bfloat16
AF = mybir.ActivationFunctionType
ALU = mybir.AluOpType
NPBF = ml_dtypes.bfloat16

T = 4096
D = 1024
EPS = 1e-6
NFFT = 8192
PI = math.pi
MAGIC = 12582912.0
DEBUG = False


class Sched:
    ENGS = ("pe", "act", "dve", "pool", "sp")
    NDMA = {"sp": 16, "pool": 8, "act": 2, "dve": 1, "pe": 1}

    def __init__(self):
        self.ops = []
        self.last_w = {}
        self.readers = {}
        self.dcnt = {}

    @classmethod
    def sem_names(cls):
        names = []
        for e in cls.ENGS:
            names.append(e)
            for j in range(cls.NDMA[e]):
                names.append(f"{e}_dma{j}")
        return names

    def add(self, eng, fn, r=(), w=(), dma=False, n=1, name=""):
        assert n == 1
        deps = set()
        raw = set()
        for k in r:
            if k in self.last_w:
                deps.add(self.last_w[k])
                raw.add(self.last_w[k])
        for k in w:
            if k in self.last_w:
                deps.add(self.last_w[k])
            deps |= self.readers.get(k, set())
        idx = len(self.ops)
        if dma:
            c = self.dcnt.get(eng, 0)
            self.dcnt[eng] = c + 1
            sem = f"{eng}_dma{c % self.NDMA[eng]}"
        else:
            sem = eng
        self.ops.append(dict(eng=eng, fn=fn, deps=sorted(deps), raw=raw, dma=dma, n=n, name=name, sem=sem))
        for k in r:
            self.readers.setdefault(k, set()).add(idx)
        for k in w:
            self.last_w[k] = idx
            self.readers[k] = set()
        return idx

    def barrier(self):
        last = {}
        for i, op in enumerate(self.ops):
            if op["fn"] is not None:
                last[op["sem"]] = i
        deps = sorted(last.values())
        for e in self.ENGS:
            self.ops.append(dict(eng=e, fn=None, deps=deps, raw=set(deps), dma=False, n=1, name="barrier", sem=e))
        self.last_w = {}
        self.readers = {}

    class _Fake:
        class _I:
            def then_inc(self, *a, **k):
                return self

        def __init__(self):
            self.calls = []

        def __getattr__(self, name):
            def f(*args, **kw):
                out = kw.get("out", args[0] if args else None)
                n = 1
                try:
                    for d_ in out.shape[1:]:
                        n *= int(d_)
                except Exception:
                    n = 256
                self.calls.append((name, n))
                return Sched._Fake._I()
            return f

    def _cost(self, op):
        fk = Sched._Fake()
        try:
            op["fn"](fk)
        except Exception:
            return 1.0, 0.0
        eng = op["eng"]
        c = 0.0
        lat = 0.0
        for name, n in fk.calls:
            if name == "dma_start":
                c += 0.15
                lat = max(lat, 2.5 + n * 3.0 / 150e3)
            elif eng == "pe":
                c += max(64, n) / 1400.0 + 0.04
            elif eng == "act":
                c += n / 1100.0 + 0.25
            elif eng == "dve":
                c += n / 900.0 + 0.12
            else:
                c += n / 450.0 + 0.25
        return max(c, 0.05), lat

    def reschedule(self):
        ops = self.ops
        n = len(ops)
        new_order = []
        i = 0
        XLAT = float(os.environ.get('XLAT', '2.5'))
        while i < n:
            if ops[i]["fn"] is None:
                new_order.append(i)
                i += 1
                continue
            j = i
            while j < n and ops[j]["fn"] is not None:
                j += 1
            seg = list(range(i, j))
            segset = set(seg)
            cost = {}
            lat = {}
            for k in seg:
                cost[k], lat[k] = self._cost(ops[k])
            succ = {k: [] for k in seg}
            indeg = {k: 0 for k in seg}
            for k in seg:
                for d_ in ops[k]["deps"]:
                    if d_ in segset:
                        succ[d_].append(k)
                        indeg[k] += 1
            prio = {}
            for k in reversed(seg):
                prio[k] = cost[k] + lat[k] + max([prio[x_] for x_ in succ[k]], default=0.0)
            eng_free = {}
            fin = {}
            ready = [k for k in seg if indeg[k] == 0]
            sched = []
            while ready:
                best = None
                for k in ready:
                    e_ = ops[k]["eng"]
                    t = eng_free.get(e_, 0.0)
                    for d_ in ops[k]["deps"]:
                        if d_ in segset:
                            t = max(t, fin[d_] + (0.0 if ops[d_]["eng"] == e_ and not ops[d_]["dma"] else XLAT))
                    key = (t, -prio[k], k)
                    if best is None or key < best[0]:
                        best = (key, k, t)
                _, k, t = best
                ready.remove(k)
                e_ = ops[k]["eng"]
                eng_free[e_] = t + cost[k]
                fin[k] = t + cost[k] + lat[k]
                sched.append((t, k))
                for x_ in succ[k]:
                    indeg[x_] -= 1
                    if indeg[x_] == 0:
                        ready.append(x_)
            assert len(sched) == len(seg)
            sched.sort()
            new_order.extend(k for _, k in sched)
            i = j
        remap = {old: new for new, old in enumerate(new_order)}
        new_ops = []
        for old in new_order:
            op = ops[old]
            op["deps"] = sorted(remap[d_] for d_ in op["deps"])
            op["raw"] = set(remap[d_] for d_ in op["raw"])
            new_ops.append(op)
        self.ops = new_ops

    def emit(self, sems, block):
        if os.environ.get("NO_RESCHED") is None:
            self.reschedule()
        run = {}
        for op in self.ops:
            s = op["sem"]
            if op["fn"] is None:
                continue
            op["prev"] = run.get(s, 0)
            run[s] = run.get(s, 0) + (16 if op["dma"] else 1)
            op["val"] = run[s]
        final = dict(run)
        handles = dict(pe="tensor", act="scalar", dve="vector", pool="gpsimd", sp="sync")

        def make(engname):
            def body(e):
                waited = {}
                for op in self.ops:
                    if op["eng"] != engname:
                        continue
                    for d in op["deps"]:
                        dop = self.ops[d]
                        if dop["fn"] is None:
                            continue
                        if dop["eng"] == "pe" and engname == "pe" and not dop["dma"]:
                            continue
                        if (SAME_ENGINE_RELAX and dop["eng"] == engname and not dop["dma"] and not op["dma"]
                                and d not in op["raw"]):
                            continue
                        s, v = dop["sem"], dop["val"]
                        if waited.get(s, 0) >= v:
                            continue
                        e.wait_ge(sems[s], v)
                        waited[s] = v
                    if op["fn"] is None:
                        continue
                    if op["dma"] and op["prev"] > waited.get(op["sem"], 0):
                        e.wait_ge(sems[op["sem"]], op["prev"])
                        waited[op["sem"]] = op["prev"]
                    res = op["fn"](e)
                    if isinstance(res, (list, tuple)):
                        res = res[-1]
                    res.then_inc(sems[op["sem"]], 16 if op["dma"] else 1)
                if engname == "sp":
                    for s, v in final.items():
                        if v > 0:
                            e.wait_ge(sems[s], v)
            return body

        for engname in self.ENGS:
            getattr(block, handles[engname])(make(engname))


def _bf(a):
    return np.ascontiguousarray(a.astype(np.float32)).astype(NPBF)


_CONST_CACHE = {}


def host_constants():
    if _CONST_CACHE:
        return _CONST_CACHE
    c = {}
    r128 = np.arange(128, dtype=np.float64)[:, None]
    F1 = np.zeros((128, 128))
    F1[:, :65] = np.cos(2 * np.pi * r128 * np.arange(65)[None, :] / 128)
    F1[:, 65:] = -np.sin(2 * np.pi * r128 * np.arange(1, 64)[None, :] / 128)
    c["F1"] = _bf(F1)
    w = np.arange(64, dtype=np.float64)[:, None]
    k2 = np.arange(64, dtype=np.float64)[None, :]
    Eb = np.zeros((128, 65, 128))
    for s in range(65):
        ang = 2 * np.pi * w * (s + 128 * k2) / NFFT
        Mr, Mi = np.cos(ang), -np.sin(ang)
        Eb[:64, s, :64] = Mr
        Eb[:64, s, 64:] = Mi
        Eb[64:, s, :64] = -Mi
        Eb[64:, s, 64:] = Mr
    c["Ebig"] = _bf(Eb)
    ang = 2 * np.pi * np.arange(64)[None, :] * np.arange(64, dtype=np.float64)[:, None] / 64
    Vr, Vi = np.cos(ang), np.sin(ang)
    Ci = np.zeros((128, 128))
    Ci[:64, :64] = Vr
    Ci[:64, 64:] = Vi
    Ci[64:, :64] = -Vi
    Ci[64:, 64:] = Vr
    c["Cinv"] = _bf(Ci)
    U = np.zeros((65, 64, 2, 64))
    sl = np.arange(65, dtype=np.float64)[:, None]
    mb = np.arange(64, dtype=np.float64)[None, :]
    wt = np.where((sl == 0) | (sl == 64), 1.0, 2.0) / NFFT
    for a in range(64):
        ang = 2 * np.pi * (mb * sl / 128 + a * sl / NFFT)
        U[:, a, 0, :] = wt * np.cos(ang)
        U[:, a, 1, :] = -wt * np.sin(ang)
    c["U"] = _bf(U)
    L = T
    f32 = np.float32
    tlin = np.linspace(0.0, 1.0, L, dtype=f32)
    wv = (f32(2.0 * math.pi) * np.arange(L, dtype=f32) / f32(L)).astype(f32)
    fb = np.linspace(1e-4, 15, 16, dtype=f32)
    angm = (wv[:, None] * fb[None, :]).astype(f32)
    emb = np.concatenate([tlin[:, None], np.cos(angm), -np.sin(angm)], axis=-1).astype(f32)
    j = np.zeros(NFFT, dtype=np.int64)
    j[:L] = np.arange(L)
    j[L] = 0
    j[L + 1:] = L - np.arange(1, L)
    emb_all = emb[j].T.copy()
    c["emb"] = np.ascontiguousarray(emb_all.astype(f32))
    tt = tlin[j].astype(np.float64)
    tt[L] = 1.0e4
    c["ntt"] = np.ascontiguousarray((-tt).reshape(128, 64).astype(f32))
    min_decay = math.log(1e-2) / 1.5
    max_decay = math.log(1e-2) / 0.3
    deltas = np.abs(np.linspace(min_decay, max_decay, 512, dtype=f32))
    c["delta"] = np.ascontiguousarray(deltas.reshape(1, 512).astype(f32))
    s_ = np.arange(128)[:, None]
    t_ = np.arange(128)[None, :]
    c["triF"] = np.where(s_ <= t_, -1.0 / 16.0, 0.0).astype(f32)
    c["triB"] = np.where(s_ >= t_, -1.0 / 16.0, 0.0).astype(f32)
    c["triFb"] = _bf(c["triF"])
    c["triBb"] = _bf(c["triB"])
    c["maskF"] = np.where(s_ <= t_, 1.0, 0.0).astype(f32)
    c["maskB"] = np.where(s_ >= t_, 1.0, 0.0).astype(f32)
    c["ident"] = _bf(np.eye(128))
    _CONST_CACHE.update(c)
    return c


IN_SPECS = [
    ("x", [T, D], F32), ("ctx", [256, D], F32), ("cc", [128, 8, 2], F32),
    ("w_ada", [D, 6 * D], F32), ("b_adaT", [128, 48], F32), ("b_ada", [1, 6 * D], F32),
    ("nmixT", [128, 8], F32), ("nmlpT", [128, 8], F32), ("nfin", [1, D], F32),
    ("w_in", [D, 3104], F32), ("convwT", [128, 12, 3], F32), ("convbT", [128, 12], F32),
    ("fw1", [33, 64], F32), ("fw2", [64, 64], F32), ("fw3", [64, 64], F32), ("fwout", [64, 1024], F32),
    ("fcol", [64, 4], F32), ("hbiasT", [128, 4], F32),
    ("gkw", [48, 256], F32), ("gkb", [1, 512], F32), ("gnT", [128, 1], F32),
    ("w_out", [D, D], F32), ("w_mlp1", [D, 4 * D], F32), ("w_mlp2", [4 * D, D], F32),
    ("F1", [128, 128], BF16), ("Ebig", [128, 65, 128], BF16), ("Cinv", [128, 128], BF16),
    ("U", [65, 64, 2, 64], BF16), ("emb", [33, NFFT], F32), ("ntt", [128, 64], F32),
    ("delta", [1, 512], F32), ("triF", [128, 128], F32), ("triB", [128, 128], F32), ("triFb", [128, 128], BF16), ("triBb", [128, 128], BF16),
    ("maskF", [128, 128], F32), ("maskB", [128, 128], F32), ("ident", [128, 128], BF16),
]


def build(debug=False):
    nc = bass.Bass("TRN2", target_bir_lowering=False)
    d = {}
    for name, shape, dt in IN_SPECS:
        d[name] = nc.dram_tensor(name, list(shape), dt, kind="ExternalInput").ap()
    out_d = nc.dram_tensor("out", [T, D], F32, kind="ExternalOutput").ap()
    skind = "ExternalOutput" if debug else "Internal"
    hT_d = nc.dram_tensor("hT_d", [D, T], BF16, kind=skind).ap()
    yT_d = nc.dram_tensor("yT_d", [D, T], BF16, kind=skind).ap()

    S = Sched()
    st = ExitStack()
    with st:
        ARENA_F = 53000
        arena_t = st.enter_context(nc.sbuf_tensor("arena", [128, ARENA_F], F32))
        banks = [st.enter_context(nc.psum_tensor(f"bank{i}", [128, 512], F32)) for i in range(8)]
        sems = {}
        for nm_ in Sched.sem_names():
            sems[nm_] = st.enter_context(nc.semaphore("s_" + nm_))
        block = st.enter_context(nc.Block())

        def V(off, shape, dt=F32):
            esz = 4 if dt == F32 else 2
            nfree = int(np.prod(shape[1:]))
            nb = nfree * esz
            assert off % 4 == 0 and nb % 4 == 0, (off, shape)
            assert off + nb <= ARENA_F * 4, (off, shape)
            ap = arena_t[0:shape[0], off // 4:(off + nb) // 4]
            if dt != F32:
                ap = ap.bitcast(dt)
            if len(shape) == 3:
                ap = ap.rearrange("p (a b) -> p a b", a=shape[1])
            elif len(shape) == 4:
                ap = ap.rearrange("p (a b c) -> p a b c", a=shape[1], b=shape[2])
            return ap

        class Alloc:
            def __init__(self, base, limit=ARENA_F * 4):
                self.o = base
                self.limit = limit

            def __call__(self, shape, dt=F32):
                esz = 4 if dt == F32 else 2
                nb = int(np.prod(shape[1:])) * esz
                nb = (nb + 31) // 32 * 32
                v = V(self.o, shape, dt)
                self.o += nb
                assert self.o <= self.limit, ("arena overflow", self.o)
                return v

        pbank = [0]
        reserved = set()

        def nbank():
            while True:
                b = pbank[0]
                pbank[0] = (b + 1) % 8
                if b not in reserved:
                    return b

        def PB(b, dt=F32):
            ap = banks[b][:]
            return ap if dt == F32 else ap.bitcast(dt)

        def op(eng, fn, r=(), w=(), dma=False, n=1, name=""):
            return S.add(eng, fn, r=r, w=w, dma=dma, n=n, name=name)

        def dma_in(out_ap, in_ap, w, r=(), eng="sp"):
            op(eng, lambda e, o=out_ap, i=in_ap: e.dma_start(out=o, in_=i), r=r, w=w, dma=True)

        PA = Alloc(0, 16384)
        ident = PA([128, 128], BF16)
        ones_bf = PA([128, 128], BF16)
        onesf = PA([128, 128], F32)
        nfrow = PA([128, D], F32)
        PA_g1_off = PA.o
        g1row = PA([128, D], F32)
        g2row = PA([128, D], F32)
        msc1 = PA([128, 8]); msh1 = PA([128, 8]); csc1 = PA([128, 8]); csh1 = PA([128, 8])
        msc2 = PA([128, 8]); msh2 = PA([128, 8])
        epsc = PA([128, 1])
        PBA = Alloc(16384, 32768)
        hctxT = PBA([128, 8, 256], BF16)
        convwT = PBA([128, 12, 3]); convbT = PBA([128, 12]); hbiasT = PBA([128, 4]); gnT = PBA([128, 1])
        gkw = PBA([48, 256]); gkb = PBA([1, 512])
        fw1 = PBA([33, 64]); fw2 = PBA([64, 64]); fw3 = PBA([64, 64]); fwout = PBA([64, 1024])
        fcol = PBA([64, 4]); fsc = PBA([64, 8])
        nmixT = PBA([128, 8]); nmlpT = PBA([128, 8])
        modT = PBA([128, 48, 2])
        Sst = [PBA([128, 2, 128]) for _ in range(2)]
        PH0 = 32768

        dma_in(ident, d["ident"], ["ident"])
        op("pool", lambda e: e.memset(ones_bf, 1.0), w=["ones_bf"])
        op("pool", lambda e: e.memset(onesf, 1.0), w=["onesf"])
        op("pool", lambda e: e.memset(epsc, EPS), w=["epsc"])
        for nm, t_ in [("convwT", convwT), ("convbT", convbT), ("hbiasT", hbiasT), ("gnT", gnT), ("gkw", gkw),
                       ("gkb", gkb), ("fw1", fw1), ("fw2", fw2), ("fw3", fw3), ("fwout", fwout), ("fcol", fcol),
                       ("nmixT", nmixT), ("nmlpT", nmlpT)]:
            dma_in(t_, d[nm], [nm])
        dma_in(nfrow, d["nfin"][0:1, :].partition_broadcast(128), ["nfrow"])
        op("dve", lambda e: e.tensor_scalar(out=fsc[:, 0:3], in0=fcol[:, 0:3], scalar1=fcol[:, 3:4], scalar2=None,
                                            op0=ALU.mult), r=["fcol"], w=["fsc"])

        A0 = Alloc(PH0)
        ccT = A0([128, 8, 2]); scT = A0([128, 8, 2]); scTb = A0([128, 8, 2], BF16); screpb = A0([128, 8, 128], BF16)
        badaT = A0([128, 48]); brow = A0([128, 2, D])
        wab = [A0([128, 8, 1024]) for _ in range(2)]
        wabb = [A0([128, 8, 1024], BF16) for _ in range(2)]
        A1 = Alloc(A0.o)
        xt4 = [A1([128, 4, D]) for _ in range(2)]
        junk = A1([128, D], BF16)
        xn = A1([128, 4, D], BF16)
        hTg = [A1([128, 8, 512], BF16) for _ in range(2)]
        ssq = A1([128, 40]); rs = A1([128, 40])
        dma_in(ccT, d["cc"], ["ccT"])
        dma_in(badaT, d["b_adaT"], ["badaT"])
        dma_in(brow[:, 0, :], d["b_ada"][0:1, 2048:3072].partition_broadcast(128), ["brow0"])
        dma_in(brow[:, 1, :], d["b_ada"][0:1, 5120:6144].partition_broadcast(128), ["brow1"])
        op("act", lambda e: e.activation(out=scT, in_=ccT, func=AF.Silu), r=["ccT"], w=["scT"])
        op("dve", lambda e: e.tensor_copy(out=scTb, in_=scT), r=["scT"], w=["scTb"])
        op("dve", lambda e: e.tensor_copy(out=screpb, in_=scT[:, :, 0:1].to_broadcast([128, 8, 128])),
           r=["scT"], w=["screpb"])
        op("pool", lambda e: e.memset(ssq, 0.0), w=["ssq"])
        wada_v = d["w_ada"].rearrange("(k p) n -> p k n", p=128)
        bmod = 7
        reserved.add(bmod)
        mcnt = [0]

        def mod_block(blk):
            i_ = mcnt[0] % 2
            mcnt[0] += 1
            buf, bufb = wab[i_], wabb[i_]
            dma_in(buf, wada_v[:, :, blk * 1024:(blk + 1) * 1024], [("wab", i_)])
            for k in range(8):
                if k % 2 == 0:
                    op("act", lambda e, k=k: e.activation(out=bufb[:, k, :], in_=buf[:, k, :], func=AF.Copy),
                       r=[("wab", i_)], w=[("wabb", i_, k)])
                else:
                    op("dve", lambda e, k=k: e.tensor_copy(out=bufb[:, k, :], in_=buf[:, k, :]), r=[("wab", i_)],
                       w=[("wabb", i_, k)])
            rk = [("wabb", i_, k) for k in range(8)]
            if blk in (0, 1, 3, 4):
                def f(e):
                    last = None
                    for j in range(8):
                        col = (blk * 8 + j) * 2
                        for k in range(8):
                            last = e.matmul(PB(bmod)[:, col:col + 2], lhsT=bufb[:, k, j * 128:(j + 1) * 128],
                                            rhs=scTb[:, k, :], start=(k == 0), stop=(k == 7))
                    return last
                op("pe", f, r=rk + ["scTb"], w=[("psmod", blk), ("ps", bmod)])
            else:
                gi = 0 if blk == 2 else 1
                bb = [nbank(), nbank()]
                def f(e):
                    last = None
                    for h in range(2):
                        for k in range(8):
                            last = e.matmul(PB(bb[h]), lhsT=screpb[:, k, :], rhs=bufb[:, k, h * 512:(h + 1) * 512],
                                            start=(k == 0), stop=(k == 7))
                    return last
                op("pe", f, r=rk + ["screpb"], w=[("ps", bb[0]), ("ps", bb[1])])
                grow = g1row if gi == 0 else g2row
                for h in range(2):
                    op("dve", lambda e, h=h: e.tensor_tensor(
                        out=grow[:, h * 512:(h + 1) * 512], in0=PB(bb[h]), in1=brow[:, gi, h * 512:(h + 1) * 512],
                        op=ALU.add), r=[("ps", bb[h]), f"brow{gi}"], w=[("grow", gi, h)])

        def mod_finalize(c0, c1, blks, derived):
            op("dve", lambda e: e.tensor_tensor(
                out=modT[:, c0:c1, :], in0=PB(bmod)[:, 2 * c0:2 * c1].rearrange("p (a b) -> p a b", b=2),
                in1=badaT[:, c0:c1].unsqueeze(2).to_broadcast([128, c1 - c0, 2]), op=ALU.add),
               r=[("psmod", b_) for b_ in blks] + ["badaT", ("ps", bmod)], w=[("modT", c0)])
            for kind, dst, nrm, cc0, col, nm in derived:
                if kind == "scale":
                    op("dve", lambda e, dst=dst, cc0=cc0, col=col: e.tensor_scalar(
                        out=dst, in0=modT[:, cc0:cc0 + 8, col], scalar1=1.0, scalar2=None, op0=ALU.add),
                       r=[("modT", c0)], w=[nm])
                    op("dve", lambda e, dst=dst, nrm=nrm: e.tensor_tensor(out=dst, in0=dst, in1=nrm, op=ALU.mult),
                       r=[nm, "nmixT", "nmlpT"], w=[nm])
                else:
                    op("dve", lambda e, dst=dst, cc0=cc0, col=col: e.tensor_copy(out=dst, in_=modT[:, cc0:cc0 + 8, col]),
                       r=[("modT", c0)], w=[nm])

        mod_block(0)
        mod_block(1)
        mod_finalize(0, 16, (0, 1), [("scale", msc1, nmixT, 8, 0, "msc1"), ("scale", csc1, nmixT, 8, 1, "csc1"),
                                     ("shift", msh1, None, 0, 0, "msh1"), ("shift", csh1, None, 0, 1, "csh1")])
        hT_v = hT_d.rearrange("(k p) t -> p k t", p=128)

        def norm_tile(xb, xkey, ti, xn_dst, sq_scale=1.0 / D):
            op("act", lambda e: e.activation(out=junk, in_=xb, func=AF.Square, accum_out=ssq[:, ti:ti + 1]),
               r=[xkey, "ssq"], w=["junk", ("ssq", ti)])
            op("act", lambda e: e.activation(out=rs[:, ti:ti + 1], in_=ssq[:, ti:ti + 1], func=AF.Ln,
                                             scale=sq_scale, bias=epsc[:, 0:1]), r=[("ssq", ti), "epsc"], w=[("rs", ti)])
            op("act", lambda e: e.activation(out=rs[:, ti:ti + 1], in_=rs[:, ti:ti + 1], func=AF.Exp, scale=-0.5),
               r=[("rs", ti)], w=[("rs", ti)])
            op("dve", lambda e: e.tensor_scalar(out=xn_dst, in0=xb, scalar1=rs[:, ti:ti + 1], scalar2=None,
                                                op0=ALU.mult), r=[xkey, ("rs", ti)], w=[("xn", ti % 4)])

        later_blocks = {0: 3, 2: 4, 4: 2, 6: 5}
        for tg in range(9):
            is_ctx = tg == 8
            ntile = 2 if is_ctx else 4
            tb = [nbank() for _ in range(4)]
            W = ntile * 128
            xb4 = xt4[tg % 2]
            xk4 = ("xt4", tg % 2)
            if is_ctx:
                dma_in(xb4[:, 0:2, :], d["ctx"].rearrange("(i p) f -> p i f", p=128), [xk4])
            else:
                dma_in(xb4, d["x"][tg * 512:(tg + 1) * 512, :].rearrange("(i p) f -> p i f", p=128), [xk4])
            for i in range(ntile):
                ti = tg * 4 + i
                norm_tile(xb4[:, i, :], xk4, ti, xn[:, i, :])

                def f(e, i=i, tb=tb, W=W):
                    last = None
                    for k in range(8):
                        pv = PB(tb[k // 2], BF16)[:, (k % 2) * 512:(k % 2) * 512 + 512]
                        last = e.transpose(out=pv[:, i * 128:(i + 1) * 128], in_=xn[:, i, k * 128:(k + 1) * 128],
                                           identity=ident)
                    return last
                op("pe", f, r=[("xn", i), "ident"], w=[("ps", b_) for b_ in tb])
            dst = hctxT if is_ctx else hTg[tg % 2]
            dkeys = [("hctxT", k) if is_ctx else ("hTg", tg % 2, k) for k in range(8)]
            sc, sh = (csc1, csh1) if is_ctx else (msc1, msh1)
            sck = ["csc1", "csh1"] if is_ctx else ["msc1", "msh1"]
            for k in range(8):
                pv = PB(tb[k // 2], BF16)[:, (k % 2) * 512:(k % 2) * 512 + W]
                if (k // 2) % 2 == 0:
                    op("act", lambda e, k=k, pv=pv, dst=dst, sc=sc, sh=sh: e.activation(
                        out=dst[:, k, :], in_=pv, func=AF.Identity, scale=sc[:, k:k + 1], bias=sh[:, k:k + 1]),
                       r=[("ps", tb[k // 2])] + sck, w=[dkeys[k]])
                else:
                    op("dve", lambda e, k=k, pv=pv, dst=dst, sc=sc, sh=sh: e.tensor_scalar(
                        out=dst[:, k, :], in0=pv, scalar1=sc[:, k:k + 1], scalar2=sh[:, k:k + 1], op0=ALU.mult,
                        op1=ALU.add), r=[("ps", tb[k // 2])] + sck, w=[dkeys[k]])
            if not is_ctx:
                op("sp", lambda e, tg=tg: e.dma_start(out=hT_v[:, :, tg * 512:(tg + 1) * 512], in_=hTg[tg % 2]),
                   r=dkeys, w=[("hT_d", tg)], dma=True)
            if tg in later_blocks:
                mod_block(later_blocks[tg])
        mod_finalize(24, 40, (3, 4), [("scale", msc2, nmlpT, 32, 0, "msc2"), ("shift", msh2, None, 24, 0, "msh2")])
        reserved.discard(bmod)
        S.barrier()

        if debug == "p1":
            op("sp", lambda e: e.dma_start(out=out_d[0:128, :], in_=nfrow), dma=True)
            op("sp", lambda e: e.dma_start(out=out_d[128:256, :], in_=g1row), dma=True)
            op("sp", lambda e: e.dma_start(out=out_d[256:384, :], in_=g2row), dma=True)
            S.emit(sems, block)
            return nc

        hTv = hT_v
        w_in_v = d["w_in"].rearrange("(k p) n -> p k n", p=128)
        yT_v = yT_d.rearrange("(k p) t -> p k t", p=128)

        if debug in (False, "full", "m1", "m2", "m3", "m4", "m5", "gla", "gla1", "gla2", "gla3"):
            AG = Alloc(PH0)
            qT = AG([128, 2, T], BF16)
            kT = AG([128, 2, T], BF16)
            k_tok = AG([128, 32, 256], BF16)
            v_tok = AG([128, 32, 512], BF16)
            gT = AG([64, T])
            o_f = AG([128, 4, T], BF16)
            gkwb = AG([16, 256]); gsh = [AG([16, 128]) for _ in range(2)]
            triF = AG([128, 128], BF16); triB = AG([128, 128], BF16); maskF = AG([128, 128]); maskB = AG([128, 128])
            kc_tok = AG([128, 2, 256], BF16); vc_tok = AG([128, 2, 512], BF16); gTc = AG([64, 256])
            for nm, t_, dn in [("triF", triF, "triFb"), ("triB", triB, "triBb"), ("maskF", maskF, "maskF"),
                               ("maskB", maskB, "maskB")]:
                dma_in(t_, d[dn], [nm])
            dma_in(gkwb, d["gkw"][32:48, :], ["gkwb"])
            G1 = Alloc(AG.o)
            wq = G1([128, 8, 256], BF16); wk = G1([128, 8, 256], BF16); wv = G1([128, 8, 512], BF16)
            wg = G1([128, 8, 64], BF16)
            hs = [G1([128, 8, 512], BF16) for _ in range(2)]
            op("dve", lambda e: e.memset(wg, 0.0), w=["wg"])
            for k in range(8):
                for t_, c0, c1, nm in [(wq, 1536, 1792, "wq"), (wk, 1792, 2048, "wk"), (wv, 2048, 2560, "wv")]:
                    dma_in(t_[:, k, :], w_in_v[:, k, c0:c1], [nm], eng="pool")
                dma_in(wg[:, k, 0:16], w_in_v[:, k, 3072:3088], ["wg"], eng="pool")
                dma_in(wg[:, k, 32:48], w_in_v[:, k, 3088:3104], ["wg"], eng="pool")

            def proj_fm(hb, hk, wt, wkey, M0, M1, dst_ap, dkey, scale, npart=128, eng="act"):
                b = nbank()
                def f(e):
                    last = None
                    for k in range(8):
                        last = e.matmul(PB(b)[0:npart, 0:hb.shape[2]], lhsT=wt[:, k, M0:M1], rhs=hb[:, k, :],
                                        start=(k == 0), stop=(k == 7))
                    return last
                op("pe", f, r=[hk, wkey], w=[("ps", b)])
                src = PB(b)[0:npart, 0:hb.shape[2]]
                if eng == "act":
                    op("act", lambda e: e.activation(out=dst_ap, in_=src, func=AF.Copy, scale=scale),
                       r=[("ps", b)], w=[dkey])
                else:
                    op("dve", lambda e: e.tensor_copy(out=dst_ap, in_=src), r=[("ps", b)], w=[dkey])

            def proj_tm(hb, hk, i, kdst, vdst, dkey):
                bk_, bv_ = nbank(), nbank()
                def f(e):
                    last = None
                    for k in range(8):
                        last = e.matmul(PB(bk_)[:, 0:256], lhsT=hb[:, k, i * 128:(i + 1) * 128], rhs=wk[:, k, :],
                                        start=(k == 0), stop=(k == 7))
                    for k in range(8):
                        last = e.matmul(PB(bv_), lhsT=hb[:, k, i * 128:(i + 1) * 128], rhs=wv[:, k, :],
                                        start=(k == 0), stop=(k == 7))
                    return last
                op("pe", f, r=[hk, "wk", "wv"], w=[("ps", bk_), ("ps", bv_)])
                op("act", lambda e: e.activation(out=kdst, in_=PB(bk_)[:, 0:256], func=AF.Copy), r=[("ps", bk_)],
                   w=[dkey + "k"])
                op("dve", lambda e: e.tensor_copy(out=vdst, in_=PB(bv_)), r=[("ps", bv_)], w=[dkey + "v"])

            for tg in range(8):
                hb = hs[tg % 2]; hk = ("hs", tg % 2)
                dma_in(hb, hTv[:, :, tg * 512:(tg + 1) * 512], [hk], r=[("hT_d", tg)])
                sl = slice(tg * 512, (tg + 1) * 512)
                for hp in range(2):
                    proj_fm(hb, hk, wq, "wq", hp * 128, (hp + 1) * 128, qT[:, hp, sl], "qT", 0.125)
                    proj_fm(hb, hk, wk, "wk", hp * 128, (hp + 1) * 128, kT[:, hp, sl], "kT", 1.0, eng="dve")
                proj_fm(hb, hk, wg, "wg", 0, 64, gT[:, sl], "gT", 1.0, npart=64)
                for i in range(4):
                    n = tg * 4 + i
                    proj_tm(hb, hk, i, k_tok[:, n, :], v_tok[:, n, :], "tok")
            proj_fm(hctxT, "hctxT", wg, "wg", 0, 64, gTc, "gTc", 1.0, npart=64)
            for i in range(2):
                proj_tm(hctxT, "hctxT", i, kc_tok[:, i, :], vc_tok[:, i, :], "ctok")
            S.barrier()
            if debug == "gla1":
                S.emit(sems, block)
                return nc

            G2 = Alloc(AG.o)
            wog = G2([128, 8, 512], BF16)
            for k in range(8):
                dma_in(wog[:, k, :], w_in_v[:, k, 2560:3072], ["wog"], eng="pool")
            lbuf = [G2([128, 256]) for _ in range(2)]
            lbf = [G2([128, 256], BF16) for _ in range(2)]
            enb_tok = [G2([128, 256], BF16) for _ in range(2)]
            kin_tok = [G2([128, 256], BF16) for _ in range(2)]
            EbT = [G2([128, 2, 128]) for _ in range(2)]
            EnbT = [G2([128, 2, 128]) for _ in range(2)]
            qinT = [G2([128, 2, 128], BF16) for _ in range(2)]
            kinT = [G2([128, 2, 128], BF16) for _ in range(2)]
            attms = [G2([128, 4, 128], BF16) for _ in range(2)]
            Stmps = [G2([128, 2, 128]) for _ in range(2)]; S_bfs = [G2([128, 2, 128], BF16) for _ in range(2)]
            o_sb = G2([128, 4, 128]); osq = G2([128, 4, 128], BF16); rr = G2([128, 4, 128])
            sgs = [G2([128, 4, 128], BF16) for _ in range(2)]
            sbc = [0]
            ybuf = [G2([128, 4, 128], BF16) for _ in range(2)]
            hc = [G2([128, 8, 128], BF16) for _ in range(2)]
            for dr in range(2):
                op("pool", lambda e, dr=dr: e.memset(Sst[dr], 0.0), w=[("S", dr)])

            def prep(dr, n, lat, slot):
                base = 0 if dr == 0 else 32
                g_src = gT if lat else gTc
                tri = triF if dr == 0 else triB
                trik = "triF" if dr == 0 else "triB"
                kt = k_tok if lat else kc_tok
                b1 = nbank()
                if dr == 0:
                    glhs, grhs = g_src[0:16, n * 128:(n + 1) * 128], gkw[0:16, :]
                else:
                    gs_ = gsh[slot]
                    op("dve", lambda e: e.tensor_copy(out=gs_, in_=g_src[32:48, n * 128:(n + 1) * 128]),
                       r=["gT", "gTc"], w=[("gsh", slot)])
                    glhs, grhs = gs_, gkwb
                def f(e):
                    e.matmul(PB(b1)[:, 0:256], lhsT=glhs, rhs=grhs, start=True, stop=False)
                    return e.matmul(PB(b1)[:, 0:256], lhsT=onesf[0:1, 0:128], rhs=gkb[0:1, dr * 256:(dr + 1) * 256],
                                    start=False, stop=True)
                op("pe", f, r=["gT", "gTc", "gkw", "gkwb", "gkb", "onesf", ("gsh", slot)], w=[("ps", b1)])
                lb = lbuf[slot]
                op("act", lambda e: e.activation(out=lb, in_=PB(b1)[:, 0:256], func=AF.Exp, scale=-1.0),
                   r=[("ps", b1)], w=[("lb", slot)])
                lb16 = lbf[slot]
                op("act", lambda e: e.activation(out=lb16, in_=lb, func=AF.Ln, bias=onesf[:, 0:1]), r=[("lb", slot), "onesf"],
                   w=[("lb16", slot)])
                b2, b3 = nbank(), nbank()
                def f2(e):
                    e.matmul(PB(b2)[:, 0:256], lhsT=tri, rhs=lb16, start=True, stop=True)
                    last = None
                    for hp in range(2):
                        last = e.matmul(PB(b3)[:, hp * 128:(hp + 1) * 128], lhsT=lb16[:, hp * 128:(hp + 1) * 128],
                                        rhs=tri, start=True, stop=True)
                    return last
                op("pe", f2, r=[("lb16", slot), trik], w=[("ps", b2), ("ps", b3)])
                op("act", lambda e: e.activation(out=enb_tok[slot], in_=PB(b2)[:, 0:256], func=AF.Exp, scale=-1.0),
                   r=[("ps", b2)], w=[("enb", slot)])
                op("dve", lambda e: e.tensor_tensor(out=kin_tok[slot], in0=kt[:, n, :], in1=enb_tok[slot], op=ALU.mult),
                   r=[("enb", slot), "tokk", "ctokk"], w=[("kin_tok", slot)])
                b3v = PB(b3)[:, 0:256].rearrange("p (a b) -> p a b", a=2)
                op("act", lambda e: e.activation(out=EbT[slot], in_=b3v, func=AF.Exp), r=[("ps", b3)], w=[("EbT", slot)])
                if lat:
                    op("act", lambda e: e.activation(out=EnbT[slot], in_=b3v, func=AF.Exp, scale=-1.0), r=[("ps", b3)],
                       w=[("EnbT", slot)])
                    cs = slice(n * 128, (n + 1) * 128)
                    op("dve", lambda e: e.tensor_tensor(out=qinT[slot], in0=qT[:, :, cs], in1=EbT[slot], op=ALU.mult),
                       r=[("EbT", slot), "qT"], w=[("qinT", slot)])
                    op("dve", lambda e: e.tensor_tensor(out=kinT[slot], in0=kT[:, :, cs], in1=EnbT[slot], op=ALU.mult),
                       r=[("EnbT", slot), "kT"], w=[("kinT", slot)])

            def state(dr, n, lat, slot):
                vt = v_tok if lat else vc_tok
                last_col = 127 if dr == 0 else 0
                Sc = Sst[dr]
                Stmp = Stmps[dr]
                for hp in range(2):
                    b = nbank()
                    op("pe", lambda e, b=b, hp=hp: e.matmul(PB(b)[:, 0:256], lhsT=kin_tok[slot][:, hp * 128:(hp + 1) * 128],
                                                            rhs=vt[:, n, hp * 256:(hp + 1) * 256], start=True, stop=True),
                       r=[("kin_tok", slot), "tokv", "ctokv"], w=[("ps", b)])
                    dec = EbT[slot][:, hp, last_col:last_col + 1]
                    if hp == 0:
                        op("dve", lambda e, hp=hp, dec=dec: e.tensor_scalar(out=Stmp[:, hp, :], in0=Sc[:, hp, :], scalar1=dec,
                                                                            scalar2=None, op0=ALU.mult),
                           r=[("S", dr), ("EbT", slot)], w=[("Stmp", dr, hp)])
                    else:
                        op("act", lambda e, hp=hp, dec=dec: e.activation(out=Stmp[:, hp, :], in_=Sc[:, hp, :], func=AF.Copy,
                                                                         scale=dec),
                           r=[("S", dr), ("EbT", slot)], w=[("Stmp", dr, hp)])
                    def fs(e, b=b, hp=hp, dec=dec):
                        e.scalar_tensor_tensor(out=Sc[0:64, hp, :], in0=PB(b)[0:64, 0:128], scalar=dec[0:64],
                                               in1=Stmp[0:64, hp, :], op0=ALU.mult, op1=ALU.add)
                        return e.scalar_tensor_tensor(out=Sc[64:128, hp, :], in0=PB(b)[64:128, 128:256], scalar=dec[64:128],
                                                      in1=Stmp[64:128, hp, :], op0=ALU.mult, op1=ALU.add)
                    op("dve", fs, r=[("ps", b), ("Stmp", dr, hp), ("EbT", slot)], w=[("S", dr)])
                if lat:
                    op("pool", lambda e: e.tensor_copy(out=S_bfs[dr], in_=Sc), r=[("S", dr)], w=[("S_bf", dr)])

            def ogproj(n, dr):
                cs = slice(n * 128, (n + 1) * 128)
                hcb = hc[dr]
                dma_in(hcb, hTv[:, :, cs], [("hc", dr)])
                bg = nbank()
                def f3(e):
                    last = None
                    for h in range(4):
                        for k in range(8):
                            last = e.matmul(PB(bg)[:, h * 128:(h + 1) * 128], lhsT=wog[:, k, h * 128:(h + 1) * 128],
                                            rhs=hcb[:, k, :], start=(k == 0), stop=(k == 7))
                    return last
                op("pe", f3, r=[("hc", dr), "wog"], w=[("ps", bg)])
                sg = sgs[dr]
                op("act", lambda e: e.activation(out=sg.rearrange("p a b -> p (a b)"), in_=PB(bg), func=AF.Silu),
                   r=[("ps", bg)], w=[("sg", dr)])

            def output(dr, n, slot):
                S_bf = S_bfs[dr]
                sbk = ("S_bf", dr)
                attm = attms[dr]
                atk = ("attm", dr)
                mask = maskF if dr == 0 else maskB
                mk = "maskF" if dr == 0 else "maskB"
                ba = [nbank(), nbank()]
                def f(e):
                    last = None
                    for h in (0, 2, 1, 3):
                        hp, base = h // 2, 64 * (h % 2)
                        last = e.matmul(PB(ba[h % 2])[:, hp * 128:(hp + 1) * 128], lhsT=kinT[slot][base:base + 64, hp, :],
                                        rhs=qinT[slot][base:base + 64, hp, :], start=True, stop=True)
                    return last
                op("pe", f, r=[("kinT", slot), ("qinT", slot)], w=[("ps", ba[0]), ("ps", ba[1])])
                for par in range(2):
                    av = attm.rearrange("p (hp par) c -> p par hp c", par=2)[:, par, :, :]
                    op("dve", lambda e, par=par, av=av: e.tensor_tensor(
                        out=av, in0=PB(ba[par])[:, 0:256].rearrange("p (a b) -> p a b", a=2),
                        in1=mask.unsqueeze(1).to_broadcast([128, 2, 128]), op=ALU.mult),
                       r=[("ps", ba[par]), mk], w=[atk])
                bo = nbank()
                def f2(e):
                    last = None
                    for h in range(4):
                        hp, base = h // 2, 64 * (h % 2)
                        e.matmul(PB(bo)[:, h * 128:(h + 1) * 128], lhsT=v_tok[:, n, h * 128:(h + 1) * 128],
                                 rhs=attm[:, h, :], start=True, stop=False)
                        last = e.matmul(PB(bo)[:, h * 128:(h + 1) * 128], lhsT=S_bf[base:base + 64, hp, :],
                                        rhs=qinT[slot][base:base + 64, hp, :], start=False, stop=True)
                    return last
                op("pe", f2, r=[atk, sbk, ("qinT", slot), "tokv"], w=[("ps", bo)])
                return bo

            def epilogue(dr, n, bo):
                bov = PB(bo).rearrange("p (a b) -> p a b", a=4)
                cs = slice(n * 128, (n + 1) * 128)
                first = (n < 16) if dr == 0 else (n >= 16)
                if first:
                    op("act", lambda e: e.activation(out=o_f[:, :, cs], in_=bov, func=AF.Copy), r=[("ps", bo)],
                       w=[("o_f", n)])
                    return
                op("dve", lambda e: e.tensor_tensor(out=o_sb, in0=bov, in1=o_f[:, :, cs], op=ALU.add),
                   r=[("ps", bo), ("o_f", n)], w=["o_sb"])
                op("pool", lambda e: e.tensor_tensor(out=osq, in0=o_sb, in1=o_sb, op=ALU.mult), r=["o_sb"], w=["osq"])
                bs = nbank()
                op("pe", lambda e: e.matmul(PB(bs), lhsT=ones_bf, rhs=osq.rearrange("p a b -> p (a b)"), start=True,
                                            stop=True), r=["osq", "ones_bf"], w=[("ps", bs)])
                op("act", lambda e: e.activation(out=rr.rearrange("p a b -> p (a b)"), in_=PB(bs), func=AF.Ln,
                                                 scale=1.0 / 128.0, bias=epsc[:, 0:1]), r=[("ps", bs)], w=["rr"])
                op("act", lambda e: e.activation(out=rr, in_=rr, func=AF.Exp, scale=-0.5), r=["rr"], w=["rr"])
                sg = sgs[dr]
                op("dve", lambda e: e.tensor_tensor(out=o_sb, in0=o_sb, in1=rr, op=ALU.mult), r=["o_sb", "rr"], w=["o_sb"])
                yb = ybuf[dr]
                op("dve", lambda e: e.scalar_tensor_tensor(out=yb, in0=o_sb, scalar=gnT[:, 0:1], in1=sg, op0=ALU.mult,
                                                           op1=ALU.mult),
                   r=["o_sb", ("sg", dr), "gnT"], w=[("yb", dr)])
                yv = yT_d[512:1024, :].rearrange("(h p) t -> p h t", p=128)
                op("sp", lambda e: e.dma_start(out=yv[:, :, cs], in_=yb), r=[("yb", dr)], w=[("yT_gla", n)], dma=True)

            cnt = [0]
            def nslot():
                cnt[0] += 1
                return cnt[0] % 2
            for dr in range(2):
                order = [0, 1] if dr == 0 else [1, 0]
                for n in order:
                    s_ = nslot()
                    prep(dr, n, False, s_)
                    state(dr, n, False, s_)
            if debug == "gla2":
                S.emit(sems, block)
                return nc
            for dr in range(2):
                op("act", lambda e, dr=dr: e.activation(out=S_bfs[dr], in_=Sst[dr], func=AF.Copy), r=[("S", dr)],
                   w=[("S_bf", dr)])
            def nchunk(dr, i):
                return i if dr == 0 else 31 - i

            def prep_full(dr, i):
                n = nchunk(dr, i)
                prep(dr, n, True, dr)
                first = (n < 16) if dr == 0 else (n >= 16)
                if not first:
                    ogproj(n, dr)

            def step(dr, i):
                n = nchunk(dr, i)
                bo = output(dr, n, dr)
                state(dr, n, True, dr)
                epilogue(dr, n, bo)

            prep_full(0, 0)
            for i in range(32):
                prep_full(1, i)
                step(0, i)
                if i + 1 < 32:
                    prep_full(0, i + 1)
                step(1, i)
            S.barrier()

        if debug in (False, "full", "m1", "m2", "m3", "m4", "m5", "hy"):
            AH = Alloc(PH0)
            zT = AH([128, T], BF16); x0T = AH([128, T], BF16)
            yTg = zT
            off_sig = AH.o
            SIGd = AH([128, 128, 2, 64], BF16)
            SIG = V(off_sig, [128, 65, 128], BF16)
            off_A = AH.o
            Abuf = AH([128, 65, 128], BF16)
            Ybuf = AH([128, 65, 128], BF16)
            Gv = V(off_A, [65, 128, 128], BF16)
            TA = Alloc(off_A)
            hsbs = [TA([128, 8, 512], BF16) for _ in range(2)]
            u3 = [TA([128, 512]) for _ in range(3)]
            assert TA.o <= AH.o
            Hs = AH([128, 65, 128], BF16)
            Ebig_s = AH([128, 65, 128], BF16)
            U_s = AH([65, 64, 2, 64], BF16)
            hdn = AH([64, NFFT], BF16)
            F1_s = AH([128, 128], BF16); Cinv_s = AH([128, 128], BF16)
            woutg = AH([64, 256], BF16)
            wsl = AH([128, 8, 384], BF16)
            Tts = [AH([128, 4, 128]) for _ in range(2)]; T2s = [AH([128, 4, 128])] * 2
            win8 = [AH([128, 8, 128]) for _ in range(2)]; sq8 = [AH([128, 128, 8], BF16)] * 2
            ntt_s = AH([128, 64]); drow = AH([128, 512])
            normT = AH([128, 1]); tmpb = AH([128, 8, 64]); tmpg = AH([128, 8, 64])
            embc = [V(PH0 + i_ * 2048, [33, 512]) for i_ in range(2)]
            argb = [V(PH0 + 4096 + i_ * 2048, [64, 512]) for i_ in range(2)]
            tb_ = [V(PH0 + 8192 + i_ * 2048, [64, 512]) for i_ in range(2)]
            ha = V(off_sig, [64, NFFT]); hb_ = V(off_sig + 32768, [64, NFFT])
            for nm, t_ in [("F1", F1_s), ("Cinv", Cinv_s), ("Ebig", Ebig_s), ("U", U_s), ("ntt", ntt_s)]:
                dma_in(t_, d[nm], [nm + "_s"])
            dma_in(drow, d["delta"][0:1, :].partition_broadcast(128), ["drow"])

            fws = [fw1, fw2, fw3]
            for l in range(3):
                Kp = 33 if l == 0 else 64
                dst = [ha, hb_, hdn][l]
                prev = [None, ha, hb_][l]
                for ch in range(16):
                    cs = slice(ch * 512, (ch + 1) * 512)
                    if l == 0:
                        dma_in(embc[ch % 2], d["emb"][:, cs], [("embc", ch % 2)])
                        rhs = embc[ch % 2]
                        rk = ("embc", ch % 2)
                    else:
                        rhs = prev[:, cs]
                        rk = ("fl", l - 1, ch)
                    b = nbank()
                    op("pe", lambda e, b=b, l=l, Kp=Kp, rhs=rhs: e.matmul(PB(b)[0:64, :], lhsT=fws[l][0:Kp, :], rhs=rhs[0:Kp],
                                                                          start=True, stop=True),
                       r=[rk, "fw1", "fw2", "fw3"], w=[("ps", b)])
                    a_, t_ = argb[ch % 2], tb_[ch % 2]
                    def fr_(e, b=b, l=l, a_=a_, t_=t_):
                        e.tensor_scalar(out=a_, in0=PB(b)[0:64, :], scalar1=fcol[:, 3:4], scalar2=fsc[:, l:l + 1],
                                        op0=ALU.mult, op1=ALU.add)
                        return e
                    op("dve", lambda e, b=b, l=l, a_=a_: e.tensor_scalar(
                        out=a_, in0=PB(b)[0:64, :], scalar1=fcol[:, 3:4], scalar2=fsc[:, l:l + 1], op0=ALU.mult,
                        op1=ALU.add), r=[("ps", b), "fsc", "fcol"], w=[("arg", ch % 2)])
                    op("dve", lambda e, a_=a_, t_=t_: e.tensor_scalar(out=t_, in0=a_, scalar1=1.0 / (2 * PI), scalar2=MAGIC,
                                                                      op0=ALU.mult, op1=ALU.add),
                       r=[("arg", ch % 2)], w=[("tb", ch % 2)])
                    op("dve", lambda e, t_=t_: e.tensor_scalar(out=t_, in0=t_, scalar1=MAGIC, scalar2=None,
                                                               op0=ALU.subtract), r=[("tb", ch % 2)], w=[("tb", ch % 2)])
                    op("dve", lambda e, a_=a_, t_=t_: e.scalar_tensor_tensor(out=a_, in0=t_, scalar=-2 * PI, in1=a_,
                                                                             op0=ALU.mult, op1=ALU.add),
                       r=[("tb", ch % 2), ("arg", ch % 2)], w=[("arg", ch % 2)])
                    op("act", lambda e, a_=a_, dst=dst, cs=cs: e.activation(out=dst[:, cs], in_=a_, func=AF.Sin,
                                                                             scale=0.999999),
                       r=[("arg", ch % 2)], w=[("fl", l, ch)])
            S.barrier()

            hdn_v = hdn.rearrange("p (r w) -> p r w", w=64)

            def load_wsl(g_):
                for j in range(3):
                    for k in range(8):
                        dma_in(wsl[:, k, j * 128:(j + 1) * 128],
                               w_in_v[:, k, j * 512 + g_ * 128: j * 512 + (g_ + 1) * 128], ["wsl"], eng="pool")
            for g in range(4):
                gs = slice(g * 128, (g + 1) * 128)
                S.barrier()
                if g == 0:
                    load_wsl(0)
                for tg in range(8):
                    hsb = hsbs[tg % 2]
                    hkey = ("hsb", tg % 2)
                    dma_in(hsb, hTv[:, :, tg * 512:(tg + 1) * 512], [hkey], r=[("hT_d", tg)])
                    sl = slice(tg * 512, (tg + 1) * 512)
                    for j in range(3):
                        b = nbank()
                        def f(e, b=b, j=j, hsb=hsb):
                            last = None
                            for k in range(8):
                                last = e.matmul(PB(b), lhsT=wsl[:, k, j * 128:(j + 1) * 128], rhs=hsb[:, k, :],
                                                start=(k == 0), stop=(k == 7))
                            return last
                        op("pe", f, r=[hkey, "wsl"], w=[("ps", b)])
                        cw = convwT[:, j * 4 + g, :]
                        cb = convbT[:, j * 4 + g:j * 4 + g + 1]
                        uj = u3[j]
                        op("act", lambda e, b=b, uj=uj, cw=cw, cb=cb: e.activation(
                            out=uj, in_=PB(b), func=AF.Identity, scale=cw[:, 1:2], bias=cb), r=[("ps", b), "convwT", "convbT"],
                           w=[("u3", j)])
                        pv = PB(b).rearrange("p (r w) -> p r w", w=64)
                        uv = uj.rearrange("p (r w) -> p r w", w=64)
                        op("dve", lambda e, pv=pv, uv=uv, cw=cw: e.scalar_tensor_tensor(
                            out=uv[:, :, 1:64], in0=pv[:, :, 0:63], scalar=cw[:, 0:1], in1=uv[:, :, 1:64], op0=ALU.mult,
                            op1=ALU.add), r=[("ps", b), ("u3", j)], w=[("u3", j)])
                        op("dve", lambda e, pv=pv, uv=uv, cw=cw: e.scalar_tensor_tensor(
                            out=uv[:, :, 0:63], in0=pv[:, :, 1:64], scalar=cw[:, 2:3], in1=uv[:, :, 0:63], op0=ALU.mult,
                            op1=ALU.add), r=[("ps", b), ("u3", j)], w=[("u3", j)])
                    op("pool", lambda e, sl=sl: e.tensor_tensor(out=zT[:, sl], in0=u3[1], in1=u3[2], op=ALU.mult),
                       r=[("u3", 1), ("u3", 2)], w=["zT"])
                    op("pool", lambda e, sl=sl: e.tensor_copy(out=x0T[:, sl], in_=u3[0]), r=[("u3", 0)], w=["x0T"])

                S.barrier()
                if g + 1 < 4:
                    load_wsl(g + 1)
                op("pool", lambda e, g=g: e.tensor_copy(out=woutg[:, 0:128], in_=fwout[:, g * 128:(g + 1) * 128]),
                   r=["fwout"], w=["woutg"])
                op("pool", lambda e, g=g: e.tensor_copy(out=woutg[:, 128:256], in_=fwout[:, 512 + g * 128:512 + (g + 1) * 128]),
                   r=["fwout"], w=["woutg"])
                bq = 7
                for wb in range(8):
                    ws_ = wb % 2
                    for j in range(8):
                        w_ = wb * 8 + j
                        op("act", lambda e, w_=w_, j=j, ws_=ws_, gs=gs: e.activation(
                            out=win8[ws_][:, j, :], in_=drow[:, gs], func=AF.Exp, scale=ntt_s[:, w_:w_ + 1]),
                           r=["drow", "ntt_s"], w=[("win8", ws_)])
                    for jp in range(4):
                        w0 = wb * 8 + jp * 2
                        b = nbank()
                        while b == bq:
                            b = nbank()
                        def fm(e, b=b, w0=w0):
                            e.matmul(PB(b)[:, 0:256], lhsT=hdn_v[:, :, w0], rhs=woutg, start=True, stop=True)
                            return e.matmul(PB(b)[:, 256:512], lhsT=hdn_v[:, :, w0 + 1], rhs=woutg, start=True, stop=True)
                        op("pe", fm, r=["woutg"], w=[("ps", b)])
                        def fk(e, b=b, w0=w0, jp=jp, ws_=ws_):
                            pv = PB(b).rearrange("p (w x) -> p w x", w=2)
                            e.tensor_tensor(out=SIGd[0:64, :, 0, w0:w0 + 2].rearrange("p c w -> p w c"),
                                            in0=pv[0:64, :, 0:128], in1=win8[ws_][0:64, jp * 2:jp * 2 + 2, :], op=ALU.mult)
                            return e.tensor_tensor(out=SIGd[64:128, :, 0, w0:w0 + 2].rearrange("p c w -> p w c"),
                                                   in0=pv[64:128, :, 128:256], in1=win8[ws_][64:128, jp * 2:jp * 2 + 2, :],
                                                   op=ALU.mult)
                        op("dve", fk, r=[("ps", b), ("win8", ws_)], w=[("SIG", wb)])
                    op("act", lambda e, wb=wb, ws_=ws_: e.activation(out=sq8[ws_], in_=SIGd[:, :, 0, wb * 8:(wb + 1) * 8],
                                                                     func=AF.Square), r=[("SIG", wb)], w=[("sq8", 0)])
                    def fa(e, wb=wb, ws_=ws_):
                        last = None
                        for j in range(8):
                            w_ = wb * 8 + j
                            last = e.matmul(PB(bq)[:, 0:1], lhsT=sq8[ws_][:, :, j], rhs=ones_bf[:, 0:1], start=(w_ == 0),
                                            stop=(w_ == 63))
                        return last
                    op("pe", fa, r=[("sq8", 0), "ones_bf"], w=[("ps", bq)])
                op("act", lambda e: e.activation(out=normT, in_=PB(bq)[:, 0:1], func=AF.Ln, bias=epsc[:, 0:1]),
                   r=[("ps", bq)], w=["normT"])
                op("act", lambda e: e.activation(out=normT, in_=normT, func=AF.Exp, scale=-0.5), r=["normT"], w=["normT"])

                def S1(npart):
                    op("act", lambda e: e.activation(out=SIGd[0:npart, 0:64, 1, :], in_=SIGd[0:npart, 0:64, 0, :],
                                                     func=AF.Copy),
                       r=[("SIG", q_) for q_ in range(8)], w=[("SIGdupA", 0)])
                    op("dve", lambda e: e.tensor_copy(out=SIGd[0:npart, 64:128, 1, :], in_=SIGd[0:npart, 64:128, 0, :]),
                       r=[("SIG", q_) for q_ in range(8)], w=[("SIGdupB", 0)])
                    def z_(e):
                        e.memset(Abuf[64:128, 0:1, :], 0.0)
                        return e.memset(Abuf[64:128, 64:65, :], 0.0)
                    op("pool", z_, w=[("A", c_) for c_ in range(32)])
                    for c4 in range(32):
                        b = nbank()
                        def f(e, b=b, c4=c4):
                            last = None
                            for j in range(4):
                                c = c4 * 4 + j
                                lt = SIGd[0:npart, c, :, :].rearrange("p d w -> p (d w)")
                                last = e.matmul(PB(b)[:, j * 128:(j + 1) * 128], lhsT=lt, rhs=F1_s[0:npart, :],
                                                start=True, stop=True)
                            return last
                        op("pe", f, r=[("SIG", q_) for q_ in range(8)] + ["F1_s", ("SIGdupA", 0), ("SIGdupB", 0)],
                           w=[("ps", b)])
                        pr = PB(b)[0:64, :].rearrange("p (c k) -> p k c", c=4)
                        pi_ = PB(b)[64:128, :].rearrange("p (c k) -> p k c", c=4)
                        op("dve", lambda e, pr=pr, c4=c4: e.tensor_copy(out=Abuf[0:64, 0:65, c4 * 4:(c4 + 1) * 4],
                                                                        in_=pr[:, 0:65, :]), r=[("ps", b)], w=[("A", c4)])
                        op("act", lambda e, pi_=pi_, c4=c4: e.activation(out=Abuf[64:128, 1:64, c4 * 4:(c4 + 1) * 4],
                                                                         in_=pi_[:, 65:128, :], func=AF.Copy),
                           r=[("ps", b)], w=[("A", c4)])

                def S2(consumer):
                    for q in range(17):
                        b = nbank()
                        slots = list(range(4 * q, min(4 * q + 4, 65)))
                        def f(e, b=b, slots=slots):
                            last = None
                            for j, s_ in enumerate(slots):
                                last = e.matmul(PB(b)[:, j * 128:(j + 1) * 128], lhsT=Abuf[:, s_, :], rhs=Ebig_s[:, s_, :],
                                                start=True, stop=True)
                            return last
                        op("pe", f, r=[("A", c_) for c_ in range(32)] + ["Ebig_s"], w=[("ps", b)])
                        consumer(q, b, slots)

                S1(128)
                def cons_h(q, b, slots):
                    ns = len(slots)
                    pv = PB(b)[:, 0:ns * 128].rearrange("p (a b) -> p a b", a=ns)
                    op("act", lambda e: e.activation(out=Hs[:, slots[0]:slots[0] + ns, :], in_=pv, func=AF.Copy),
                       r=[("ps", b)], w=[("Hs", q)])
                S2(cons_h)

                zv = zT.rearrange("p (r w) -> p r w", w=64)
                for q in range(8):
                    b = nbank()
                    def f(e, b=b, q=q):
                        last = None
                        for j in range(8):
                            w_ = q * 8 + j
                            last = e.transpose(out=PB(b, BF16)[0:64, j * 128:(j + 1) * 128], in_=zv[:, :, w_], identity=ident)
                        return last
                    op("pe", f, r=["zT", "ident"], w=[("ps", b)])
                    src = PB(b, BF16)[0:64, :].rearrange("p (a b) -> p a b", a=8)
                    dstv = SIGd[0:64, :, 0, q * 8:(q + 1) * 8].rearrange("p c w -> p w c")
                    if q % 2 == 0:
                        op("act", lambda e, src=src, dstv=dstv: e.activation(out=dstv, in_=src, func=AF.Copy),
                           r=[("ps", b)], w=[("SIG", q)])
                    else:
                        op("dve", lambda e, src=src, dstv=dstv: e.tensor_copy(out=dstv, in_=src),
                           r=[("ps", b)], w=[("SIG", q)])
                S1(64)
                def cons_z(q, b, slots):
                    ns = len(slots)
                    s0 = slots[0]
                    Tt, T2 = Tts[q % 2], T2s[q % 2]
                    kT, kT2 = ("Tt", q % 2), ("T2", 0)
                    pv = PB(b)[:, 0:ns * 128].rearrange("p (a b) -> p a b", a=ns)
                    op("dve", lambda e: e.tensor_tensor(out=Tt[:, 0:ns, :], in0=pv, in1=Hs[:, s0:s0 + ns, :], op=ALU.mult),
                       r=[("ps", b), ("Hs", q)], w=[kT])
                    op("pool", lambda e: e.tensor_tensor(out=Ybuf[:, s0:s0 + ns, 0:64], in0=Tt[:, 0:ns, 0:64],
                                                         in1=Tt[:, 0:ns, 64:128], op=ALU.subtract), r=[kT], w=[("Y", q)])
                    def f(e):
                        e.tensor_tensor(out=T2[:, 0:ns, 0:64], in0=pv[:, :, 0:64], in1=Hs[:, s0:s0 + ns, 64:128], op=ALU.mult)
                        return e.tensor_tensor(out=T2[:, 0:ns, 64:128], in0=pv[:, :, 64:128], in1=Hs[:, s0:s0 + ns, 0:64],
                                               op=ALU.mult)
                    op("dve", f, r=[("ps", b), ("Hs", q)], w=[kT2])
                    op("pool", lambda e: e.tensor_tensor(out=Ybuf[:, s0:s0 + ns, 64:128], in0=T2[:, 0:ns, 0:64],
                                                         in1=T2[:, 0:ns, 64:128], op=ALU.add), r=[kT2], w=[("Y", q)])
                S2(cons_z)
                for q in range(9):
                    b = nbank()
                    slots = list(range(8 * q, min(8 * q + 8, 65)))
                    def f(e, b=b, slots=slots):
                        last = None
                        for j, s_ in enumerate(slots):
                            last = e.transpose(out=PB(b, BF16)[:, j * 128:(j + 1) * 128], in_=Ybuf[:, s_, :], identity=ident)
                        return last
                    op("pe", f, r=[("Y", q_) for q_ in range(17)] + ["ident"], w=[("ps", b)])
                    ns = len(slots)
                    src = PB(b, BF16)[:, 0:ns * 128].rearrange("p (a b) -> p a b", a=ns)
                    if q % 2 == 0:
                        op("act", lambda e, src=src, slots=slots, ns=ns: e.activation(
                            out=SIG[:, slots[0]:slots[0] + ns, :], in_=src, func=AF.Copy), r=[("ps", b)],
                           w=[("SIG", q_) for q_ in range(8)])
                    else:
                        op("dve", lambda e, src=src, slots=slots, ns=ns: e.tensor_copy(
                            out=SIG[:, slots[0]:slots[0] + ns, :], in_=src), r=[("ps", b)],
                           w=[("SIG", q_) for q_ in range(8)])
                allAY = [("A", c_) for c_ in range(32)] + [("Y", q_) for q_ in range(17)]
                for c4 in range(32):
                    b = nbank()
                    def f(e, b=b, c4=c4):
                        last = None
                        for j in range(4):
                            c = c4 * 4 + j
                            last = e.matmul(PB(b)[0:65, j * 128:(j + 1) * 128], lhsT=SIG[:, 0:65, c], rhs=Cinv_s, start=True,
                                            stop=True)
                        return last
                    op("pe", f, r=[("SIG", q_) for q_ in range(8)] + ["Cinv_s"], w=[("ps", b)])
                    src = PB(b)[0:65, :].rearrange("p (c k) -> p k c", c=4)
                    wk_ = allAY if c4 == 0 else []
                    if c4 % 2 == 0:
                        op("act", lambda e, src=src, c4=c4: e.activation(out=Gv[:, :, c4 * 4:(c4 + 1) * 4], in_=src,
                                                                         func=AF.Copy), r=[("ps", b)], w=[("G", c4)] + wk_)
                    else:
                        op("dve", lambda e, src=src, c4=c4: e.tensor_copy(out=Gv[:, :, c4 * 4:(c4 + 1) * 4], in_=src),
                           r=[("ps", b)] + ([("G", 0)] if True else []), w=[("G", c4)])
                def tv(ap, q):
                    return ap.rearrange("p (mb ma) -> p ma mb", ma=64)[:, q * 8:(q + 1) * 8, :]
                for q in range(8):
                    b = nbank()
                    def f(e, b=b, q=q):
                        last = None
                        for j in range(8):
                            ma = q * 8 + j
                            e.matmul(PB(b)[:, j * 64:(j + 1) * 64], lhsT=Gv[:, ma, :], rhs=U_s[:, ma, 0, :], start=True,
                                     stop=False)
                            last = e.matmul(PB(b)[:, j * 64:(j + 1) * 64], lhsT=Gv[:, 64 + ma, :], rhs=U_s[:, ma, 1, :],
                                            start=False, stop=True)
                        return last
                    op("pe", f, r=[("G", c_) for c_ in range(32)] + ["U_s"] + allAY, w=[("ps", b)])
                    op("act", lambda e, q=q, g=g: e.activation(out=tmpb, in_=tv(zT, q), func=AF.Copy,
                                                               scale=hbiasT[:, g:g + 1]), r=["zT", "hbiasT"], w=["tmpb"])
                    op("dve", lambda e, b=b: e.scalar_tensor_tensor(
                        out=tmpg, in0=PB(b).rearrange("p (a b) -> p a b", a=8), scalar=normT[:, 0:1], in1=tmpb, op0=ALU.mult,
                        op1=ALU.add), r=[("ps", b), "normT", "tmpb"], w=["tmpg"])
                    op("pool", lambda e, q=q: e.tensor_tensor(out=tv(yTg, q), in0=tmpg, in1=tv(x0T, q), op=ALU.mult),
                       r=["tmpg", "x0T"], w=["yTg"])
                op("sp", lambda e, gs=gs: e.dma_start(out=yT_d[gs, :], in_=yTg), r=["yTg"], w=[("yT_hy", g)], dma=True)
            S.barrier()

        if debug in ("gla", "hy"):
            S.emit(sems, block)
            return nc

        AM = Alloc(16384)
        woutp = AM([128, 8, D], BF16)
        w1s = AM([128, 8, 4 * D], BF16)
        w2s = AM([128, 32, D], BF16)
        off_hid = AM.o
        hidT = AM([128, 32, 256], BF16)
        off_xtm = AM.o
        stgs = [V(off_hid, [128, 4, D]), V(off_xtm, [128, 4, D])]
        xtm = [AM([128, D]) for _ in range(4)]
        yg = AM([128, 8, 256], BF16)
        h2Ts = [AM([128, 8, 256], BF16) for _ in range(2)]
        rl = [AM([128, 512], BF16) for _ in range(2)]
        ssq2 = AM([128, 64]); rs2 = AM([128, 64])
        xnm = V(PA_g1_off, [128, 2, D], BF16)
        junkB = V(PA_g1_off + 4096, [128, D], BF16)
        w1_v = d["w_mlp1"].rearrange("(k p) n -> p k n", p=128)
        w2_v = d["w_mlp2"].rearrange("(k p) n -> p k n", p=128)
        wo_v = d["w_out"].rearrange("(k p) n -> p k n", p=128)
        op("pool", lambda e: e.memset(ssq2, 0.0), w=["ssq2"])
        for k in range(8):
            dma_in(w1s[:, k, :], w1_v[:, k, :], [("w1s", k)], eng="pool")
        for rd in range(10):
            stg = stgs[rd % 2]
            sk = ("stg", rd % 2)
            if rd < 2:
                dma_in(stg, wo_v[:, rd * 4:(rd + 1) * 4, :], [sk])
            else:
                dma_in(stg, w2_v[:, (rd - 2) * 4:(rd - 1) * 4, :], [sk])
            for kk in range(4):
                eng = "dve" if kk % 2 == 0 else "pool"
                if rd < 2:
                    op(eng, lambda e, rd=rd, kk=kk, stg=stg: e.tensor_tensor(out=woutp[:, rd * 4 + kk, :], in0=stg[:, kk, :],
                                                                             in1=g1row, op=ALU.mult), r=[sk], w=["woutp"])
                else:
                    op(eng, lambda e, rd=rd, kk=kk, stg=stg: e.tensor_tensor(out=w2s[:, (rd - 2) * 4 + kk, :], in0=stg[:, kk, :],
                                                                             in1=g2row, op=ALU.mult), r=[sk], w=["w2s"])
        S.barrier()

        def rms_stats(src, col, sq_key, junk_ap, junk_key):
            op("act", lambda e: e.activation(out=junk_ap, in_=src, func=AF.Square, accum_out=ssq2[:, col:col + 1]),
               r=[sq_key, "ssq2"], w=[junk_key, ("ssq2", col)])
            op("act", lambda e: e.activation(out=rs2[:, col:col + 1], in_=ssq2[:, col:col + 1], func=AF.Ln, scale=1.0 / D,
                                             bias=epsc[:, 0:1]), r=[("ssq2", col)], w=[("rs2", col)])
            op("act", lambda e: e.activation(out=rs2[:, col:col + 1], in_=rs2[:, col:col + 1], func=AF.Exp, scale=-0.5),
               r=[("rs2", col)], w=[("rs2", col)])

        tbs = {}

        def stageA1(gq):
            dma_in(yg, yT_v[:, :, gq * 256:(gq + 1) * 256], ["yg"])
            for i in range(2):
                ti = gq * 2 + i
                slot = ti % 4
                xb = xtm[slot]
                dma_in(xb, d["x"][ti * 128:(ti + 1) * 128, :], [("xtm", slot)])
                bh = [nbank(), nbank()]
                def f(e, i=i, bh=bh):
                    last = None
                    for h in range(2):
                        for k in range(8):
                            last = e.matmul(PB(bh[h]), lhsT=yg[:, k, i * 128:(i + 1) * 128],
                                            rhs=woutp[:, k, h * 512:(h + 1) * 512], start=(k == 0), stop=(k == 7))
                    return last
                op("pe", f, r=["yg", "woutp"], w=[("ps", bh[0]), ("ps", bh[1])])
                for h in range(2):
                    op("dve", lambda e, h=h, bh=bh, xb=xb: e.tensor_tensor(
                        out=xb[:, h * 512:(h + 1) * 512], in0=PB(bh[h]), in1=xb[:, h * 512:(h + 1) * 512], op=ALU.add),
                       r=[("ps", bh[h]), ("xtm", slot)], w=[("xtm", slot)])
                rms_stats(xb, ti, ("xtm", slot), xnm[:, i, :], ("xnm", i))
                op("dve", lambda e, i=i, xb=xb, ti=ti: e.tensor_scalar(out=xnm[:, i, :], in0=xb, scalar1=rs2[:, ti:ti + 1],
                                                                       scalar2=None, op0=ALU.mult),
                   r=[("xtm", slot), ("rs2", ti)], w=[("xnm", i)])

        def stageA2(gq):
            tb = [nbank(), nbank()]
            hb = h2Ts[gq % 2]
            hk = ("h2T", gq % 2)
            for i in range(2):
                def ft(e, i=i, tb=tb):
                    last = None
                    for k in range(8):
                        last = e.transpose(out=PB(tb[k // 4], BF16)[:, (k % 4) * 256 + i * 128:(k % 4) * 256 + (i + 1) * 128],
                                           in_=xnm[:, i, k * 128:(k + 1) * 128], identity=ident)
                    return last
                op("pe", ft, r=[("xnm", i), "ident"], w=[("ps", tb[0]), ("ps", tb[1])])
            for k in range(8):
                pv = PB(tb[k // 4], BF16)[:, (k % 4) * 256:(k % 4 + 1) * 256]
                if k // 4 == 0:
                    op("act", lambda e, k=k, pv=pv, hb=hb: e.activation(out=hb[:, k, :], in_=pv, func=AF.Identity,
                                                                        scale=msc2[:, k:k + 1], bias=msh2[:, k:k + 1]),
                       r=[("ps", tb[k // 4])], w=[(hk, k)])
                else:
                    op("dve", lambda e, k=k, pv=pv, hb=hb: e.tensor_scalar(out=hb[:, k, :], in0=pv, scalar1=msc2[:, k:k + 1],
                                                                           scalar2=msh2[:, k:k + 1], op0=ALU.mult, op1=ALU.add),
                       r=[("ps", tb[k // 4])], w=[(hk, k)])

        def stageB1(gq):
            hb = h2Ts[gq % 2]
            hk = ("h2T", gq % 2)
            for hc2 in range(16):
                b = nbank()
                def f(e, b=b, hc2=hc2):
                    last = None
                    for j in range(2):
                        hcx = hc2 * 2 + j
                        for k in range(8):
                            last = e.matmul(PB(b)[:, j * 256:(j + 1) * 256], lhsT=w1s[:, k, hcx * 128:(hcx + 1) * 128],
                                            rhs=hb[:, k, :], start=(k == 0), stop=(k == 7))
                    return last
                op("pe", f, r=[(hk, k) for k in range(8)] + [("w1s", k) for k in range(8)], w=[("ps", b)])
                rb = rl[hc2 % 2]
                op("act", lambda e, b=b, rb=rb: e.activation(out=rb, in_=PB(b), func=AF.Relu), r=[("ps", b)],
                   w=[("rl", hc2 % 2)])
                op("pool" if hc2 % 2 else "dve", lambda e, rb=rb, hc2=hc2: e.tensor_tensor(
                    out=hidT[:, hc2 * 2:(hc2 + 1) * 2, :], in0=rb.rearrange("p (a b) -> p a b", a=2),
                    in1=rb.rearrange("p (a b) -> p a b", a=2), op=ALU.mult), r=[("rl", hc2 % 2)], w=[("hidT", hc2)])

        def stageB2(gq):
            for i in range(2):
                ti = gq * 2 + i
                slot = ti % 4
                xb = xtm[slot]
                bh = [nbank(), nbank()]
                def f(e, i=i, bh=bh):
                    last = None
                    for h in range(2):
                        for k in range(32):
                            last = e.matmul(PB(bh[h]), lhsT=hidT[:, k, i * 128:(i + 1) * 128],
                                            rhs=w2s[:, k, h * 512:(h + 1) * 512], start=(k == 0), stop=(k == 31))
                    return last
                op("pe", f, r=[("hidT", q_) for q_ in range(16)] + ["w2s"], w=[("ps", bh[0]), ("ps", bh[1])])
                for h in range(2):
                    op("dve", lambda e, h=h, bh=bh, xb=xb: e.tensor_tensor(
                        out=xb[:, h * 512:(h + 1) * 512], in0=PB(bh[h]), in1=xb[:, h * 512:(h + 1) * 512], op=ALU.add),
                       r=[("ps", bh[h]), ("xtm", slot)], w=[("xtm", slot)])
                rms_stats(xb, 32 + ti, ("xtm", slot), junkB, "junkB")
                op("dve", lambda e, xb=xb, ti=ti: e.tensor_scalar(out=xb, in0=xb, scalar1=rs2[:, 32 + ti:33 + ti],
                                                                   scalar2=None, op0=ALU.mult),
                   r=[("xtm", slot), ("rs2", 32 + ti)], w=[("xtm", slot)])
                op("pool", lambda e, xb=xb: e.tensor_tensor(out=xb, in0=xb, in1=nfrow, op=ALU.mult),
                   r=[("xtm", slot), "nfrow"], w=[("xtm", slot)])
                op("sp", lambda e, xb=xb, ti=ti: e.dma_start(out=out_d[ti * 128:(ti + 1) * 128, :], in_=xb),
                   r=[("xtm", slot)], w=[("out", ti)], dma=True)

        stageA1(0)
        stageA2(0)
        for gq in range(16):
            stageB1(gq)
            if gq + 1 < 16:
                stageA1(gq + 1)
            stageB2(gq)
            if gq + 1 < 16:
                stageA2(gq + 1)
        if debug == "m5":
            S.barrier()
            op("sp", lambda e: e.dma_start(out=out_d[0:128, :], in_=nfrow), dma=True)
            op("sp", lambda e: e.dma_start(out=out_d[128:256, :], in_=g1row), dma=True)
        if debug == "m4":
            S.barrier()
            op("sp", lambda e: e.dma_start(out=out_d[3968:4096, 0:64], in_=ssq2), dma=True)
            op("sp", lambda e: e.dma_start(out=out_d[3968:4096, 64:128], in_=rs2), dma=True)
        S.emit(sems, block)
    return nc


def make_in_maps(inputs):
    f = lambda a: np.ascontiguousarray(np.asarray(a, dtype=np.float32))
    c = host_constants()
    x = f(inputs["x"]); cvec = f(inputs["c"]); ctx = f(inputs["ctx"]); c_ctx = f(inputs["c_ctx"])
    common = {}
    common["w_ada"] = f(inputs["w_ada"][0])
    b_ada = f(inputs["b_ada"][0])
    common["b_adaT"] = np.ascontiguousarray(b_ada.reshape(48, 128).T)
    common["b_ada"] = b_ada.reshape(1, -1)
    common["nmixT"] = np.ascontiguousarray(f(inputs["norm_mix"][0]).reshape(8, 128).T)
    common["nmlpT"] = np.ascontiguousarray(f(inputs["norm_mlp"][0]).reshape(8, 128).T)
    common["nfin"] = f(inputs["norm_final"]).reshape(1, -1)
    common["w_in"] = f(inputs["w_in"][0])
    cw = f(inputs["conv_w"][0])
    common["convwT"] = np.ascontiguousarray(cw.reshape(3, 12, 128).transpose(2, 1, 0))
    common["convbT"] = np.ascontiguousarray(f(inputs["conv_b"][0]).reshape(12, 128).T)
    common["fw1"] = f(inputs["filt_w1"][0]); common["fw2"] = f(inputs["filt_w2"][0]); common["fw3"] = f(inputs["filt_w3"][0])
    common["fwout"] = f(inputs["filt_wout"][0])
    common["fcol"] = np.ascontiguousarray(np.stack([f(inputs["filt_b1"][0]), f(inputs["filt_b2"][0]),
                                                    f(inputs["filt_b3"][0]), f(inputs["filt_freq"][0])], axis=1))
    common["hbiasT"] = np.ascontiguousarray(f(inputs["hyena_bias"][0]).reshape(4, 128).T)
    gkw = np.zeros((48, 256), np.float32)
    gkw[0:16] = f(inputs["gk_w_fwd"][0]); gkw[32:48] = f(inputs["gk_w_bwd"][0])
    common["gkw"] = gkw
    common["gkb"] = np.concatenate([f(inputs["gk_b_fwd"][0]), f(inputs["gk_b_bwd"][0])]).reshape(1, 512)
    common["gnT"] = f(inputs["gla_norm"][0]).reshape(128, 1)
    common["w_out"] = f(inputs["w_out"][0]); common["w_mlp1"] = f(inputs["w_mlp1"][0]); common["w_mlp2"] = f(inputs["w_mlp2"][0])
    for k in ["F1", "Ebig", "Cinv", "U", "emb", "ntt", "delta", "triF", "triB", "triFb", "triBb", "maskF", "maskB", "ident"]:
        common[k] = c[k]
    maps = []
    for b in range(8):
        m = dict(common)
        m["x"] = x[b]
        m["ctx"] = ctx[b]
        cc = np.stack([cvec[b], c_ctx], axis=1)
        m["cc"] = np.ascontiguousarray(cc.reshape(8, 128, 2).transpose(1, 0, 2))
        maps.append(m)
    return maps


_NC_CACHE = {}


def kernel(**inputs):
    if "nc" not in _NC_CACHE:
        _NC_CACHE["nc"] = build(debug=DEBUG)
    nc = _NC_CACHE["nc"]
    maps = make_in_maps(inputs)
    res = run_bass_kernel_spmd(nc, maps, core_ids=list(range(8)))
    out = np.stack([np.asarray(r["out"], dtype=np.float32) for r in res.results], axis=0)
    return out
```

```python
import math
import os
GSTOP = int(os.environ.get('GLA_STOP', '99'))
SAME_ENGINE_RELAX = os.environ.get('RELAX_SAME') is not None
from contextlib import ExitStack
import numpy as np
import ml_dtypes
import concourse.bass as bass
import concourse.mybir as mybir
from concourse.bass_utils import run_bass_kernel_spmd

F32 = mybir.dt.float32
BF16 = mybir.dt.bfloat16
AF = mybir.ActivationFunctionType
ALU = mybir.AluOpType
NPBF = ml_dtypes.bfloat16

T = 4096
D = 1024
EPS = 1e-6
NFFT = 8192
PI = math.pi
MAGIC = 12582912.0
DEBUG = False


class Sched:
    ENGS = ("pe", "act", "dve", "pool", "sp")
    NDMA = {"sp": 16, "pool": 8, "act": 2, "dve": 1, "pe": 1}

    def __init__(self):
        self.ops = []
        self.last_w = {}
        self.readers = {}
        self.dcnt = {}

    @classmethod
    def sem_names(cls):
        names = []
        for e in cls.ENGS:
            names.append(e)
            for j in range(cls.NDMA[e]):
                names.append(f"{e}_dma{j}")
        return names

    def add(self, eng, fn, r=(), w=(), dma=False, n=1, name=""):
        assert n == 1
        deps = set()
        raw = set()
        for k in r:
            if k in self.last_w:
                deps.add(self.last_w[k])
                raw.add(self.last_w[k])
        for k in w:
            if k in self.last_w:
                deps.add(self.last_w[k])
            deps |= self.readers.get(k, set())
        idx = len(self.ops)
        if dma:
            c = self.dcnt.get(eng, 0)
            self.dcnt[eng] = c + 1
            sem = f"{eng}_dma{c % self.NDMA[eng]}"
        else:
            sem = eng
        self.ops.append(dict(eng=eng, fn=fn, deps=sorted(deps), raw=raw, dma=dma, n=n, name=name, sem=sem))
        for k in r:
            self.readers.setdefault(k, set()).add(idx)
        for k in w:
            self.last_w[k] = idx
            self.readers[k] = set()
        return idx

    def barrier(self):
        last = {}
        for i, op in enumerate(self.ops):
            if op["fn"] is not None:
                last[op["sem"]] = i
        deps = sorted(last.values())
        for e in self.ENGS:
            self.ops.append(dict(eng=e, fn=None, deps=deps, raw=set(deps), dma=False, n=1, name="barrier", sem=e))
        self.last_w = {}
        self.readers = {}

    class _Fake:
        class _I:
            def then_inc(self, *a, **k):
                return self

        def __init__(self):
            self.calls = []

        def __getattr__(self, name):
            def f(*args, **kw):
                out = kw.get("out", args[0] if args else None)
                n = 1
                try:
                    for d_ in out.shape[1:]:
                        n *= int(d_)
                except Exception:
                    n = 256
                self.calls.append((name, n))
                return Sched._Fake._I()
            return f

    def _cost(self, op):
        fk = Sched._Fake()
        try:
            op["fn"](fk)
        except Exception:
            return 1.0, 0.0
        eng = op["eng"]
        c = 0.0
        lat = 0.0
        for name, n in fk.calls:
            if name == "dma_start":
                c += 0.15
                lat = max(lat, 2.5 + n * 3.0 / 150e3)
            elif eng == "pe":
                c += max(64, n) / 1400.0 + 0.04
            elif eng == "act":
                c += n / 1100.0 + 0.25
            elif eng == "dve":
                c += n / 900.0 + 0.12
            else:
                c += n / 450.0 + 0.25
        return max(c, 0.05), lat

    def reschedule(self):
        ops = self.ops
        n = len(ops)
        new_order = []
        i = 0
        XLAT = float(os.environ.get('XLAT', '2.5'))
        while i < n:
            if ops[i]["fn"] is None:
                new_order.append(i)
                i += 1
                continue
            j = i
            while j < n and ops[j]["fn"] is not None:
                j += 1
            seg = list(range(i, j))
            segset = set(seg)
            cost = {}
            lat = {}
            for k in seg:
                cost[k], lat[k] = self._cost(ops[k])
            succ = {k: [] for k in seg}
            indeg = {k: 0 for k in seg}
            for k in seg:
                for d_ in ops[k]["deps"]:
                    if d_ in segset:
                        succ[d_].append(k)
                        indeg[k] += 1
            prio = {}
            for k in reversed(seg):
                prio[k] = cost[k] + lat[k] + max([prio[x_] for x_ in succ[k]], default=0.0)
            eng_free = {}
            fin = {}
            ready = [k for k in seg if indeg[k] == 0]
            sched = []
            while ready:
                best = None
                for k in ready:
                    e_ = ops[k]["eng"]
                    t = eng_free.get(e_, 0.0)
                    for d_ in ops[k]["deps"]:
                        if d_ in segset:
                            t = max(t, fin[d_] + (0.0 if ops[d_]["eng"] == e_ and not ops[d_]["dma"] else XLAT))
                    key = (t, -prio[k], k)
                    if best is None or key < best[0]:
                        best = (key, k, t)
                _, k, t = best
                ready.remove(k)
                e_ = ops[k]["eng"]
                eng_free[e_] = t + cost[k]
                fin[k] = t + cost[k] + lat[k]
                sched.append((t, k))
                for x_ in succ[k]:
                    indeg[x_] -= 1
                    if indeg[x_] == 0:
                        ready.append(x_)
            assert len(sched) == len(seg)
            sched.sort()
            new_order.extend(k for _, k in sched)
            i = j
        remap = {old: new for new, old in enumerate(new_order)}
        new_ops = []
        for old in new_order:
            op = ops[old]
            op["deps"] = sorted(remap[d_] for d_ in op["deps"])
            op["raw"] = set(remap[d_] for d_ in op["raw"])
            new_ops.append(op)
        last = {}
        for idx, op in enumerate(new_ops):
            if op["fn"] is None:
                op["deps"] = sorted(last.values())
                op["raw"] = set(op["deps"])
            else:
                last[op["sem"]] = idx
        self.ops = new_ops

    def emit(self, sems, block):
        if os.environ.get("NO_RESCHED") is None:
            self.reschedule()
        run = {}
        for op in self.ops:
            s = op["sem"]
            if op["fn"] is None:
                continue
            op["prev"] = run.get(s, 0)
            run[s] = run.get(s, 0) + (16 if op["dma"] else 1)
            op["val"] = run[s]
        final = dict(run)
        handles = dict(pe="tensor", act="scalar", dve="vector", pool="gpsimd", sp="sync")

        def make(engname):
            def body(e):
                waited = {}
                for op in self.ops:
                    if op["eng"] != engname:
                        continue
                    for d in op["deps"]:
                        dop = self.ops[d]
                        if dop["fn"] is None:
                            continue
                        if dop["eng"] == "pe" and engname == "pe" and not dop["dma"]:
                            continue
                        if (SAME_ENGINE_RELAX and dop["eng"] == engname and not dop["dma"] and not op["dma"]
                                and d not in op["raw"]):
                            continue
                        s, v = dop["sem"], dop["val"]
                        if waited.get(s, 0) >= v:
                            continue
                        e.wait_ge(sems[s], v)
                        waited[s] = v
                    if op["fn"] is None:
                        continue
                    if op["dma"] and op["prev"] > waited.get(op["sem"], 0):
                        e.wait_ge(sems[op["sem"]], op["prev"])
                        waited[op["sem"]] = op["prev"]
                    res = op["fn"](e)
                    if isinstance(res, (list, tuple)):
                        res = res[-1]
                    res.then_inc(sems[op["sem"]], 16 if op["dma"] else 1)
                if engname == "sp":
                    for s, v in final.items():
                        if v > 0:
                            e.wait_ge(sems[s], v)
            return body

        for engname in self.ENGS:
            getattr(block, handles[engname])(make(engname))


def _bf(a):
    return np.ascontiguousarray(a.astype(np.float32)).astype(NPBF)


_CONST_CACHE = {}


def host_constants():
    if _CONST_CACHE:
        return _CONST_CACHE
    c = {}
    r128 = np.arange(128, dtype=np.float64)[:, None]
    F1 = np.zeros((128, 128))
    F1[:, :65] = np.cos(2 * np.pi * r128 * np.arange(65)[None, :] / 128)
    F1[:, 65:] = -np.sin(2 * np.pi * r128 * np.arange(1, 64)[None, :] / 128)
    c["F1"] = _bf(F1)
    w = np.arange(64, dtype=np.float64)[:, None]
    k2 = np.arange(64, dtype=np.float64)[None, :]
    Eb = np.zeros((128, 65, 128))
    for s in range(65):
        ang = 2 * np.pi * w * (s + 128 * k2) / NFFT
        Mr, Mi = np.cos(ang), -np.sin(ang)
        Eb[:64, s, :64] = Mr
        Eb[:64, s, 64:] = Mi
        Eb[64:, s, :64] = -Mi
        Eb[64:, s, 64:] = Mr
    c["Ebig"] = _bf(Eb)
    ang = 2 * np.pi * np.arange(64)[None, :] * np.arange(64, dtype=np.float64)[:, None] / 64
    Vr, Vi = np.cos(ang), np.sin(ang)
    Ci = np.zeros((128, 128))
    Ci[:64, :64] = Vr
    Ci[:64, 64:] = Vi
    Ci[64:, :64] = -Vi
    Ci[64:, 64:] = Vr
    c["Cinv"] = _bf(Ci)
    U = np.zeros((65, 64, 2, 64))
    sl = np.arange(65, dtype=np.float64)[:, None]
    mb = np.arange(64, dtype=np.float64)[None, :]
    wt = np.where((sl == 0) | (sl == 64), 1.0, 2.0) / NFFT
    for a in range(64):
        ang = 2 * np.pi * (mb * sl / 128 + a * sl / NFFT)
        U[:, a, 0, :] = wt * np.cos(ang)
        U[:, a, 1, :] = -wt * np.sin(ang)
    c["U"] = _bf(U)
    L = T
    f32 = np.float32
    tlin = np.linspace(0.0, 1.0, L, dtype=f32)
    wv = (f32(2.0 * math.pi) * np.arange(L, dtype=f32) / f32(L)).astype(f32)
    fb = np.linspace(1e-4, 15, 16, dtype=f32)
    angm = (wv[:, None] * fb[None, :]).astype(f32)
    emb = np.concatenate([tlin[:, None], np.cos(angm), -np.sin(angm)], axis=-1).astype(f32)
    j = np.zeros(NFFT, dtype=np.int64)
    j[:L] = np.arange(L)
    j[L] = 0
    j[L + 1:] = L - np.arange(1, L)
    emb_all = emb[j].T.copy()
    c["emb"] = np.ascontiguousarray(emb_all.astype(f32))
    tt = tlin[j].astype(np.float64)
    tt[L] = 1.0e4
    c["ntt"] = np.ascontiguousarray((-tt).reshape(128, 64).astype(f32))
    min_decay = math.log(1e-2) / 1.5
    max_decay = math.log(1e-2) / 0.3
    deltas = np.abs(np.linspace(min_decay, max_decay, 512, dtype=f32))
    c["delta"] = np.ascontiguousarray(deltas.reshape(1, 512).astype(f32))
    s_ = np.arange(128)[:, None]
    t_ = np.arange(128)[None, :]
    c["triF"] = np.where(s_ <= t_, -1.0 / 16.0, 0.0).astype(f32)
    c["triB"] = np.where(s_ >= t_, -1.0 / 16.0, 0.0).astype(f32)
    c["triFb"] = _bf(c["triF"])
    c["triBb"] = _bf(c["triB"])
    c["maskF"] = np.where(s_ <= t_, 1.0, 0.0).astype(f32)
    c["maskB"] = np.where(s_ >= t_, 1.0, 0.0).astype(f32)
    c["ident"] = _bf(np.eye(128))
    _CONST_CACHE.update(c)
    return c


IN_SPECS = [
    ("x", [T, D], F32), ("ctx", [256, D], F32), ("cc", [128, 8, 2], F32),
    ("w_ada", [D, 6 * D], F32), ("b_adaT", [128, 48], F32), ("b_ada", [1, 6 * D], F32),
    ("nmixT", [128, 8], F32), ("nmlpT", [128, 8], F32), ("nfin", [1, D], F32),
    ("w_in", [D, 3104], F32), ("convwT", [128, 12, 3], F32), ("convbT", [128, 12], F32),
    ("fw1", [33, 64], F32), ("fw2", [64, 64], F32), ("fw3", [64, 64], F32), ("fwout", [64, 1024], F32),
    ("fcol", [64, 4], F32), ("hbiasT", [128, 4], F32),
    ("gkw", [48, 256], F32), ("gkb", [1, 512], F32), ("gnT", [128, 1], F32),
    ("w_out", [D, D], F32), ("w_mlp1", [D, 4 * D], F32), ("w_mlp2", [4 * D, D], F32),
    ("F1", [128, 128], BF16), ("Ebig", [128, 65, 128], BF16), ("Cinv", [128, 128], BF16),
    ("U", [65, 64, 2, 64], BF16), ("emb", [33, NFFT], F32), ("ntt", [128, 64], F32),
    ("delta", [1, 512], F32), ("triF", [128, 128], F32), ("triB", [128, 128], F32), ("triFb", [128, 128], BF16), ("triBb", [128, 128], BF16),
    ("maskF", [128, 128], F32), ("maskB", [128, 128], F32), ("ident", [128, 128], BF16),
]


def build(debug=False):
    nc = bass.Bass("TRN2", target_bir_lowering=False)
    d = {}
    for name, shape, dt in IN_SPECS:
        d[name] = nc.dram_tensor(name, list(shape), dt, kind="ExternalInput").ap()
    out_d = nc.dram_tensor("out", [T, D], F32, kind="ExternalOutput").ap()
    skind = "ExternalOutput" if debug else "Internal"
    hT_d = nc.dram_tensor("hT_d", [D, T], BF16, kind=skind).ap()
    yT_d = nc.dram_tensor("yT_d", [D, T], BF16, kind=skind).ap()

    S = Sched()
    st = ExitStack()
    with st:
        ARENA_F = 53000
        arena_t = st.enter_context(nc.sbuf_tensor("arena", [128, ARENA_F], F32))
        banks = [st.enter_context(nc.psum_tensor(f"bank{i}", [128, 512], F32)) for i in range(8)]
        sems = {}
        for nm_ in Sched.sem_names():
            sems[nm_] = st.enter_context(nc.semaphore("s_" + nm_))
        block = st.enter_context(nc.Block())

        def V(off, shape, dt=F32):
            esz = 4 if dt == F32 else 2
            nfree = int(np.prod(shape[1:]))
            nb = nfree * esz
            assert off % 4 == 0 and nb % 4 == 0, (off, shape)
            assert off + nb <= ARENA_F * 4, (off, shape)
            ap = arena_t[0:shape[0], off // 4:(off + nb) // 4]
            if dt != F32:
                ap = ap.bitcast(dt)
            if len(shape) == 3:
                ap = ap.rearrange("p (a b) -> p a b", a=shape[1])
            elif len(shape) == 4:
                ap = ap.rearrange("p (a b c) -> p a b c", a=shape[1], b=shape[2])
            return ap

        class Alloc:
            def __init__(self, base, limit=ARENA_F * 4):
                self.o = base
                self.limit = limit

            def __call__(self, shape, dt=F32):
                esz = 4 if dt == F32 else 2
                nb = int(np.prod(shape[1:])) * esz
                nb = (nb + 31) // 32 * 32
                v = V(self.o, shape, dt)
                self.o += nb
                assert self.o <= self.limit, ("arena overflow", self.o)
                return v

        pbank = [0]
        reserved = set()

        def nbank():
            while True:
                b = pbank[0]
                pbank[0] = (b + 1) % 8
                if b not in reserved:
                    return b

        def PB(b, dt=F32):
            ap = banks[b][:]
            return ap if dt == F32 else ap.bitcast(dt)

        def op(eng, fn, r=(), w=(), dma=False, n=1, name=""):
            return S.add(eng, fn, r=r, w=w, dma=dma, n=n, name=name)

        def dma_in(out_ap, in_ap, w, r=(), eng="sp"):
            op(eng, lambda e, o=out_ap, i=in_ap: e.dma_start(out=o, in_=i), r=r, w=w, dma=True)

        PA = Alloc(0, 16384)
        ident = PA([128, 128], BF16)
        ones_bf = PA([128, 128], BF16)
        onesf = PA([128, 128], F32)
        nfrow = PA([128, D], F32)
        PA_g1_off = PA.o
        g1row = PA([128, D], F32)
        g2row = PA([128, D], F32)
        msc1 = PA([128, 8]); msh1 = PA([128, 8]); csc1 = PA([128, 8]); csh1 = PA([128, 8])
        msc2 = PA([128, 8]); msh2 = PA([128, 8])
        epsc = PA([128, 1])
        PBA = Alloc(16384, 32768)
        hctxT = PBA([128, 8, 256], BF16)
        convwT = PBA([128, 12, 3]); convbT = PBA([128, 12]); hbiasT = PBA([128, 4]); gnT = PBA([128, 1])
        gkw = PBA([48, 256]); gkb = PBA([1, 512])
        fw1 = PBA([33, 64]); fw2 = PBA([64, 64]); fw3 = PBA([64, 64]); fwout = PBA([64, 1024])
        fcol = PBA([64, 4]); fsc = PBA([64, 8])
        nmixT = PBA([128, 8]); nmlpT = PBA([128, 8])
        modT = PBA([128, 48, 2])
        Sst = [PBA([128, 2, 128]) for _ in range(2)]
        PH0 = 32768

        dma_in(ident, d["ident"], ["ident"])
        op("pool", lambda e: e.memset(ones_bf, 1.0), w=["ones_bf"])
        op("pool", lambda e: e.memset(onesf, 1.0), w=["onesf"])
        op("pool", lambda e: e.memset(epsc, EPS), w=["epsc"])
        for nm, t_ in [("convwT", convwT), ("convbT", convbT), ("hbiasT", hbiasT), ("gnT", gnT), ("gkw", gkw),
                       ("gkb", gkb), ("fw1", fw1), ("fw2", fw2), ("fw3", fw3), ("fwout", fwout), ("fcol", fcol),
                       ("nmixT", nmixT), ("nmlpT", nmlpT)]:
            dma_in(t_, d[nm], [nm])
        dma_in(nfrow, d["nfin"][0:1, :].partition_broadcast(128), ["nfrow"])
        op("dve", lambda e: e.tensor_scalar(out=fsc[:, 0:3], in0=fcol[:, 0:3], scalar1=fcol[:, 3:4], scalar2=None,
                                            op0=ALU.mult), r=["fcol"], w=["fsc"])

        A0 = Alloc(PH0)
        ccT = A0([128, 8, 2]); scT = A0([128, 8, 2]); scTb = A0([128, 8, 2], BF16); screpb = A0([128, 8, 128], BF16)
        badaT = A0([128, 48]); brow = A0([128, 2, D])
        wab = [A0([128, 8, 1024]) for _ in range(2)]
        wabb = [A0([128, 8, 1024], BF16) for _ in range(2)]
        A1 = Alloc(A0.o)
        xt4 = [A1([128, 4, D]) for _ in range(2)]
        junk = A1([128, D], BF16)
        xn = A1([128, 4, D], BF16)
        hTg = [A1([128, 8, 512], BF16) for _ in range(2)]
        ssq = A1([128, 40]); rs = A1([128, 40])
        dma_in(ccT, d["cc"], ["ccT"])
        dma_in(badaT, d["b_adaT"], ["badaT"])
        dma_in(brow[:, 0, :], d["b_ada"][0:1, 2048:3072].partition_broadcast(128), ["brow0"])
        dma_in(brow[:, 1, :], d["b_ada"][0:1, 5120:6144].partition_broadcast(128), ["brow1"])
        op("act", lambda e: e.activation(out=scT, in_=ccT, func=AF.Silu), r=["ccT"], w=["scT"])
        op("dve", lambda e: e.tensor_copy(out=scTb, in_=scT), r=["scT"], w=["scTb"])
        op("dve", lambda e: e.tensor_copy(out=screpb, in_=scT[:, :, 0:1].to_broadcast([128, 8, 128])),
           r=["scT"], w=["screpb"])
        op("pool", lambda e: e.memset(ssq, 0.0), w=["ssq"])
        wada_v = d["w_ada"].rearrange("(k p) n -> p k n", p=128)
        bmod = 7
        reserved.add(bmod)
        mcnt = [0]

        def mod_block(blk):
            i_ = mcnt[0] % 2
            mcnt[0] += 1
            buf, bufb = wab[i_], wabb[i_]
            dma_in(buf, wada_v[:, :, blk * 1024:(blk + 1) * 1024], [("wab", i_)])
            for k in range(8):
                if k % 2 == 0:
                    op("act", lambda e, k=k: e.activation(out=bufb[:, k, :], in_=buf[:, k, :], func=AF.Copy),
                       r=[("wab", i_)], w=[("wabb", i_, k)])
                else:
                    op("dve", lambda e, k=k: e.tensor_copy(out=bufb[:, k, :], in_=buf[:, k, :]), r=[("wab", i_)],
                       w=[("wabb", i_, k)])
            rk = [("wabb", i_, k) for k in range(8)]
            if blk in (0, 1, 3, 4):
                def f(e):
                    last = None
                    for j in range(8):
                        col = (blk * 8 + j) * 2
                        for k in range(8):
                            last = e.matmul(PB(bmod)[:, col:col + 2], lhsT=bufb[:, k, j * 128:(j + 1) * 128],
                                            rhs=scTb[:, k, :], start=(k == 0), stop=(k == 7))
                    return last
                op("pe", f, r=rk + ["scTb"], w=[("psmod", blk), ("ps", bmod)])
            else:
                gi = 0 if blk == 2 else 1
                bb = [nbank(), nbank()]
                def f(e):
                    last = None
                    for h in range(2):
                        for k in range(8):
                            last = e.matmul(PB(bb[h]), lhsT=screpb[:, k, :], rhs=bufb[:, k, h * 512:(h + 1) * 512],
                                            start=(k == 0), stop=(k == 7))
                    return last
                op("pe", f, r=rk + ["screpb"], w=[("ps", bb[0]), ("ps", bb[1])])
                grow = g1row if gi == 0 else g2row
                for h in range(2):
                    op("dve", lambda e, h=h: e.tensor_tensor(
                        out=grow[:, h * 512:(h + 1) * 512], in0=PB(bb[h]), in1=brow[:, gi, h * 512:(h + 1) * 512],
                        op=ALU.add), r=[("ps", bb[h]), f"brow{gi}"], w=[("grow", gi, h)])

        def mod_finalize(c0, c1, blks, derived):
            op("dve", lambda e: e.tensor_tensor(
                out=modT[:, c0:c1, :], in0=PB(bmod)[:, 2 * c0:2 * c1].rearrange("p (a b) -> p a b", b=2),
                in1=badaT[:, c0:c1].unsqueeze(2).to_broadcast([128, c1 - c0, 2]), op=ALU.add),
               r=[("psmod", b_) for b_ in blks] + ["badaT", ("ps", bmod)], w=[("modT", c0)])
            for kind, dst, nrm, cc0, col, nm in derived:
                if kind == "scale":
                    op("dve", lambda e, dst=dst, cc0=cc0, col=col: e.tensor_scalar(
                        out=dst, in0=modT[:, cc0:cc0 + 8, col], scalar1=1.0, scalar2=None, op0=ALU.add),
                       r=[("modT", c0)], w=[nm])
                    op("dve", lambda e, dst=dst, nrm=nrm: e.tensor_tensor(out=dst, in0=dst, in1=nrm, op=ALU.mult),
                       r=[nm, "nmixT", "nmlpT"], w=[nm])
                else:
                    op("dve", lambda e, dst=dst, cc0=cc0, col=col: e.tensor_copy(out=dst, in_=modT[:, cc0:cc0 + 8, col]),
                       r=[("modT", c0)], w=[nm])

        mod_block(0)
        mod_block(1)
        mod_finalize(0, 16, (0, 1), [("scale", msc1, nmixT, 8, 0, "msc1"), ("scale", csc1, nmixT, 8, 1, "csc1"),
                                     ("shift", msh1, None, 0, 0, "msh1"), ("shift", csh1, None, 0, 1, "csh1")])
        hT_v = hT_d.rearrange("(k p) t -> p k t", p=128)

        def norm_tile(xb, xkey, ti, xn_dst, sq_scale=1.0 / D):
            op("act", lambda e: e.activation(out=junk, in_=xb, func=AF.Square, accum_out=ssq[:, ti:ti + 1]),
               r=[xkey, "ssq"], w=["junk", ("ssq", ti)])
            op("act", lambda e: e.activation(out=rs[:, ti:ti + 1], in_=ssq[:, ti:ti + 1], func=AF.Ln,
                                             scale=sq_scale, bias=epsc[:, 0:1]), r=[("ssq", ti), "epsc"], w=[("rs", ti)])
            op("act", lambda e: e.activation(out=rs[:, ti:ti + 1], in_=rs[:, ti:ti + 1], func=AF.Exp, scale=-0.5),
               r=[("rs", ti)], w=[("rs", ti)])
            op("dve", lambda e: e.tensor_scalar(out=xn_dst, in0=xb, scalar1=rs[:, ti:ti + 1], scalar2=None,
                                                op0=ALU.mult), r=[xkey, ("rs", ti)], w=[("xn", ti % 4)])

        later_blocks = {0: 3, 2: 4, 4: 2, 6: 5}
        for tg in range(9):
            is_ctx = tg == 8
            ntile = 2 if is_ctx else 4
            tb = [nbank() for _ in range(4)]
            W = ntile * 128
            xb4 = xt4[tg % 2]
            xk4 = ("xt4", tg % 2)
            if is_ctx:
                dma_in(xb4[:, 0:2, :], d["ctx"].rearrange("(i p) f -> p i f", p=128), [xk4])
            else:
                dma_in(xb4, d["x"][tg * 512:(tg + 1) * 512, :].rearrange("(i p) f -> p i f", p=128), [xk4])
            for i in range(ntile):
                ti = tg * 4 + i
                norm_tile(xb4[:, i, :], xk4, ti, xn[:, i, :])

                def f(e, i=i, tb=tb, W=W):
                    last = None
                    for k in range(8):
                        pv = PB(tb[k // 2], BF16)[:, (k % 2) * 512:(k % 2) * 512 + 512]
                        last = e.transpose(out=pv[:, i * 128:(i + 1) * 128], in_=xn[:, i, k * 128:(k + 1) * 128],
                                           identity=ident)
                    return last
                op("pe", f, r=[("xn", i), "ident"], w=[("ps", b_) for b_ in tb])
            dst = hctxT if is_ctx else hTg[tg % 2]
            dkeys = [("hctxT", k) if is_ctx else ("hTg", tg % 2, k) for k in range(8)]
            sc, sh = (csc1, csh1) if is_ctx else (msc1, msh1)
            sck = ["csc1", "csh1"] if is_ctx else ["msc1", "msh1"]
            for k in range(8):
                pv = PB(tb[k // 2], BF16)[:, (k % 2) * 512:(k % 2) * 512 + W]
                if (k // 2) % 2 == 0:
                    op("act", lambda e, k=k, pv=pv, dst=dst, sc=sc, sh=sh: e.activation(
                        out=dst[:, k, :], in_=pv, func=AF.Identity, scale=sc[:, k:k + 1], bias=sh[:, k:k + 1]),
                       r=[("ps", tb[k // 2])] + sck, w=[dkeys[k]])
                else:
                    op("dve", lambda e, k=k, pv=pv, dst=dst, sc=sc, sh=sh: e.tensor_scalar(
                        out=dst[:, k, :], in0=pv, scalar1=sc[:, k:k + 1], scalar2=sh[:, k:k + 1], op0=ALU.mult,
                        op1=ALU.add), r=[("ps", tb[k // 2])] + sck, w=[dkeys[k]])
            if not is_ctx:
                op("sp", lambda e, tg=tg: e.dma_start(out=hT_v[:, :, tg * 512:(tg + 1) * 512], in_=hTg[tg % 2]),
                   r=dkeys, w=[("hT_d", tg)], dma=True)
            if tg in later_blocks:
                mod_block(later_blocks[tg])
        mod_finalize(24, 40, (3, 4), [("scale", msc2, nmlpT, 32, 0, "msc2"), ("shift", msh2, None, 24, 0, "msh2")])
        reserved.discard(bmod)
        S.barrier()

        if debug == "p1":
            op("sp", lambda e: e.dma_start(out=out_d[0:128, :], in_=nfrow), dma=True)
            op("sp", lambda e: e.dma_start(out=out_d[128:256, :], in_=g1row), dma=True)
            op("sp", lambda e: e.dma_start(out=out_d[256:384, :], in_=g2row), dma=True)
            S.emit(sems, block)
            return nc

        hTv = hT_v
        w_in_v = d["w_in"].rearrange("(k p) n -> p k n", p=128)
        yT_v = yT_d.rearrange("(k p) t -> p k t", p=128)

        if debug in (False, "full", "m1", "m2", "m3", "m4", "m5", "gla", "gla1", "gla2", "gla3"):
            AG = Alloc(PH0)
            qT = AG([128, 2, T], BF16)
            kT = AG([128, 2, T], BF16)
            k_tok = AG([128, 32, 256], BF16)
            v_tok = AG([128, 32, 512], BF16)
            gT = AG([64, T])
            o_f = AG([128, 4, T], BF16)
            gkwb = AG([16, 256]); gsh = [AG([16, 128]) for _ in range(2)]
            triF = AG([128, 128], BF16); triB = AG([128, 128], BF16); maskF = AG([128, 128]); maskB = AG([128, 128])
            kc_tok = AG([128, 2, 256], BF16); vc_tok = AG([128, 2, 512], BF16); gTc = AG([64, 256])
            for nm, t_, dn in [("triF", triF, "triFb"), ("triB", triB, "triBb"), ("maskF", maskF, "maskF"),
                               ("maskB", maskB, "maskB")]:
                dma_in(t_, d[dn], [nm])
            dma_in(gkwb, d["gkw"][32:48, :], ["gkwb"])
            G1 = Alloc(AG.o)
            wq = G1([128, 8, 256], BF16); wk = G1([128, 8, 256], BF16); wv = G1([128, 8, 512], BF16)
            wg = G1([128, 8, 64], BF16)
            hs = [G1([128, 8, 512], BF16) for _ in range(2)]
            op("dve", lambda e: e.memset(wg, 0.0), w=["wg"])
            for k in range(8):
                for t_, c0, c1, nm in [(wq, 1536, 1792, "wq"), (wk, 1792, 2048, "wk"), (wv, 2048, 2560, "wv")]:
                    dma_in(t_[:, k, :], w_in_v[:, k, c0:c1], [nm], eng="pool")
                dma_in(wg[:, k, 0:16], w_in_v[:, k, 3072:3088], ["wg"], eng="pool")
                dma_in(wg[:, k, 32:48], w_in_v[:, k, 3088:3104], ["wg"], eng="pool")

            def proj_fm(hb, hk, wt, wkey, M0, M1, dst_ap, dkey, scale, npart=128, eng="act"):
                b = nbank()
                def f(e):
                    last = None
                    for k in range(8):
                        last = e.matmul(PB(b)[0:npart, 0:hb.shape[2]], lhsT=wt[:, k, M0:M1], rhs=hb[:, k, :],
                                        start=(k == 0), stop=(k == 7))
                    return last
                op("pe", f, r=[hk, wkey], w=[("ps", b)])
                src = PB(b)[0:npart, 0:hb.shape[2]]
                if eng == "act":
                    op("act", lambda e: e.activation(out=dst_ap, in_=src, func=AF.Copy, scale=scale),
                       r=[("ps", b)], w=[dkey])
                else:
                    op("dve", lambda e: e.tensor_copy(out=dst_ap, in_=src), r=[("ps", b)], w=[dkey])

            def proj_tm(hb, hk, i, kdst, vdst, dkey):
                bk_, bv_ = nbank(), nbank()
                def f(e):
                    last = None
                    for k in range(8):
                        last = e.matmul(PB(bk_)[:, 0:256], lhsT=hb[:, k, i * 128:(i + 1) * 128], rhs=wk[:, k, :],
                                        start=(k == 0), stop=(k == 7))
                    for k in range(8):
                        last = e.matmul(PB(bv_), lhsT=hb[:, k, i * 128:(i + 1) * 128], rhs=wv[:, k, :],
                                        start=(k == 0), stop=(k == 7))
                    return last
                op("pe", f, r=[hk, "wk", "wv"], w=[("ps", bk_), ("ps", bv_)])
                op("act", lambda e: e.activation(out=kdst, in_=PB(bk_)[:, 0:256], func=AF.Copy), r=[("ps", bk_)],
                   w=[dkey + "k"])
                op("dve", lambda e: e.tensor_copy(out=vdst, in_=PB(bv_)), r=[("ps", bv_)], w=[dkey + "v"])

            for tg in range(8):
                hb = hs[tg % 2]; hk = ("hs", tg % 2)
                dma_in(hb, hTv[:, :, tg * 512:(tg + 1) * 512], [hk], r=[("hT_d", tg)])
                sl = slice(tg * 512, (tg + 1) * 512)
                for hp in range(2):
                    proj_fm(hb, hk, wq, "wq", hp * 128, (hp + 1) * 128, qT[:, hp, sl], "qT", 0.125)
                    proj_fm(hb, hk, wk, "wk", hp * 128, (hp + 1) * 128, kT[:, hp, sl], "kT", 1.0, eng="dve")
                proj_fm(hb, hk, wg, "wg", 0, 64, gT[:, sl], "gT", 1.0, npart=64)
                for i in range(4):
                    n = tg * 4 + i
                    proj_tm(hb, hk, i, k_tok[:, n, :], v_tok[:, n, :], "tok")
            proj_fm(hctxT, "hctxT", wg, "wg", 0, 64, gTc, "gTc", 1.0, npart=64)
            for i in range(2):
                proj_tm(hctxT, "hctxT", i, kc_tok[:, i, :], vc_tok[:, i, :], "ctok")
            S.barrier()
            if debug == "gla1":
                S.emit(sems, block)
                return nc

            G2 = Alloc(AG.o)
            wog = G2([128, 8, 512], BF16)
            for k in range(8):
                dma_in(wog[:, k, :], w_in_v[:, k, 2560:3072], ["wog"], eng="pool")
            lbuf = [G2([128, 256]) for _ in range(2)]
            lbf = [G2([128, 256], BF16) for _ in range(2)]
            enb_tok = [G2([128, 256], BF16) for _ in range(2)]
            kin_tok = [G2([128, 256], BF16) for _ in range(2)]
            EbT = [G2([128, 2, 128]) for _ in range(2)]
            EnbT = [G2([128, 2, 128]) for _ in range(2)]
            qinT = [G2([128, 2, 128], BF16) for _ in range(2)]
            kinT = [G2([128, 2, 128], BF16) for _ in range(2)]
            attms = [G2([128, 4, 128], BF16) for _ in range(2)]
            Stmps = [G2([128, 2, 128]) for _ in range(2)]; S_bfs = [G2([128, 2, 128], BF16) for _ in range(2)]
            o_sb = G2([128, 4, 128]); osq = G2([128, 4, 128], BF16); rr = G2([128, 4, 128])
            sgs = [G2([128, 4, 128], BF16) for _ in range(2)]
            sbc = [0]
            ybuf = [G2([128, 4, 128], BF16) for _ in range(2)]
            hc = [G2([128, 8, 128], BF16) for _ in range(2)]
            for dr in range(2):
                op("pool", lambda e, dr=dr: e.memset(Sst[dr], 0.0), w=[("S", dr)])

            def prep(dr, n, lat, slot):
                base = 0 if dr == 0 else 32
                g_src = gT if lat else gTc
                tri = triF if dr == 0 else triB
                trik = "triF" if dr == 0 else "triB"
                kt = k_tok if lat else kc_tok
                b1 = nbank()
                if dr == 0:
                    glhs, grhs = g_src[0:16, n * 128:(n + 1) * 128], gkw[0:16, :]
                else:
                    gs_ = gsh[slot]
                    op("dve", lambda e: e.tensor_copy(out=gs_, in_=g_src[32:48, n * 128:(n + 1) * 128]),
                       r=["gT", "gTc"], w=[("gsh", slot)])
                    glhs, grhs = gs_, gkwb
                def f(e):
                    e.matmul(PB(b1)[:, 0:256], lhsT=glhs, rhs=grhs, start=True, stop=False)
                    return e.matmul(PB(b1)[:, 0:256], lhsT=onesf[0:1, 0:128], rhs=gkb[0:1, dr * 256:(dr + 1) * 256],
                                    start=False, stop=True)
                op("pe", f, r=["gT", "gTc", "gkw", "gkwb", "gkb", "onesf", ("gsh", slot)], w=[("ps", b1)])
                lb = lbuf[slot]
                op("act", lambda e: e.activation(out=lb, in_=PB(b1)[:, 0:256], func=AF.Exp, scale=-1.0),
                   r=[("ps", b1)], w=[("lb", slot)])
                lb16 = lbf[slot]
                op("act", lambda e: e.activation(out=lb16, in_=lb, func=AF.Ln, bias=onesf[:, 0:1]), r=[("lb", slot), "onesf"],
                   w=[("lb16", slot)])
                b2, b3 = nbank(), nbank()
                def f2(e):
                    e.matmul(PB(b2)[:, 0:256], lhsT=tri, rhs=lb16, start=True, stop=True)
                    last = None
                    for hp in range(2):
                        last = e.matmul(PB(b3)[:, hp * 128:(hp + 1) * 128], lhsT=lb16[:, hp * 128:(hp + 1) * 128],
                                        rhs=tri, start=True, stop=True)
                    return last
                op("pe", f2, r=[("lb16", slot), trik], w=[("ps", b2), ("ps", b3)])
                op("act", lambda e: e.activation(out=enb_tok[slot], in_=PB(b2)[:, 0:256], func=AF.Exp, scale=-1.0),
                   r=[("ps", b2)], w=[("enb", slot)])
                op("dve", lambda e: e.tensor_tensor(out=kin_tok[slot], in0=kt[:, n, :], in1=enb_tok[slot], op=ALU.mult),
                   r=[("enb", slot), "tokk", "ctokk"], w=[("kin_tok", slot)])
                b3v = PB(b3)[:, 0:256].rearrange("p (a b) -> p a b", a=2)
                op("act", lambda e: e.activation(out=EbT[slot], in_=b3v, func=AF.Exp), r=[("ps", b3)], w=[("EbT", slot)])
                if lat:
                    op("act", lambda e: e.activation(out=EnbT[slot], in_=b3v, func=AF.Exp, scale=-1.0), r=[("ps", b3)],
                       w=[("EnbT", slot)])
                    cs = slice(n * 128, (n + 1) * 128)
                    op("dve", lambda e: e.tensor_tensor(out=qinT[slot], in0=qT[:, :, cs], in1=EbT[slot], op=ALU.mult),
                       r=[("EbT", slot), "qT"], w=[("qinT", slot)])
                    op("dve", lambda e: e.tensor_tensor(out=kinT[slot], in0=kT[:, :, cs], in1=EnbT[slot], op=ALU.mult),
                       r=[("EnbT", slot), "kT"], w=[("kinT", slot)])

            def state(dr, n, lat, slot):
                vt = v_tok if lat else vc_tok
                last_col = 127 if dr == 0 else 0
                Sc = Sst[dr]
                Stmp = Stmps[dr]
                for hp in range(2):
                    b = nbank()
                    op("pe", lambda e, b=b, hp=hp: e.matmul(PB(b)[:, 0:256], lhsT=kin_tok[slot][:, hp * 128:(hp + 1) * 128],
                                                            rhs=vt[:, n, hp * 256:(hp + 1) * 256], start=True, stop=True),
                       r=[("kin_tok", slot), "tokv", "ctokv"], w=[("ps", b)])
                    dec = EbT[slot][:, hp, last_col:last_col + 1]
                    if hp == 0:
                        op("dve", lambda e, hp=hp, dec=dec: e.tensor_scalar(out=Stmp[:, hp, :], in0=Sc[:, hp, :], scalar1=dec,
                                                                            scalar2=None, op0=ALU.mult),
                           r=[("S", dr), ("EbT", slot)], w=[("Stmp", dr, hp)])
                    else:
                        op("act", lambda e, hp=hp, dec=dec: e.activation(out=Stmp[:, hp, :], in_=Sc[:, hp, :], func=AF.Copy,
                                                                         scale=dec),
                           r=[("S", dr), ("EbT", slot)], w=[("Stmp", dr, hp)])
                    def fs(e, b=b, hp=hp, dec=dec):
                        e.scalar_tensor_tensor(out=Sc[0:64, hp, :], in0=PB(b)[0:64, 0:128], scalar=dec[0:64],
                                               in1=Stmp[0:64, hp, :], op0=ALU.mult, op1=ALU.add)
                        return e.scalar_tensor_tensor(out=Sc[64:128, hp, :], in0=PB(b)[64:128, 128:256], scalar=dec[64:128],
                                                      in1=Stmp[64:128, hp, :], op0=ALU.mult, op1=ALU.add)
                    op("dve", fs, r=[("ps", b), ("Stmp", dr, hp), ("EbT", slot)], w=[("S", dr)])
                if lat:
                    op("pool", lambda e: e.tensor_copy(out=S_bfs[dr], in_=Sc), r=[("S", dr)], w=[("S_bf", dr)])

            def ogproj(n, dr):
                cs = slice(n * 128, (n + 1) * 128)
                hcb = hc[dr]
                dma_in(hcb, hTv[:, :, cs], [("hc", dr)])
                bg = nbank()
                def f3(e):
                    last = None
                    for h in range(4):
                        for k in range(8):
                            last = e.matmul(PB(bg)[:, h * 128:(h + 1) * 128], lhsT=wog[:, k, h * 128:(h + 1) * 128],
                                            rhs=hcb[:, k, :], start=(k == 0), stop=(k == 7))
                    return last
                op("pe", f3, r=[("hc", dr), "wog"], w=[("ps", bg)])
                sg = sgs[dr]
                op("act", lambda e: e.activation(out=sg.rearrange("p a b -> p (a b)"), in_=PB(bg), func=AF.Silu),
                   r=[("ps", bg)], w=[("sg", dr)])

            def output(dr, n, slot):
                S_bf = S_bfs[dr]
                sbk = ("S_bf", dr)
                attm = attms[dr]
                atk = ("attm", dr)
                mask = maskF if dr == 0 else maskB
                mk = "maskF" if dr == 0 else "maskB"
                ba = [nbank(), nbank()]
                def f(e):
                    last = None
                    for h in (0, 2, 1, 3):
                        hp, base = h // 2, 64 * (h % 2)
                        last = e.matmul(PB(ba[h % 2])[:, hp * 128:(hp + 1) * 128], lhsT=kinT[slot][base:base + 64, hp, :],
                                        rhs=qinT[slot][base:base + 64, hp, :], start=True, stop=True)
                    return last
                op("pe", f, r=[("kinT", slot), ("qinT", slot)], w=[("ps", ba[0]), ("ps", ba[1])])
                for par in range(2):
                    av = attm.rearrange("p (hp par) c -> p par hp c", par=2)[:, par, :, :]
                    op("dve", lambda e, par=par, av=av: e.tensor_tensor(
                        out=av, in0=PB(ba[par])[:, 0:256].rearrange("p (a b) -> p a b", a=2),
                        in1=mask.unsqueeze(1).to_broadcast([128, 2, 128]), op=ALU.mult),
                       r=[("ps", ba[par]), mk], w=[atk])
                bo = nbank()
                def f2(e):
                    last = None
                    for h in range(4):
                        hp, base = h // 2, 64 * (h % 2)
                        e.matmul(PB(bo)[:, h * 128:(h + 1) * 128], lhsT=v_tok[:, n, h * 128:(h + 1) * 128],
                                 rhs=attm[:, h, :], start=True, stop=False)
                        last = e.matmul(PB(bo)[:, h * 128:(h + 1) * 128], lhsT=S_bf[base:base + 64, hp, :],
                                        rhs=qinT[slot][base:base + 64, hp, :], start=False, stop=True)
                    return last
                op("pe", f2, r=[atk, sbk, ("qinT", slot), "tokv"], w=[("ps", bo)])
                return bo

            def epilogue(dr, n, bo):
                bov = PB(bo).rearrange("p (a b) -> p a b", a=4)
                cs = slice(n * 128, (n + 1) * 128)
                first = (n < 16) if dr == 0 else (n >= 16)
                if first:
                    op("act", lambda e: e.activation(out=o_f[:, :, cs], in_=bov, func=AF.Copy), r=[("ps", bo)],
                       w=[("o_f", n)])
                    return
                op("dve", lambda e: e.tensor_tensor(out=o_sb, in0=bov, in1=o_f[:, :, cs], op=ALU.add),
                   r=[("ps", bo), ("o_f", n)], w=["o_sb"])
                op("pool", lambda e: e.tensor_tensor(out=osq, in0=o_sb, in1=o_sb, op=ALU.mult), r=["o_sb"], w=["osq"])
                bs = nbank()
                op("pe", lambda e: e.matmul(PB(bs), lhsT=ones_bf, rhs=osq.rearrange("p a b -> p (a b)"), start=True,
                                            stop=True), r=["osq", "ones_bf"], w=[("ps", bs)])
                op("act", lambda e: e.activation(out=rr.rearrange("p a b -> p (a b)"), in_=PB(bs), func=AF.Ln,
                                                 scale=1.0 / 128.0, bias=epsc[:, 0:1]), r=[("ps", bs)], w=["rr"])
                op("act", lambda e: e.activation(out=rr, in_=rr, func=AF.Exp, scale=-0.5), r=["rr"], w=["rr"])
                sg = sgs[dr]
                op("dve", lambda e: e.tensor_tensor(out=o_sb, in0=o_sb, in1=rr, op=ALU.mult), r=["o_sb", "rr"], w=["o_sb"])
                yb = ybuf[dr]
                op("dve", lambda e: e.scalar_tensor_tensor(out=yb, in0=o_sb, scalar=gnT[:, 0:1], in1=sg, op0=ALU.mult,
                                                           op1=ALU.mult),
                   r=["o_sb", ("sg", dr), "gnT"], w=[("yb", dr)])
                yv = yT_d[512:1024, :].rearrange("(h p) t -> p h t", p=128)
                op("sp", lambda e: e.dma_start(out=yv[:, :, cs], in_=yb), r=[("yb", dr)], w=[("yT_gla", n)], dma=True)

            cnt = [0]
            def nslot():
                cnt[0] += 1
                return cnt[0] % 2
            for dr in range(2):
                order = [0, 1] if dr == 0 else [1, 0]
                for n in order:
                    s_ = nslot()
                    prep(dr, n, False, s_)
                    state(dr, n, False, s_)
            if debug == "gla2":
                S.emit(sems, block)
                return nc
            for dr in range(2):
                op("act", lambda e, dr=dr: e.activation(out=S_bfs[dr], in_=Sst[dr], func=AF.Copy), r=[("S", dr)],
                   w=[("S_bf", dr)])
            def nchunk(dr, i):
                return i if dr == 0 else 31 - i

            def prep_full(dr, i):
                n = nchunk(dr, i)
                prep(dr, n, True, dr)
                first = (n < 16) if dr == 0 else (n >= 16)
                if not first:
                    ogproj(n, dr)

            def step(dr, i):
                n = nchunk(dr, i)
                bo = output(dr, n, dr)
                state(dr, n, True, dr)
                epilogue(dr, n, bo)

            prep_full(0, 0)
            for i in range(32):
                prep_full(1, i)
                step(0, i)
                if i + 1 < 32:
                    prep_full(0, i + 1)
                step(1, i)
            S.barrier()

        if debug in (False, "full", "m1", "m2", "m3", "m4", "m5", "hy"):
            AH = Alloc(PH0)
            zT = AH([128, T], BF16); x0T = AH([128, T], BF16)
            yTg = zT
            off_sig = AH.o
            SIGd = AH([128, 128, 2, 64], BF16)
            SIG = V(off_sig, [128, 65, 128], BF16)
            off_A = AH.o
            Abuf = AH([128, 65, 128], BF16)
            Ybuf = AH([128, 65, 128], BF16)
            Gv = V(off_A, [65, 128, 128], BF16)
            TA = Alloc(off_A)
            hsbs = [TA([128, 8, 512], BF16) for _ in range(2)]
            u3 = [TA([128, 512]) for _ in range(3)]
            assert TA.o <= AH.o
            Hs = AH([128, 65, 128], BF16)
            Ebig_s = AH([128, 65, 128], BF16)
            U_s = AH([65, 64, 2, 64], BF16)
            hdn = AH([64, NFFT], BF16)
            F1_s = AH([128, 128], BF16); Cinv_s = AH([128, 128], BF16)
            woutg = AH([64, 256], BF16)
            wsl = AH([128, 8, 384], BF16)
            Tts = [AH([128, 4, 128]) for _ in range(2)]; T2s = [AH([128, 4, 128])] * 2
            win8 = [AH([128, 8, 128]) for _ in range(2)]; sq8 = [AH([128, 128, 8], BF16)] * 2
            ntt_s = AH([128, 64]); drow = AH([128, 512])
            normT = AH([128, 1]); tmpb = AH([128, 8, 64]); tmpg = AH([128, 8, 64])
            embc = [V(PH0 + i_ * 2048, [33, 512]) for i_ in range(2)]
            argb = [V(PH0 + 4096 + i_ * 2048, [64, 512]) for i_ in range(2)]
            tb_ = [V(PH0 + 8192 + i_ * 2048, [64, 512]) for i_ in range(2)]
            ha = V(off_sig, [64, NFFT]); hb_ = V(off_sig + 32768, [64, NFFT])
            for nm, t_ in [("F1", F1_s), ("Cinv", Cinv_s), ("Ebig", Ebig_s), ("U", U_s), ("ntt", ntt_s)]:
                dma_in(t_, d[nm], [nm + "_s"])
            dma_in(drow, d["delta"][0:1, :].partition_broadcast(128), ["drow"])

            fws = [fw1, fw2, fw3]
            for l in range(3):
                Kp = 33 if l == 0 else 64
                dst = [ha, hb_, hdn][l]
                prev = [None, ha, hb_][l]
                for ch in range(16):
                    cs = slice(ch * 512, (ch + 1) * 512)
                    if l == 0:
                        dma_in(embc[ch % 2], d["emb"][:, cs], [("embc", ch % 2)])
                        rhs = embc[ch % 2]
                        rk = ("embc", ch % 2)
                    else:
                        rhs = prev[:, cs]
                        rk = ("fl", l - 1, ch)
                    b = nbank()
                    op("pe", lambda e, b=b, l=l, Kp=Kp, rhs=rhs: e.matmul(PB(b)[0:64, :], lhsT=fws[l][0:Kp, :], rhs=rhs[0:Kp],
                                                                          start=True, stop=True),
                       r=[rk, "fw1", "fw2", "fw3"], w=[("ps", b)])
                    a_, t_ = argb[ch % 2], tb_[ch % 2]
                    def fr_(e, b=b, l=l, a_=a_, t_=t_):
                        e.tensor_scalar(out=a_, in0=PB(b)[0:64, :], scalar1=fcol[:, 3:4], scalar2=fsc[:, l:l + 1],
                                        op0=ALU.mult, op1=ALU.add)
                        return e
                    op("dve", lambda e, b=b, l=l, a_=a_: e.tensor_scalar(
                        out=a_, in0=PB(b)[0:64, :], scalar1=fcol[:, 3:4], scalar2=fsc[:, l:l + 1], op0=ALU.mult,
                        op1=ALU.add), r=[("ps", b), "fsc", "fcol"], w=[("arg", ch % 2)])
                    op("dve", lambda e, a_=a_, t_=t_: e.tensor_scalar(out=t_, in0=a_, scalar1=1.0 / (2 * PI), scalar2=MAGIC,
                                                                      op0=ALU.mult, op1=ALU.add),
                       r=[("arg", ch % 2)], w=[("tb", ch % 2)])
                    op("dve", lambda e, t_=t_: e.tensor_scalar(out=t_, in0=t_, scalar1=MAGIC, scalar2=None,
                                                               op0=ALU.subtract), r=[("tb", ch % 2)], w=[("tb", ch % 2)])
                    op("dve", lambda e, a_=a_, t_=t_: e.scalar_tensor_tensor(out=a_, in0=t_, scalar=-2 * PI, in1=a_,
                                                                             op0=ALU.mult, op1=ALU.add),
                       r=[("tb", ch % 2), ("arg", ch % 2)], w=[("arg", ch % 2)])
                    op("act", lambda e, a_=a_, dst=dst, cs=cs: e.activation(out=dst[:, cs], in_=a_, func=AF.Sin,
                                                                             scale=0.999999),
                       r=[("arg", ch % 2)], w=[("fl", l, ch)])
            S.barrier()

            hdn_v = hdn.rearrange("p (r w) -> p r w", w=64)

            def load_wsl(g_):
                for j in range(3):
                    for k in range(8):
                        dma_in(wsl[:, k, j * 128:(j + 1) * 128],
                               w_in_v[:, k, j * 512 + g_ * 128: j * 512 + (g_ + 1) * 128], ["wsl"], eng="pool")
            for g in range(4):
                gs = slice(g * 128, (g + 1) * 128)
                S.barrier()
                if g == 0:
                    load_wsl(0)
                for tg in range(8):
                    hsb = hsbs[tg % 2]
                    hkey = ("hsb", tg % 2)
                    dma_in(hsb, hTv[:, :, tg * 512:(tg + 1) * 512], [hkey], r=[("hT_d", tg)])
                    sl = slice(tg * 512, (tg + 1) * 512)
                    for j in range(3):
                        b = nbank()
                        def f(e, b=b, j=j, hsb=hsb):
                            last = None
                            for k in range(8):
                                last = e.matmul(PB(b), lhsT=wsl[:, k, j * 128:(j + 1) * 128], rhs=hsb[:, k, :],
                                                start=(k == 0), stop=(k == 7))
                            return last
                        op("pe", f, r=[hkey, "wsl"], w=[("ps", b)])
                        cw = convwT[:, j * 4 + g, :]
                        cb = convbT[:, j * 4 + g:j * 4 + g + 1]
                        uj = u3[j]
                        op("act", lambda e, b=b, uj=uj, cw=cw, cb=cb: e.activation(
                            out=uj, in_=PB(b), func=AF.Identity, scale=cw[:, 1:2], bias=cb), r=[("ps", b), "convwT", "convbT"],
                           w=[("u3", j)])
                        pv = PB(b).rearrange("p (r w) -> p r w", w=64)
                        uv = uj.rearrange("p (r w) -> p r w", w=64)
                        op("dve", lambda e, pv=pv, uv=uv, cw=cw: e.scalar_tensor_tensor(
                            out=uv[:, :, 1:64], in0=pv[:, :, 0:63], scalar=cw[:, 0:1], in1=uv[:, :, 1:64], op0=ALU.mult,
                            op1=ALU.add), r=[("ps", b), ("u3", j)], w=[("u3", j)])
                        op("dve", lambda e, pv=pv, uv=uv, cw=cw: e.scalar_tensor_tensor(
                            out=uv[:, :, 0:63], in0=pv[:, :, 1:64], scalar=cw[:, 2:3], in1=uv[:, :, 0:63], op0=ALU.mult,
                            op1=ALU.add), r=[("ps", b), ("u3", j)], w=[("u3", j)])
                    op("pool", lambda e, sl=sl: e.tensor_tensor(out=zT[:, sl], in0=u3[1], in1=u3[2], op=ALU.mult),
                       r=[("u3", 1), ("u3", 2)], w=["zT"])
                    op("pool", lambda e, sl=sl: e.tensor_copy(out=x0T[:, sl], in_=u3[0]), r=[("u3", 0)], w=["x0T"])

                S.barrier()
                if g + 1 < 4:
                    load_wsl(g + 1)
                op("pool", lambda e, g=g: e.tensor_copy(out=woutg[:, 0:128], in_=fwout[:, g * 128:(g + 1) * 128]),
                   r=["fwout"], w=["woutg"])
                op("pool", lambda e, g=g: e.tensor_copy(out=woutg[:, 128:256], in_=fwout[:, 512 + g * 128:512 + (g + 1) * 128]),
                   r=["fwout"], w=["woutg"])
                bq = 7
                for wb in range(8):
                    ws_ = wb % 2
                    for j in range(8):
                        w_ = wb * 8 + j
                        op("act", lambda e, w_=w_, j=j, ws_=ws_, gs=gs: e.activation(
                            out=win8[ws_][:, j, :], in_=drow[:, gs], func=AF.Exp, scale=ntt_s[:, w_:w_ + 1]),
                           r=["drow", "ntt_s"], w=[("win8", ws_)])
                    for jp in range(4):
                        w0 = wb * 8 + jp * 2
                        b = nbank()
                        while b == bq:
                            b = nbank()
                        def fm(e, b=b, w0=w0):
                            e.matmul(PB(b)[:, 0:256], lhsT=hdn_v[:, :, w0], rhs=woutg, start=True, stop=True)
                            return e.matmul(PB(b)[:, 256:512], lhsT=hdn_v[:, :, w0 + 1], rhs=woutg, start=True, stop=True)
                        op("pe", fm, r=["woutg"], w=[("ps", b)])
                        def fk(e, b=b, w0=w0, jp=jp, ws_=ws_):
                            pv = PB(b).rearrange("p (w x) -> p w x", w=2)
                            e.tensor_tensor(out=SIGd[0:64, :, 0, w0:w0 + 2].rearrange("p c w -> p w c"),
                                            in0=pv[0:64, :, 0:128], in1=win8[ws_][0:64, jp * 2:jp * 2 + 2, :], op=ALU.mult)
                            return e.tensor_tensor(out=SIGd[64:128, :, 0, w0:w0 + 2].rearrange("p c w -> p w c"),
                                                   in0=pv[64:128, :, 128:256], in1=win8[ws_][64:128, jp * 2:jp * 2 + 2, :],
                                                   op=ALU.mult)
                        op("dve", fk, r=[("ps", b), ("win8", ws_)], w=[("SIG", wb)])
                    op("act", lambda e, wb=wb, ws_=ws_: e.activation(out=sq8[ws_], in_=SIGd[:, :, 0, wb * 8:(wb + 1) * 8],
                                                                     func=AF.Square), r=[("SIG", wb)], w=[("sq8", 0)])
                    def fa(e, wb=wb, ws_=ws_):
                        last = None
                        for j in range(8):
                            w_ = wb * 8 + j
                            last = e.matmul(PB(bq)[:, 0:1], lhsT=sq8[ws_][:, :, j], rhs=ones_bf[:, 0:1], start=(w_ == 0),
                                            stop=(w_ == 63))
                        return last
                    op("pe", fa, r=[("sq8", 0), "ones_bf"], w=[("ps", bq)])
                op("act", lambda e: e.activation(out=normT, in_=PB(bq)[:, 0:1], func=AF.Ln, bias=epsc[:, 0:1]),
                   r=[("ps", bq)], w=["normT"])
                op("act", lambda e: e.activation(out=normT, in_=normT, func=AF.Exp, scale=-0.5), r=["normT"], w=["normT"])

                def S1(npart):
                    op("act", lambda e: e.activation(out=SIGd[0:npart, 0:64, 1, :], in_=SIGd[0:npart, 0:64, 0, :],
                                                     func=AF.Copy),
                       r=[("SIG", q_) for q_ in range(8)], w=[("SIGdupA", 0)])
                    op("dve", lambda e: e.tensor_copy(out=SIGd[0:npart, 64:128, 1, :], in_=SIGd[0:npart, 64:128, 0, :]),
                       r=[("SIG", q_) for q_ in range(8)], w=[("SIGdupB", 0)])
                    def z_(e):
                        e.memset(Abuf[64:128, 0:1, :], 0.0)
                        return e.memset(Abuf[64:128, 64:65, :], 0.0)
                    op("pool", z_, w=[("A", c_) for c_ in range(32)])
                    for c4 in range(32):
                        b = nbank()
                        def f(e, b=b, c4=c4):
                            last = None
                            for j in range(4):
                                c = c4 * 4 + j
                                lt = SIGd[0:npart, c, :, :].rearrange("p d w -> p (d w)")
                                last = e.matmul(PB(b)[:, j * 128:(j + 1) * 128], lhsT=lt, rhs=F1_s[0:npart, :],
                                                start=True, stop=True)
                            return last
                        op("pe", f, r=[("SIG", q_) for q_ in range(8)] + ["F1_s", ("SIGdupA", 0), ("SIGdupB", 0)],
                           w=[("ps", b)])
                        pr = PB(b)[0:64, :].rearrange("p (c k) -> p k c", c=4)
                        pi_ = PB(b)[64:128, :].rearrange("p (c k) -> p k c", c=4)
                        op("dve", lambda e, pr=pr, c4=c4: e.tensor_copy(out=Abuf[0:64, 0:65, c4 * 4:(c4 + 1) * 4],
                                                                        in_=pr[:, 0:65, :]), r=[("ps", b)], w=[("A", c4)])
                        op("act", lambda e, pi_=pi_, c4=c4: e.activation(out=Abuf[64:128, 1:64, c4 * 4:(c4 + 1) * 4],
                                                                         in_=pi_[:, 65:128, :], func=AF.Copy),
                           r=[("ps", b)], w=[("A", c4)])

                def S2(consumer):
                    for q in range(17):
                        b = nbank()
                        slots = list(range(4 * q, min(4 * q + 4, 65)))
                        def f(e, b=b, slots=slots):
                            last = None
                            for j, s_ in enumerate(slots):
                                last = e.matmul(PB(b)[:, j * 128:(j + 1) * 128], lhsT=Abuf[:, s_, :], rhs=Ebig_s[:, s_, :],
                                                start=True, stop=True)
                            return last
                        op("pe", f, r=[("A", c_) for c_ in range(32)] + ["Ebig_s"], w=[("ps", b)])
                        consumer(q, b, slots)

                S1(128)
                def cons_h(q, b, slots):
                    ns = len(slots)
                    pv = PB(b)[:, 0:ns * 128].rearrange("p (a b) -> p a b", a=ns)
                    op("act", lambda e: e.activation(out=Hs[:, slots[0]:slots[0] + ns, :], in_=pv, func=AF.Copy),
                       r=[("ps", b)], w=[("Hs", q)])
                S2(cons_h)

                zv = zT.rearrange("p (r w) -> p r w", w=64)
                for q in range(8):
                    b = nbank()
                    def f(e, b=b, q=q):
                        last = None
                        for j in range(8):
                            w_ = q * 8 + j
                            last = e.transpose(out=PB(b, BF16)[0:64, j * 128:(j + 1) * 128], in_=zv[:, :, w_], identity=ident)
                        return last
                    op("pe", f, r=["zT", "ident"], w=[("ps", b)])
                    src = PB(b, BF16)[0:64, :].rearrange("p (a b) -> p a b", a=8)
                    dstv = SIGd[0:64, :, 0, q * 8:(q + 1) * 8].rearrange("p c w -> p w c")
                    if q % 2 == 0:
                        op("act", lambda e, src=src, dstv=dstv: e.activation(out=dstv, in_=src, func=AF.Copy),
                           r=[("ps", b)], w=[("SIG", q)])
                    else:
                        op("dve", lambda e, src=src, dstv=dstv: e.tensor_copy(out=dstv, in_=src),
                           r=[("ps", b)], w=[("SIG", q)])
                S1(64)
                def cons_z(q, b, slots):
                    ns = len(slots)
                    s0 = slots[0]
                    Tt, T2 = Tts[q % 2], T2s[q % 2]
                    kT, kT2 = ("Tt", q % 2), ("T2", 0)
                    pv = PB(b)[:, 0:ns * 128].rearrange("p (a b) -> p a b", a=ns)
                    op("dve", lambda e: e.tensor_tensor(out=Tt[:, 0:ns, :], in0=pv, in1=Hs[:, s0:s0 + ns, :], op=ALU.mult),
                       r=[("ps", b), ("Hs", q)], w=[kT])
                    op("pool", lambda e: e.tensor_tensor(out=Ybuf[:, s0:s0 + ns, 0:64], in0=Tt[:, 0:ns, 0:64],
                                                         in1=Tt[:, 0:ns, 64:128], op=ALU.subtract), r=[kT], w=[("Y", q)])
                    def f(e):
                        e.tensor_tensor(out=T2[:, 0:ns, 0:64], in0=pv[:, :, 0:64], in1=Hs[:, s0:s0 + ns, 64:128], op=ALU.mult)
                        return e.tensor_tensor(out=T2[:, 0:ns, 64:128], in0=pv[:, :, 64:128], in1=Hs[:, s0:s0 + ns, 0:64],
                                               op=ALU.mult)
                    op("dve", f, r=[("ps", b), ("Hs", q)], w=[kT2])
                    op("pool", lambda e: e.tensor_tensor(out=Ybuf[:, s0:s0 + ns, 64:128], in0=T2[:, 0:ns, 0:64],
                                                         in1=T2[:, 0:ns, 64:128], op=ALU.add), r=[kT2], w=[("Y", q)])
                S2(cons_z)
                for q in range(9):
                    b = nbank()
                    slots = list(range(8 * q, min(8 * q + 8, 65)))
                    def f(e, b=b, slots=slots):
                        last = None
                        for j, s_ in enumerate(slots):
                            last = e.transpose(out=PB(b, BF16)[:, j * 128:(j + 1) * 128], in_=Ybuf[:, s_, :], identity=ident)
                        return last
                    op("pe", f, r=[("Y", q_) for q_ in range(17)] + ["ident"], w=[("ps", b)])
                    ns = len(slots)
                    src = PB(b, BF16)[:, 0:ns * 128].rearrange("p (a b) -> p a b", a=ns)
                    if q % 2 == 0:
                        op("act", lambda e, src=src, slots=slots, ns=ns: e.activation(
                            out=SIG[:, slots[0]:slots[0] + ns, :], in_=src, func=AF.Copy), r=[("ps", b)],
                           w=[("SIG", q_) for q_ in range(8)])
                    else:
                        op("dve", lambda e, src=src, slots=slots, ns=ns: e.tensor_copy(
                            out=SIG[:, slots[0]:slots[0] + ns, :], in_=src), r=[("ps", b)],
                           w=[("SIG", q_) for q_ in range(8)])
                allAY = [("A", c_) for c_ in range(32)] + [("Y", q_) for q_ in range(17)]
                for c4 in range(32):
                    b = nbank()
                    def f(e, b=b, c4=c4):
                        last = None
                        for j in range(4):
                            c = c4 * 4 + j
                            last = e.matmul(PB(b)[0:65, j * 128:(j + 1) * 128], lhsT=SIG[:, 0:65, c], rhs=Cinv_s, start=True,
                                            stop=True)
                        return last
                    op("pe", f, r=[("SIG", q_) for q_ in range(8)] + ["Cinv_s"], w=[("ps", b)])
                    src = PB(b)[0:65, :].rearrange("p (c k) -> p k c", c=4)
                    wk_ = allAY if c4 == 0 else []
                    if c4 % 2 == 0:
                        op("act", lambda e, src=src, c4=c4: e.activation(out=Gv[:, :, c4 * 4:(c4 + 1) * 4], in_=src,
                                                                         func=AF.Copy), r=[("ps", b)], w=[("G", c4)] + wk_)
                    else:
                        op("dve", lambda e, src=src, c4=c4: e.tensor_copy(out=Gv[:, :, c4 * 4:(c4 + 1) * 4], in_=src),
                           r=[("ps", b)] + ([("G", 0)] if True else []), w=[("G", c4)])
                def tv(ap, q):
                    return ap.rearrange("p (mb ma) -> p ma mb", ma=64)[:, q * 8:(q + 1) * 8, :]
                for q in range(8):
                    b = nbank()
                    def f(e, b=b, q=q):
                        last = None
                        for j in range(8):
                            ma = q * 8 + j
                            e.matmul(PB(b)[:, j * 64:(j + 1) * 64], lhsT=Gv[:, ma, :], rhs=U_s[:, ma, 0, :], start=True,
                                     stop=False)
                            last = e.matmul(PB(b)[:, j * 64:(j + 1) * 64], lhsT=Gv[:, 64 + ma, :], rhs=U_s[:, ma, 1, :],
                                            start=False, stop=True)
                        return last
                    op("pe", f, r=[("G", c_) for c_ in range(32)] + ["U_s"] + allAY, w=[("ps", b)])
                    op("act", lambda e, q=q, g=g: e.activation(out=tmpb, in_=tv(zT, q), func=AF.Copy,
                                                               scale=hbiasT[:, g:g + 1]), r=["zT", "hbiasT"], w=["tmpb"])
                    op("dve", lambda e, b=b: e.scalar_tensor_tensor(
                        out=tmpg, in0=PB(b).rearrange("p (a b) -> p a b", a=8), scalar=normT[:, 0:1], in1=tmpb, op0=ALU.mult,
                        op1=ALU.add), r=[("ps", b), "normT", "tmpb"], w=["tmpg"])
                    op("pool", lambda e, q=q: e.tensor_tensor(out=tv(yTg, q), in0=tmpg, in1=tv(x0T, q), op=ALU.mult),
                       r=["tmpg", "x0T"], w=["yTg"])
                op("sp", lambda e, gs=gs: e.dma_start(out=yT_d[gs, :], in_=yTg), r=["yTg"], w=[("yT_hy", g)], dma=True)
            S.barrier()

        if debug in ("gla", "hy"):
            S.emit(sems, block)
            return nc

        AM = Alloc(16384)
        woutp = AM([128, 8, D], BF16)
        w1s = AM([128, 8, 4 * D], BF16)
        w2s = AM([128, 32, D], BF16)
        off_hid = AM.o
        hidT = AM([128, 32, 256], BF16)
        off_xtm = AM.o
        stgs = [V(off_hid, [128, 4, D]), V(off_xtm, [128, 4, D])]
        xtm = [AM([128, D]) for _ in range(4)]
        yg = AM([128, 8, 256], BF16)
        h2Ts = [AM([128, 8, 256], BF16) for _ in range(2)]
        rl = [AM([128, 512], BF16) for _ in range(2)]
        ssq2 = AM([128, 64]); rs2 = AM([128, 64])
        xnm = V(PA_g1_off, [128, 2, D], BF16)
        junkB = V(PA_g1_off + 4096, [128, D], BF16)
        w1_v = d["w_mlp1"].rearrange("(k p) n -> p k n", p=128)
        w2_v = d["w_mlp2"].rearrange("(k p) n -> p k n", p=128)
        wo_v = d["w_out"].rearrange("(k p) n -> p k n", p=128)
        op("pool", lambda e: e.memset(ssq2, 0.0), w=["ssq2"])
        for k in range(8):
            dma_in(w1s[:, k, :], w1_v[:, k, :], [("w1s", k)], eng="pool")
        for rd in range(10):
            stg = stgs[rd % 2]
            sk = ("stg", rd % 2)
            if rd < 2:
                dma_in(stg, wo_v[:, rd * 4:(rd + 1) * 4, :], [sk])
            else:
                dma_in(stg, w2_v[:, (rd - 2) * 4:(rd - 1) * 4, :], [sk])
            for kk in range(4):
                eng = "dve" if kk % 2 == 0 else "pool"
                if rd < 2:
                    op(eng, lambda e, rd=rd, kk=kk, stg=stg: e.tensor_tensor(out=woutp[:, rd * 4 + kk, :], in0=stg[:, kk, :],
                                                                             in1=g1row, op=ALU.mult), r=[sk], w=["woutp"])
                else:
                    op(eng, lambda e, rd=rd, kk=kk, stg=stg: e.tensor_tensor(out=w2s[:, (rd - 2) * 4 + kk, :], in0=stg[:, kk, :],
                                                                             in1=g2row, op=ALU.mult), r=[sk], w=["w2s"])
        S.barrier()

        def rms_stats(src, col, sq_key, junk_ap, junk_key):
            op("act", lambda e: e.activation(out=junk_ap, in_=src, func=AF.Square, accum_out=ssq2[:, col:col + 1]),
               r=[sq_key, "ssq2"], w=[junk_key, ("ssq2", col)])
            op("act", lambda e: e.activation(out=rs2[:, col:col + 1], in_=ssq2[:, col:col + 1], func=AF.Ln, scale=1.0 / D,
                                             bias=epsc[:, 0:1]), r=[("ssq2", col)], w=[("rs2", col)])
            op("act", lambda e: e.activation(out=rs2[:, col:col + 1], in_=rs2[:, col:col + 1], func=AF.Exp, scale=-0.5),
               r=[("rs2", col)], w=[("rs2", col)])

        tbs = {}

        def stageA1(gq):
            dma_in(yg, yT_v[:, :, gq * 256:(gq + 1) * 256], ["yg"])
            for i in range(2):
                ti = gq * 2 + i
                slot = ti % 4
                xb = xtm[slot]
                dma_in(xb, d["x"][ti * 128:(ti + 1) * 128, :], [("xtm", slot)])
                bh = [nbank(), nbank()]
                def f(e, i=i, bh=bh):
                    last = None
                    for h in range(2):
                        for k in range(8):
                            last = e.matmul(PB(bh[h]), lhsT=yg[:, k, i * 128:(i + 1) * 128],
                                            rhs=woutp[:, k, h * 512:(h + 1) * 512], start=(k == 0), stop=(k == 7))
                    return last
                op("pe", f, r=["yg", "woutp"], w=[("ps", bh[0]), ("ps", bh[1])])
                for h in range(2):
                    op("dve", lambda e, h=h, bh=bh, xb=xb: e.tensor_tensor(
                        out=xb[:, h * 512:(h + 1) * 512], in0=PB(bh[h]), in1=xb[:, h * 512:(h + 1) * 512], op=ALU.add),
                       r=[("ps", bh[h]), ("xtm", slot)], w=[("xtm", slot)])
                rms_stats(xb, ti, ("xtm", slot), xnm[:, i, :], ("xnm", i))
                op("dve", lambda e, i=i, xb=xb, ti=ti: e.tensor_scalar(out=xnm[:, i, :], in0=xb, scalar1=rs2[:, ti:ti + 1],
                                                                       scalar2=None, op0=ALU.mult),
                   r=[("xtm", slot), ("rs2", ti)], w=[("xnm", i)])

        def stageA2(gq):
            tb = [nbank(), nbank()]
            hb = h2Ts[gq % 2]
            hk = ("h2T", gq % 2)
            for i in range(2):
                def ft(e, i=i, tb=tb):
                    last = None
                    for k in range(8):
                        last = e.transpose(out=PB(tb[k // 4], BF16)[:, (k % 4) * 256 + i * 128:(k % 4) * 256 + (i + 1) * 128],
                                           in_=xnm[:, i, k * 128:(k + 1) * 128], identity=ident)
                    return last
                op("pe", ft, r=[("xnm", i), "ident"], w=[("ps", tb[0]), ("ps", tb[1])])
            for k in range(8):
                pv = PB(tb[k // 4], BF16)[:, (k % 4) * 256:(k % 4 + 1) * 256]
                if k // 4 == 0:
                    op("act", lambda e, k=k, pv=pv, hb=hb: e.activation(out=hb[:, k, :], in_=pv, func=AF.Identity,
                                                                        scale=msc2[:, k:k + 1], bias=msh2[:, k:k + 1]),
                       r=[("ps", tb[k // 4])], w=[(hk, k)])
                else:
                    op("dve", lambda e, k=k, pv=pv, hb=hb: e.tensor_scalar(out=hb[:, k, :], in0=pv, scalar1=msc2[:, k:k + 1],
                                                                           scalar2=msh2[:, k:k + 1], op0=ALU.mult, op1=ALU.add),
                       r=[("ps", tb[k // 4])], w=[(hk, k)])

        def stageB1(gq):
            hb = h2Ts[gq % 2]
            hk = ("h2T", gq % 2)
            for hc2 in range(16):
                b = nbank()
                def f(e, b=b, hc2=hc2):
                    last = None
                    for j in range(2):
                        hcx = hc2 * 2 + j
                        for k in range(8):
                            last = e.matmul(PB(b)[:, j * 256:(j + 1) * 256], lhsT=w1s[:, k, hcx * 128:(hcx + 1) * 128],
                                            rhs=hb[:, k, :], start=(k == 0), stop=(k == 7))
                    return last
                op("pe", f, r=[(hk, k) for k in range(8)] + [("w1s", k) for k in range(8)], w=[("ps", b)])
                rb = rl[hc2 % 2]
                op("act", lambda e, b=b, rb=rb: e.activation(out=rb, in_=PB(b), func=AF.Relu), r=[("ps", b)],
                   w=[("rl", hc2 % 2)])
                op("pool" if hc2 % 2 else "dve", lambda e, rb=rb, hc2=hc2: e.tensor_tensor(
                    out=hidT[:, hc2 * 2:(hc2 + 1) * 2, :], in0=rb.rearrange("p (a b) -> p a b", a=2),
                    in1=rb.rearrange("p (a b) -> p a b", a=2), op=ALU.mult), r=[("rl", hc2 % 2)], w=[("hidT", hc2)])

        def stageB2(gq):
            for i in range(2):
                ti = gq * 2 + i
                slot = ti % 4
                xb = xtm[slot]
                bh = [nbank(), nbank()]
                def f(e, i=i, bh=bh):
                    last = None
                    for h in range(2):
                        for k in range(32):
                            last = e.matmul(PB(bh[h]), lhsT=hidT[:, k, i * 128:(i + 1) * 128],
                                            rhs=w2s[:, k, h * 512:(h + 1) * 512], start=(k == 0), stop=(k == 31))
                    return last
                op("pe", f, r=[("hidT", q_) for q_ in range(16)] + ["w2s"], w=[("ps", bh[0]), ("ps", bh[1])])
                for h in range(2):
                    op("dve", lambda e, h=h, bh=bh, xb=xb: e.tensor_tensor(
                        out=xb[:, h * 512:(h + 1) * 512], in0=PB(bh[h]), in1=xb[:, h * 512:(h + 1) * 512], op=ALU.add),
                       r=[("ps", bh[h]), ("xtm", slot)], w=[("xtm", slot)])
                rms_stats(xb, 32 + ti, ("xtm", slot), junkB, "junkB")
                op("dve", lambda e, xb=xb, ti=ti: e.tensor_scalar(out=xb, in0=xb, scalar1=rs2[:, 32 + ti:33 + ti],
                                                                   scalar2=None, op0=ALU.mult),
                   r=[("xtm", slot), ("rs2", 32 + ti)], w=[("xtm", slot)])
                op("pool", lambda e, xb=xb: e.tensor_tensor(out=xb, in0=xb, in1=nfrow, op=ALU.mult),
                   r=[("xtm", slot), "nfrow"], w=[("xtm", slot)])
                op("sp", lambda e, xb=xb, ti=ti: e.dma_start(out=out_d[ti * 128:(ti + 1) * 128, :], in_=xb),
                   r=[("xtm", slot)], w=[("out", ti)], dma=True)

        stageA1(0)
        stageA2(0)
        for gq in range(16):
            stageB1(gq)
            if gq + 1 < 16:
                stageA1(gq + 1)
            stageB2(gq)
            if gq + 1 < 16:
                stageA2(gq + 1)
        if debug == "m5":
            S.barrier()
            op("sp", lambda e: e.dma_start(out=out_d[0:128, :], in_=nfrow), dma=True)
            op("sp", lambda e: e.dma_start(out=out_d[128:256, :], in_=g1row), dma=True)
        if debug == "m4":
            S.barrier()
            op("sp", lambda e: e.dma_start(out=out_d[3968:4096, 0:64], in_=ssq2), dma=True)
            op("sp", lambda e: e.dma_start(out=out_d[3968:4096, 64:128], in_=rs2), dma=True)
        S.emit(sems, block)
    return nc


def make_in_maps(inputs):
    f = lambda a: np.ascontiguousarray(np.asarray(a, dtype=np.float32))
    c = host_constants()
    x = f(inputs["x"]); cvec = f(inputs["c"]); ctx = f(inputs["ctx"]); c_ctx = f(inputs["c_ctx"])
    common = {}
    common["w_ada"] = f(inputs["w_ada"][0])
    b_ada = f(inputs["b_ada"][0])
    common["b_adaT"] = np.ascontiguousarray(b_ada.reshape(48, 128).T)
    common["b_ada"] = b_ada.reshape(1, -1)
    common["nmixT"] = np.ascontiguousarray(f(inputs["norm_mix"][0]).reshape(8, 128).T)
    common["nmlpT"] = np.ascontiguousarray(f(inputs["norm_mlp"][0]).reshape(8, 128).T)
    common["nfin"] = f(inputs["norm_final"]).reshape(1, -1)
    common["w_in"] = f(inputs["w_in"][0])
    cw = f(inputs["conv_w"][0])
    common["convwT"] = np.ascontiguousarray(cw.reshape(3, 12, 128).transpose(2, 1, 0))
    common["convbT"] = np.ascontiguousarray(f(inputs["conv_b"][0]).reshape(12, 128).T)
    common["fw1"] = f(inputs["filt_w1"][0]); common["fw2"] = f(inputs["filt_w2"][0]); common["fw3"] = f(inputs["filt_w3"][0])
    common["fwout"] = f(inputs["filt_wout"][0])
    common["fcol"] = np.ascontiguousarray(np.stack([f(inputs["filt_b1"][0]), f(inputs["filt_b2"][0]),
                                                    f(inputs["filt_b3"][0]), f(inputs["filt_freq"][0])], axis=1))
    common["hbiasT"] = np.ascontiguousarray(f(inputs["hyena_bias"][0]).reshape(4, 128).T)
    gkw = np.zeros((48, 256), np.float32)
    gkw[0:16] = f(inputs["gk_w_fwd"][0]); gkw[32:48] = f(inputs["gk_w_bwd"][0])
    common["gkw"] = gkw
    common["gkb"] = np.concatenate([f(inputs["gk_b_fwd"][0]), f(inputs["gk_b_bwd"][0])]).reshape(1, 512)
    common["gnT"] = f(inputs["gla_norm"][0]).reshape(128, 1)
    common["w_out"] = f(inputs["w_out"][0]); common["w_mlp1"] = f(inputs["w_mlp1"][0]); common["w_mlp2"] = f(inputs["w_mlp2"][0])
    for k in ["F1", "Ebig", "Cinv", "U", "emb", "ntt", "delta", "triF", "triB", "triFb", "triBb", "maskF", "maskB", "ident"]:
        common[k] = c[k]
    maps = []
    for b in range(8):
        m = dict(common)
        m["x"] = x[b]
        m["ctx"] = ctx[b]
        cc = np.stack([cvec[b], c_ctx], axis=1)
        m["cc"] = np.ascontiguousarray(cc.reshape(8, 128, 2).transpose(1, 0, 2))
        maps.append(m)
    return maps


_NC_CACHE = {}


def kernel(**inputs):
    if "nc" not in _NC_CACHE:
        _NC_CACHE["nc"] = build(debug=DEBUG)
    nc = _NC_CACHE["nc"]
    maps = make_in_maps(inputs)
    res = run_bass_kernel_spmd(nc, maps, core_ids=list(range(8)))
    out = np.stack([np.asarray(r["out"], dtype=np.float32) for r in res.results], axis=0)
    return out
```

```python
import math
import os
GSTOP = int(os.environ.get('GLA_STOP', '99'))
SAME_ENGINE_RELAX = os.environ.get('RELAX_SAME') is not None
from contextlib import ExitStack
import numpy as np
import ml_dtypes
import concourse.bass as bass
import concourse.mybir as mybir
from concourse.bass_utils import run_bass_kernel_spmd

F32 = mybir.dt.float32
BF16 = mybir.dt.bfloat16
AF = mybir.ActivationFunctionType
ALU = mybir.AluOpType
NPBF = ml_dtypes.bfloat16

T = 4096
D = 1024
EPS = 1e-6
NFFT = 8192
PI = math.pi
MAGIC = 12582912.0
DEBUG = False


class Sched:
    ENGS = ("pe", "act", "dve", "pool", "sp")
    NDMA = {"sp": 16, "pool": 8, "act": 2, "dve": 1, "pe": 1}

    def __init__(self):
        self.ops = []
        self.last_w = {}
        self.readers = {}
        self.dcnt = {}

    @classmethod
    def sem_names(cls):
        names = []
        for e in cls.ENGS:
            names.append(e)
            for j in range(cls.NDMA[e]):
                names.append(f"{e}_dma{j}")
        return names

    def add(self, eng, fn, r=(), w=(), dma=False, n=1, name=""):
        assert n == 1
        deps = set()
        raw = set()
        for k in r:
            if k in self.last_w:
                deps.add(self.last_w[k])
                raw.add(self.last_w[k])
        for k in w:
            if k in self.last_w:
                deps.add(self.last_w[k])
            deps |= self.readers.get(k, set())
        idx = len(self.ops)
        if dma:
            c = self.dcnt.get(eng, 0)
            self.dcnt[eng] = c + 1
            sem = f"{eng}_dma{c % self.NDMA[eng]}"
        else:
            sem = eng
        self.ops.append(dict(eng=eng, fn=fn, deps=sorted(deps), raw=raw, dma=dma, n=n, name=name, sem=sem))
        for k in r:
            self.readers.setdefault(k, set()).add(idx)
        for k in w:
            self.last_w[k] = idx
            self.readers[k] = set()
        return idx

    def barrier(self):
        last = {}
        for i, op in enumerate(self.ops):
            if op["fn"] is not None:
                last[op["sem"]] = i
        deps = sorted(last.values())
        for e in self.ENGS:
            self.ops.append(dict(eng=e, fn=None, deps=deps, raw=set(deps), dma=False, n=1, name="barrier", sem=e))
        self.last_w = {}
        self.readers = {}

    class _Fake:
        class _I:
            def then_inc(self, *a, **k):
                return self

        def __init__(self):
            self.calls = []

        def __getattr__(self, name):
            def f(*args, **kw):
                out = kw.get("out", args[0] if args else None)
                n = 1
                try:
                    for d_ in out.shape[1:]:
                        n *= int(d_)
                except Exception:
                    n = 256
                self.calls.append((name, n))
                return Sched._Fake._I()
            return f

    def _cost(self, op):
        fk = Sched._Fake()
        try:
            op["fn"](fk)
        except Exception:
            return 1.0, 0.0
        eng = op["eng"]
        c = 0.0
        lat = 0.0
        for name, n in fk.calls:
            if name == "dma_start":
                c += 0.15
                lat = max(lat, 2.5 + n * 3.0 / 150e3)
            elif eng == "pe":
                c += max(64, n) / 1400.0 + 0.04
            elif eng == "act":
                c += n / 1100.0 + 0.25
            elif eng == "dve":
                c += n / 900.0 + 0.12
            else:
                c += n / 450.0 + 0.25
        return max(c, 0.05), lat

    def reschedule(self):
        ops = self.ops
        n = len(ops)
        new_order = []
        i = 0
        SLACK = float(os.environ.get('SLACK', '1.0'))
        XLAT = float(os.environ.get('XLAT', '2.5'))
        while i < n:
            if ops[i]["fn"] is None:
                new_order.append(i)
                i += 1
                continue
            j = i
            while j < n and ops[j]["fn"] is not None:
                j += 1
            seg = list(range(i, j))
            segset = set(seg)
            cost = {}
            lat = {}
            for k in seg:
                cost[k], lat[k] = self._cost(ops[k])
            succ = {k: [] for k in seg}
            indeg = {k: 0 for k in seg}
            for k in seg:
                for d_ in ops[k]["deps"]:
                    if d_ in segset:
                        succ[d_].append(k)
                        indeg[k] += 1
            prio = {}
            for k in reversed(seg):
                prio[k] = cost[k] + lat[k] + max([prio[x_] for x_ in succ[k]], default=0.0)
            eng_free = {}
            fin = {}
            ready = [k for k in seg if indeg[k] == 0]
            sched = []
            while ready:
                cands = []
                for k in ready:
                    e_ = ops[k]["eng"]
                    t = eng_free.get(e_, 0.0)
                    for d_ in ops[k]["deps"]:
                        if d_ in segset:
                            t = max(t, fin[d_] + (0.0 if ops[d_]["eng"] == e_ and not ops[d_]["dma"] else XLAT))
                    cands.append((t, -prio[k], k))
                tmin = min(c_[0] for c_ in cands)
                sel = min((c_ for c_ in cands if c_[0] <= tmin + SLACK), key=lambda c_: (c_[1], c_[0], c_[2]))
                t, _, k = sel
                ready.remove(k)
                e_ = ops[k]["eng"]
                eng_free[e_] = t + cost[k]
                fin[k] = t + cost[k] + lat[k]
                sched.append((t, k))
                for x_ in succ[k]:
                    indeg[x_] -= 1
                    if indeg[x_] == 0:
                        ready.append(x_)
            assert len(sched) == len(seg)
            sched.sort()
            new_order.extend(k for _, k in sched)
            i = j
        remap = {old: new for new, old in enumerate(new_order)}
        new_ops = []
        for old in new_order:
            op = ops[old]
            op["deps"] = sorted(remap[d_] for d_ in op["deps"])
            op["raw"] = set(remap[d_] for d_ in op["raw"])
            new_ops.append(op)
        last = {}
        for idx, op in enumerate(new_ops):
            if op["fn"] is None:
                op["deps"] = sorted(last.values())
                op["raw"] = set(op["deps"])
            else:
                last[op["sem"]] = idx
        self.ops = new_ops

    def emit(self, sems, block):
        if os.environ.get("NO_RESCHED") is None:
            self.reschedule()
        run = {}
        for op in self.ops:
            s = op["sem"]
            if op["fn"] is None:
                continue
            op["prev"] = run.get(s, 0)
            run[s] = run.get(s, 0) + (16 if op["dma"] else 1)
            op["val"] = run[s]
        final = dict(run)
        handles = dict(pe="tensor", act="scalar", dve="vector", pool="gpsimd", sp="sync")

        def make(engname):
            def body(e):
                waited = {}
                for op in self.ops:
                    if op["eng"] != engname:
                        continue
                    for d in op["deps"]:
                        dop = self.ops[d]
                        if dop["fn"] is None:
                            continue
                        if dop["eng"] == "pe" and engname == "pe" and not dop["dma"]:
                            continue
                        if (SAME_ENGINE_RELAX and dop["eng"] == engname and not dop["dma"] and not op["dma"]
                                and d not in op["raw"]):
                            continue
                        s, v = dop["sem"], dop["val"]
                        if waited.get(s, 0) >= v:
                            continue
                        e.wait_ge(sems[s], v)
                        waited[s] = v
                    if op["fn"] is None:
                        continue
                    if op["dma"] and op["prev"] > waited.get(op["sem"], 0):
                        e.wait_ge(sems[op["sem"]], op["prev"])
                        waited[op["sem"]] = op["prev"]
                    res = op["fn"](e)
                    if isinstance(res, (list, tuple)):
                        res = res[-1]
                    res.then_inc(sems[op["sem"]], 16 if op["dma"] else 1)
                if engname == "sp":
                    for s, v in final.items():
                        if v > 0:
                            e.wait_ge(sems[s], v)
            return body

        for engname in self.ENGS:
            getattr(block, handles[engname])(make(engname))


def _bf(a):
    return np.ascontiguousarray(a.astype(np.float32)).astype(NPBF)


_CONST_CACHE = {}


def host_constants():
    if _CONST_CACHE:
        return _CONST_CACHE
    c = {}
    r128 = np.arange(128, dtype=np.float64)[:, None]
    F1 = np.zeros((128, 128))
    F1[:, :65] = np.cos(2 * np.pi * r128 * np.arange(65)[None, :] / 128)
    F1[:, 65:] = -np.sin(2 * np.pi * r128 * np.arange(1, 64)[None, :] / 128)
    c["F1"] = _bf(F1)
    w = np.arange(64, dtype=np.float64)[:, None]
    k2 = np.arange(64, dtype=np.float64)[None, :]
    Eb = np.zeros((128, 65, 128))
    for s in range(65):
        ang = 2 * np.pi * w * (s + 128 * k2) / NFFT
        Mr, Mi = np.cos(ang), -np.sin(ang)
        Eb[:64, s, :64] = Mr
        Eb[:64, s, 64:] = Mi
        Eb[64:, s, :64] = -Mi
        Eb[64:, s, 64:] = Mr
    c["Ebig"] = _bf(Eb)
    ang = 2 * np.pi * np.arange(64)[None, :] * np.arange(64, dtype=np.float64)[:, None] / 64
    Vr, Vi = np.cos(ang), np.sin(ang)
    Ci = np.zeros((128, 128))
    Ci[:64, :64] = Vr
    Ci[:64, 64:] = Vi
    Ci[64:, :64] = -Vi
    Ci[64:, 64:] = Vr
    c["Cinv"] = _bf(Ci)
    U = np.zeros((65, 64, 2, 64))
    sl = np.arange(65, dtype=np.float64)[:, None]
    mb = np.arange(64, dtype=np.float64)[None, :]
    wt = np.where((sl == 0) | (sl == 64), 1.0, 2.0) / NFFT
    for a in range(64):
        ang = 2 * np.pi * (mb * sl / 128 + a * sl / NFFT)
        U[:, a, 0, :] = wt * np.cos(ang)
        U[:, a, 1, :] = -wt * np.sin(ang)
    c["U"] = _bf(U)
    L = T
    f32 = np.float32
    tlin = np.linspace(0.0, 1.0, L, dtype=f32)
    wv = (f32(2.0 * math.pi) * np.arange(L, dtype=f32) / f32(L)).astype(f32)
    fb = np.linspace(1e-4, 15, 16, dtype=f32)
    angm = (wv[:, None] * fb[None, :]).astype(f32)
    emb = np.concatenate([tlin[:, None], np.cos(angm), -np.sin(angm)], axis=-1).astype(f32)
    j = np.zeros(NFFT, dtype=np.int64)
    j[:L] = np.arange(L)
    j[L] = 0
    j[L + 1:] = L - np.arange(1, L)
    emb_all = emb[j].T.copy()
    c["emb"] = np.ascontiguousarray(emb_all.astype(f32))
    tt = tlin[j].astype(np.float64)
    tt[L] = 1.0e4
    c["ntt"] = np.ascontiguousarray((-tt).reshape(128, 64).astype(f32))
    min_decay = math.log(1e-2) / 1.5
    max_decay = math.log(1e-2) / 0.3
    deltas = np.abs(np.linspace(min_decay, max_decay, 512, dtype=f32))
    c["delta"] = np.ascontiguousarray(deltas.reshape(1, 512).astype(f32))
    s_ = np.arange(128)[:, None]
    t_ = np.arange(128)[None, :]
    c["triF"] = np.where(s_ <= t_, -1.0 / 16.0, 0.0).astype(f32)
    c["triB"] = np.where(s_ >= t_, -1.0 / 16.0, 0.0).astype(f32)
    c["triFb"] = _bf(c["triF"])
    c["triBb"] = _bf(c["triB"])
    c["maskF"] = np.where(s_ <= t_, 1.0, 0.0).astype(f32)
    c["maskB"] = np.where(s_ >= t_, 1.0, 0.0).astype(f32)
    c["ident"] = _bf(np.eye(128))
    _CONST_CACHE.update(c)
    return c


IN_SPECS = [
    ("x", [T, D], F32), ("ctx", [256, D], F32), ("cc", [128, 8, 2], F32),
    ("w_ada", [D, 6 * D], F32), ("b_adaT", [128, 48], F32), ("b_ada", [1, 6 * D], F32),
    ("nmixT", [128, 8], F32), ("nmlpT", [128, 8], F32), ("nfin", [1, D], F32),
    ("w_in", [D, 3104], F32), ("convwT", [128, 12, 3], F32), ("convbT", [128, 12], F32),
    ("fw1", [33, 64], F32), ("fw2", [64, 64], F32), ("fw3", [64, 64], F32), ("fwout", [64, 1024], F32),
    ("fcol", [64, 4], F32), ("hbiasT", [128, 4], F32),
    ("gkw", [48, 256], F32), ("gkb", [1, 512], F32), ("gnT", [128, 1], F32),
    ("w_out", [D, D], F32), ("w_mlp1", [D, 4 * D], F32), ("w_mlp2", [4 * D, D], F32),
    ("F1", [128, 128], BF16), ("Ebig", [128, 65, 128], BF16), ("Cinv", [128, 128], BF16),
    ("U", [65, 64, 2, 64], BF16), ("emb", [33, NFFT], F32), ("ntt", [128, 64], F32),
    ("delta", [1, 512], F32), ("triF", [128, 128], F32), ("triB", [128, 128], F32), ("triFb", [128, 128], BF16), ("triBb", [128, 128], BF16),
    ("maskF", [128, 128], F32), ("maskB", [128, 128], F32), ("ident", [128, 128], BF16),
]


def build(debug=False):
    nc = bass.Bass("TRN2", target_bir_lowering=False)
    d = {}
    for name, shape, dt in IN_SPECS:
        d[name] = nc.dram_tensor(name, list(shape), dt, kind="ExternalInput").ap()
    out_d = nc.dram_tensor("out", [T, D], F32, kind="ExternalOutput").ap()
    skind = "ExternalOutput" if debug else "Internal"
    hT_d = nc.dram_tensor("hT_d", [D, T], BF16, kind=skind).ap()
    yT_d = nc.dram_tensor("yT_d", [D, T], BF16, kind=skind).ap()

    S = Sched()
    st = ExitStack()
    with st:
        ARENA_F = 53000
        arena_t = st.enter_context(nc.sbuf_tensor("arena", [128, ARENA_F], F32))
        banks = [st.enter_context(nc.psum_tensor(f"bank{i}", [128, 512], F32)) for i in range(8)]
        sems = {}
        for nm_ in Sched.sem_names():
            sems[nm_] = st.enter_context(nc.semaphore("s_" + nm_))
        block = st.enter_context(nc.Block())

        def V(off, shape, dt=F32):
            esz = 4 if dt == F32 else 2
            nfree = int(np.prod(shape[1:]))
            nb = nfree * esz
            assert off % 4 == 0 and nb % 4 == 0, (off, shape)
            assert off + nb <= ARENA_F * 4, (off, shape)
            ap = arena_t[0:shape[0], off // 4:(off + nb) // 4]
            if dt != F32:
                ap = ap.bitcast(dt)
            if len(shape) == 3:
                ap = ap.rearrange("p (a b) -> p a b", a=shape[1])
            elif len(shape) == 4:
                ap = ap.rearrange("p (a b c) -> p a b c", a=shape[1], b=shape[2])
            return ap

        class Alloc:
            def __init__(self, base, limit=ARENA_F * 4):
                self.o = base
                self.limit = limit

            def __call__(self, shape, dt=F32):
                esz = 4 if dt == F32 else 2
                nb = int(np.prod(shape[1:])) * esz
                nb = (nb + 31) // 32 * 32
                v = V(self.o, shape, dt)
                self.o += nb
                assert self.o <= self.limit, ("arena overflow", self.o)
                return v

        pbank = [0]
        reserved = set()

        def nbank():
            while True:
                b = pbank[0]
                pbank[0] = (b + 1) % 8
                if b not in reserved:
                    return b

        def PB(b, dt=F32):
            ap = banks[b][:]
            return ap if dt == F32 else ap.bitcast(dt)

        def op(eng, fn, r=(), w=(), dma=False, n=1, name=""):
            return S.add(eng, fn, r=r, w=w, dma=dma, n=n, name=name)

        def dma_in(out_ap, in_ap, w, r=(), eng="sp"):
            op(eng, lambda e, o=out_ap, i=in_ap: e.dma_start(out=o, in_=i), r=r, w=w, dma=True)

        PA = Alloc(0, 16384)
        ident = PA([128, 128], BF16)
        ones_bf = PA([128, 128], BF16)
        onesf = PA([128, 128], F32)
        nfrow = PA([128, D], F32)
        PA_g1_off = PA.o
        g1row = PA([128, D], F32)
        g2row = PA([128, D], F32)
        msc1 = PA([128, 8]); msh1 = PA([128, 8]); csc1 = PA([128, 8]); csh1 = PA([128, 8])
        msc2 = PA([128, 8]); msh2 = PA([128, 8])
        epsc = PA([128, 1])
        PBA = Alloc(16384, 32768)
        hctxT = PBA([128, 8, 256], BF16)
        convwT = PBA([128, 12, 3]); convbT = PBA([128, 12]); hbiasT = PBA([128, 4]); gnT = PBA([128, 1])
        gkw = PBA([48, 256]); gkb = PBA([1, 512])
        fw1 = PBA([33, 64]); fw2 = PBA([64, 64]); fw3 = PBA([64, 64]); fwout = PBA([64, 1024])
        fcol = PBA([64, 4]); fsc = PBA([64, 8])
        nmixT = PBA([128, 8]); nmlpT = PBA([128, 8])
        modT = PBA([128, 48, 2])
        Sst = [PBA([128, 2, 128]) for _ in range(2)]
        PH0 = 32768

        dma_in(ident, d["ident"], ["ident"])
        op("pool", lambda e: e.memset(ones_bf, 1.0), w=["ones_bf"])
        op("pool", lambda e: e.memset(onesf, 1.0), w=["onesf"])
        op("pool", lambda e: e.memset(epsc, EPS), w=["epsc"])
        for nm, t_ in [("convwT", convwT), ("convbT", convbT), ("hbiasT", hbiasT), ("gnT", gnT), ("gkw", gkw),
                       ("gkb", gkb), ("fw1", fw1), ("fw2", fw2), ("fw3", fw3), ("fwout", fwout), ("fcol", fcol),
                       ("nmixT", nmixT), ("nmlpT", nmlpT)]:
            dma_in(t_, d[nm], [nm])
        dma_in(nfrow, d["nfin"][0:1, :].partition_broadcast(128), ["nfrow"])
        op("dve", lambda e: e.tensor_scalar(out=fsc[:, 0:3], in0=fcol[:, 0:3], scalar1=fcol[:, 3:4], scalar2=None,
                                            op0=ALU.mult), r=["fcol"], w=["fsc"])

        A0 = Alloc(PH0)
        ccT = A0([128, 8, 2]); scT = A0([128, 8, 2]); scTb = A0([128, 8, 2], BF16); screpb = A0([128, 8, 128], BF16)
        badaT = A0([128, 48]); brow = A0([128, 2, D])
        wab = [A0([128, 8, 1024]) for _ in range(2)]
        wabb = [A0([128, 8, 1024], BF16) for _ in range(2)]
        A1 = Alloc(A0.o)
        xt4 = [A1([128, 4, D]) for _ in range(2)]
        junk = A1([128, D], BF16)
        xn = A1([128, 4, D], BF16)
        hTg = [A1([128, 8, 512], BF16) for _ in range(2)]
        ssq = A1([128, 40]); rs = A1([128, 40])
        dma_in(ccT, d["cc"], ["ccT"])
        dma_in(badaT, d["b_adaT"], ["badaT"])
        dma_in(brow[:, 0, :], d["b_ada"][0:1, 2048:3072].partition_broadcast(128), ["brow0"])
        dma_in(brow[:, 1, :], d["b_ada"][0:1, 5120:6144].partition_broadcast(128), ["brow1"])
        op("act", lambda e: e.activation(out=scT, in_=ccT, func=AF.Silu), r=["ccT"], w=["scT"])
        op("dve", lambda e: e.tensor_copy(out=scTb, in_=scT), r=["scT"], w=["scTb"])
        op("dve", lambda e: e.tensor_copy(out=screpb, in_=scT[:, :, 0:1].to_broadcast([128, 8, 128])),
           r=["scT"], w=["screpb"])
        op("pool", lambda e: e.memset(ssq, 0.0), w=["ssq"])
        wada_v = d["w_ada"].rearrange("(k p) n -> p k n", p=128)
        bmod = 7
        reserved.add(bmod)
        mcnt = [0]

        def mod_block(blk):
            i_ = mcnt[0] % 2
            mcnt[0] += 1
            buf, bufb = wab[i_], wabb[i_]
            dma_in(buf, wada_v[:, :, blk * 1024:(blk + 1) * 1024], [("wab", i_)])
            for k in range(8):
                if k % 2 == 0:
                    op("act", lambda e, k=k: e.activation(out=bufb[:, k, :], in_=buf[:, k, :], func=AF.Copy),
                       r=[("wab", i_)], w=[("wabb", i_, k)])
                else:
                    op("dve", lambda e, k=k: e.tensor_copy(out=bufb[:, k, :], in_=buf[:, k, :]), r=[("wab", i_)],
                       w=[("wabb", i_, k)])
            rk = [("wabb", i_, k) for k in range(8)]
            if blk in (0, 1, 3, 4):
                def f(e):
                    last = None
                    for j in range(8):
                        col = (blk * 8 + j) * 2
                        for k in range(8):
                            last = e.matmul(PB(bmod)[:, col:col + 2], lhsT=bufb[:, k, j * 128:(j + 1) * 128],
                                            rhs=scTb[:, k, :], start=(k == 0), stop=(k == 7))
                    return last
                op("pe", f, r=rk + ["scTb"], w=[("psmod", blk), ("ps", bmod)])
            else:
                gi = 0 if blk == 2 else 1
                bb = [nbank(), nbank()]
                def f(e):
                    last = None
                    for h in range(2):
                        for k in range(8):
                            last = e.matmul(PB(bb[h]), lhsT=screpb[:, k, :], rhs=bufb[:, k, h * 512:(h + 1) * 512],
                                            start=(k == 0), stop=(k == 7))
                    return last
                op("pe", f, r=rk + ["screpb"], w=[("ps", bb[0]), ("ps", bb[1])])
                grow = g1row if gi == 0 else g2row
                for h in range(2):
                    op("dve", lambda e, h=h: e.tensor_tensor(
                        out=grow[:, h * 512:(h + 1) * 512], in0=PB(bb[h]), in1=brow[:, gi, h * 512:(h + 1) * 512],
                        op=ALU.add), r=[("ps", bb[h]), f"brow{gi}"], w=[("grow", gi, h)])

        def mod_finalize(c0, c1, blks, derived):
            op("dve", lambda e: e.tensor_tensor(
                out=modT[:, c0:c1, :], in0=PB(bmod)[:, 2 * c0:2 * c1].rearrange("p (a b) -> p a b", b=2),
                in1=badaT[:, c0:c1].unsqueeze(2).to_broadcast([128, c1 - c0, 2]), op=ALU.add),
               r=[("psmod", b_) for b_ in blks] + ["badaT", ("ps", bmod)], w=[("modT", c0)])
            for kind, dst, nrm, cc0, col, nm in derived:
                if kind == "scale":
                    op("dve", lambda e, dst=dst, cc0=cc0, col=col: e.tensor_scalar(
                        out=dst, in0=modT[:, cc0:cc0 + 8, col], scalar1=1.0, scalar2=None, op0=ALU.add),
                       r=[("modT", c0)], w=[nm])
                    op("dve", lambda e, dst=dst, nrm=nrm: e.tensor_tensor(out=dst, in0=dst, in1=nrm, op=ALU.mult),
                       r=[nm, "nmixT", "nmlpT"], w=[nm])
                else:
                    op("dve", lambda e, dst=dst, cc0=cc0, col=col: e.tensor_copy(out=dst, in_=modT[:, cc0:cc0 + 8, col]),
                       r=[("modT", c0)], w=[nm])

        mod_block(0)
        mod_block(1)
        mod_finalize(0, 16, (0, 1), [("scale", msc1, nmixT, 8, 0, "msc1"), ("scale", csc1, nmixT, 8, 1, "csc1"),
                                     ("shift", msh1, None, 0, 0, "msh1"), ("shift", csh1, None, 0, 1, "csh1")])
        hT_v = hT_d.rearrange("(k p) t -> p k t", p=128)

        def norm_tile(xb, xkey, ti, xn_dst, sq_scale=1.0 / D):
            op("act", lambda e: e.activation(out=junk, in_=xb, func=AF.Square, accum_out=ssq[:, ti:ti + 1]),
               r=[xkey, "ssq"], w=["junk", ("ssq", ti)])
            op("act", lambda e: e.activation(out=rs[:, ti:ti + 1], in_=ssq[:, ti:ti + 1], func=AF.Ln,
                                             scale=sq_scale, bias=epsc[:, 0:1]), r=[("ssq", ti), "epsc"], w=[("rs", ti)])
            op("act", lambda e: e.activation(out=rs[:, ti:ti + 1], in_=rs[:, ti:ti + 1], func=AF.Exp, scale=-0.5),
               r=[("rs", ti)], w=[("rs", ti)])
            op("dve", lambda e: e.tensor_scalar(out=xn_dst, in0=xb, scalar1=rs[:, ti:ti + 1], scalar2=None,
                                                op0=ALU.mult), r=[xkey, ("rs", ti)], w=[("xn", ti % 4)])

        later_blocks = {0: 3, 2: 4, 4: 2, 6: 5}
        for tg in range(9):
            is_ctx = tg == 8
            ntile = 2 if is_ctx else 4
            tb = [nbank() for _ in range(4)]
            W = ntile * 128
            xb4 = xt4[tg % 2]
            xk4 = ("xt4", tg % 2)
            if is_ctx:
                dma_in(xb4[:, 0:2, :], d["ctx"].rearrange("(i p) f -> p i f", p=128), [xk4])
            else:
                dma_in(xb4, d["x"][tg * 512:(tg + 1) * 512, :].rearrange("(i p) f -> p i f", p=128), [xk4])
            for i in range(ntile):
                ti = tg * 4 + i
                norm_tile(xb4[:, i, :], xk4, ti, xn[:, i, :])

                def f(e, i=i, tb=tb, W=W):
                    last = None
                    for k in range(8):
                        pv = PB(tb[k // 2], BF16)[:, (k % 2) * 512:(k % 2) * 512 + 512]
                        last = e.transpose(out=pv[:, i * 128:(i + 1) * 128], in_=xn[:, i, k * 128:(k + 1) * 128],
                                           identity=ident)
                    return last
                op("pe", f, r=[("xn", i), "ident"], w=[("ps", b_) for b_ in tb])
            dst = hctxT if is_ctx else hTg[tg % 2]
            dkeys = [("hctxT", k) if is_ctx else ("hTg", tg % 2, k) for k in range(8)]
            sc, sh = (csc1, csh1) if is_ctx else (msc1, msh1)
            sck = ["csc1", "csh1"] if is_ctx else ["msc1", "msh1"]
            for k in range(8):
                pv = PB(tb[k // 2], BF16)[:, (k % 2) * 512:(k % 2) * 512 + W]
                if (k // 2) % 2 == 0:
                    op("act", lambda e, k=k, pv=pv, dst=dst, sc=sc, sh=sh: e.activation(
                        out=dst[:, k, :], in_=pv, func=AF.Identity, scale=sc[:, k:k + 1], bias=sh[:, k:k + 1]),
                       r=[("ps", tb[k // 2])] + sck, w=[dkeys[k]])
                else:
                    op("dve", lambda e, k=k, pv=pv, dst=dst, sc=sc, sh=sh: e.tensor_scalar(
                        out=dst[:, k, :], in0=pv, scalar1=sc[:, k:k + 1], scalar2=sh[:, k:k + 1], op0=ALU.mult,
                        op1=ALU.add), r=[("ps", tb[k // 2])] + sck, w=[dkeys[k]])
            if not is_ctx:
                op("sp", lambda e, tg=tg: e.dma_start(out=hT_v[:, :, tg * 512:(tg + 1) * 512], in_=hTg[tg % 2]),
                   r=dkeys, w=[("hT_d", tg)], dma=True)
            if tg in later_blocks:
                mod_block(later_blocks[tg])
        mod_finalize(24, 40, (3, 4), [("scale", msc2, nmlpT, 32, 0, "msc2"), ("shift", msh2, None, 24, 0, "msh2")])
        reserved.discard(bmod)
        S.barrier()

        if debug == "p1":
            op("sp", lambda e: e.dma_start(out=out_d[0:128, :], in_=nfrow), dma=True)
            op("sp", lambda e: e.dma_start(out=out_d[128:256, :], in_=g1row), dma=True)
            op("sp", lambda e: e.dma_start(out=out_d[256:384, :], in_=g2row), dma=True)
            S.emit(sems, block)
            return nc

        hTv = hT_v
        w_in_v = d["w_in"].rearrange("(k p) n -> p k n", p=128)
        yT_v = yT_d.rearrange("(k p) t -> p k t", p=128)

        if debug in (False, "full", "m1", "m2", "m3", "m4", "m5", "gla", "gla1", "gla2", "gla3"):
            AG = Alloc(PH0)
            qT = AG([128, 2, T], BF16)
            kT = AG([128, 2, T], BF16)
            k_tok = AG([128, 32, 256], BF16)
            v_tok = AG([128, 32, 512], BF16)
            gT = AG([64, T])
            o_f = AG([128, 4, T], BF16)
            gkwb = AG([16, 256]); gsh = [AG([16, 128]) for _ in range(2)]
            triF = AG([128, 128], BF16); triB = AG([128, 128], BF16); maskF = AG([128, 128]); maskB = AG([128, 128])
            kc_tok = AG([128, 2, 256], BF16); vc_tok = AG([128, 2, 512], BF16); gTc = AG([64, 256])
            for nm, t_, dn in [("triF", triF, "triFb"), ("triB", triB, "triBb"), ("maskF", maskF, "maskF"),
                               ("maskB", maskB, "maskB")]:
                dma_in(t_, d[dn], [nm])
            dma_in(gkwb, d["gkw"][32:48, :], ["gkwb"])
            G1 = Alloc(AG.o)
            wq = G1([128, 8, 256], BF16); wk = G1([128, 8, 256], BF16); wv = G1([128, 8, 512], BF16)
            wg = G1([128, 8, 64], BF16)
            hs = [G1([128, 8, 512], BF16) for _ in range(2)]
            op("dve", lambda e: e.memset(wg, 0.0), w=["wg"])
            for k in range(8):
                for t_, c0, c1, nm in [(wq, 1536, 1792, "wq"), (wk, 1792, 2048, "wk"), (wv, 2048, 2560, "wv")]:
                    dma_in(t_[:, k, :], w_in_v[:, k, c0:c1], [nm], eng="pool")
                dma_in(wg[:, k, 0:16], w_in_v[:, k, 3072:3088], ["wg"], eng="pool")
                dma_in(wg[:, k, 32:48], w_in_v[:, k, 3088:3104], ["wg"], eng="pool")

            def proj_fm(hb, hk, wt, wkey, M0, M1, dst_ap, dkey, scale, npart=128, eng="act"):
                b = nbank()
                def f(e):
                    last = None
                    for k in range(8):
                        last = e.matmul(PB(b)[0:npart, 0:hb.shape[2]], lhsT=wt[:, k, M0:M1], rhs=hb[:, k, :],
                                        start=(k == 0), stop=(k == 7))
                    return last
                op("pe", f, r=[hk, wkey], w=[("ps", b)])
                src = PB(b)[0:npart, 0:hb.shape[2]]
                if eng == "act":
                    op("act", lambda e: e.activation(out=dst_ap, in_=src, func=AF.Copy, scale=scale),
                       r=[("ps", b)], w=[dkey])
                else:
                    op("dve", lambda e: e.tensor_copy(out=dst_ap, in_=src), r=[("ps", b)], w=[dkey])

            def proj_tm(hb, hk, i, kdst, vdst, dkey):
                bk_, bv_ = nbank(), nbank()
                def f(e):
                    last = None
                    for k in range(8):
                        last = e.matmul(PB(bk_)[:, 0:256], lhsT=hb[:, k, i * 128:(i + 1) * 128], rhs=wk[:, k, :],
                                        start=(k == 0), stop=(k == 7))
                    for k in range(8):
                        last = e.matmul(PB(bv_), lhsT=hb[:, k, i * 128:(i + 1) * 128], rhs=wv[:, k, :],
                                        start=(k == 0), stop=(k == 7))
                    return last
                op("pe", f, r=[hk, "wk", "wv"], w=[("ps", bk_), ("ps", bv_)])
                op("act", lambda e: e.activation(out=kdst, in_=PB(bk_)[:, 0:256], func=AF.Copy), r=[("ps", bk_)],
                   w=[dkey + "k"])
                op("dve", lambda e: e.tensor_copy(out=vdst, in_=PB(bv_)), r=[("ps", bv_)], w=[dkey + "v"])

            for tg in range(8):
                hb = hs[tg % 2]; hk = ("hs", tg % 2)
                dma_in(hb, hTv[:, :, tg * 512:(tg + 1) * 512], [hk], r=[("hT_d", tg)])
                sl = slice(tg * 512, (tg + 1) * 512)
                for hp in range(2):
                    proj_fm(hb, hk, wq, "wq", hp * 128, (hp + 1) * 128, qT[:, hp, sl], "qT", 0.125)
                    proj_fm(hb, hk, wk, "wk", hp * 128, (hp + 1) * 128, kT[:, hp, sl], "kT", 1.0, eng="dve")
                proj_fm(hb, hk, wg, "wg", 0, 64, gT[:, sl], "gT", 1.0, npart=64)
                for i in range(4):
                    n = tg * 4 + i
                    proj_tm(hb, hk, i, k_tok[:, n, :], v_tok[:, n, :], "tok")
            proj_fm(hctxT, "hctxT", wg, "wg", 0, 64, gTc, "gTc", 1.0, npart=64)
            for i in range(2):
                proj_tm(hctxT, "hctxT", i, kc_tok[:, i, :], vc_tok[:, i, :], "ctok")
            S.barrier()
            if debug == "gla1":
                S.emit(sems, block)
                return nc

            G2 = Alloc(AG.o)
            wog = G2([128, 8, 512], BF16)
            for k in range(8):
                dma_in(wog[:, k, :], w_in_v[:, k, 2560:3072], ["wog"], eng="pool")
            lbuf = [G2([128, 256]) for _ in range(2)]
            lbf = [G2([128, 256], BF16) for _ in range(2)]
            enb_tok = [G2([128, 256], BF16) for _ in range(2)]
            kin_tok = [G2([128, 256], BF16) for _ in range(2)]
            EbT = [G2([128, 2, 128]) for _ in range(2)]
            EnbT = [G2([128, 2, 128]) for _ in range(2)]
            qinT = [G2([128, 2, 128], BF16) for _ in range(2)]
            kinT = [G2([128, 2, 128], BF16) for _ in range(2)]
            attms = [G2([128, 4, 128], BF16) for _ in range(2)]
            Stmps = [G2([128, 2, 128]) for _ in range(2)]; S_bfs = [G2([128, 2, 128], BF16) for _ in range(2)]
            o_sb = G2([128, 4, 128]); osq = G2([128, 4, 128], BF16); rr = G2([128, 4, 128])
            sgs = [G2([128, 4, 128], BF16) for _ in range(2)]
            sbc = [0]
            ybuf = [G2([128, 4, 128], BF16) for _ in range(2)]
            hc = [G2([128, 8, 128], BF16) for _ in range(2)]
            for dr in range(2):
                op("pool", lambda e, dr=dr: e.memset(Sst[dr], 0.0), w=[("S", dr)])

            def prep(dr, n, lat, slot):
                base = 0 if dr == 0 else 32
                g_src = gT if lat else gTc
                tri = triF if dr == 0 else triB
                trik = "triF" if dr == 0 else "triB"
                kt = k_tok if lat else kc_tok
                b1 = nbank()
                if dr == 0:
                    glhs, grhs = g_src[0:16, n * 128:(n + 1) * 128], gkw[0:16, :]
                else:
                    gs_ = gsh[slot]
                    op("dve", lambda e: e.tensor_copy(out=gs_, in_=g_src[32:48, n * 128:(n + 1) * 128]),
                       r=["gT", "gTc"], w=[("gsh", slot)])
                    glhs, grhs = gs_, gkwb
                def f(e):
                    e.matmul(PB(b1)[:, 0:256], lhsT=glhs, rhs=grhs, start=True, stop=False)
                    return e.matmul(PB(b1)[:, 0:256], lhsT=onesf[0:1, 0:128], rhs=gkb[0:1, dr * 256:(dr + 1) * 256],
                                    start=False, stop=True)
                op("pe", f, r=["gT", "gTc", "gkw", "gkwb", "gkb", "onesf", ("gsh", slot)], w=[("ps", b1)])
                lb = lbuf[slot]
                op("act", lambda e: e.activation(out=lb, in_=PB(b1)[:, 0:256], func=AF.Exp, scale=-1.0),
                   r=[("ps", b1)], w=[("lb", slot)])
                lb16 = lbf[slot]
                op("act", lambda e: e.activation(out=lb16, in_=lb, func=AF.Ln, bias=onesf[:, 0:1]), r=[("lb", slot), "onesf"],
                   w=[("lb16", slot)])
                b2, b3 = nbank(), nbank()
                def f2(e):
                    e.matmul(PB(b2)[:, 0:256], lhsT=tri, rhs=lb16, start=True, stop=True)
                    last = None
                    for hp in range(2):
                        last = e.matmul(PB(b3)[:, hp * 128:(hp + 1) * 128], lhsT=lb16[:, hp * 128:(hp + 1) * 128],
                                        rhs=tri, start=True, stop=True)
                    return last
                op("pe", f2, r=[("lb16", slot), trik], w=[("ps", b2), ("ps", b3)])
                op("act", lambda e: e.activation(out=enb_tok[slot], in_=PB(b2)[:, 0:256], func=AF.Exp, scale=-1.0),
                   r=[("ps", b2)], w=[("enb", slot)])
                op("dve", lambda e: e.tensor_tensor(out=kin_tok[slot], in0=kt[:, n, :], in1=enb_tok[slot], op=ALU.mult),
                   r=[("enb", slot), "tokk", "ctokk"], w=[("kin_tok", slot)])
                b3v = PB(b3)[:, 0:256].rearrange("p (a b) -> p a b", a=2)
                op("act", lambda e: e.activation(out=EbT[slot], in_=b3v, func=AF.Exp), r=[("ps", b3)], w=[("EbT", slot)])
                if lat:
                    op("act", lambda e: e.activation(out=EnbT[slot], in_=b3v, func=AF.Exp, scale=-1.0), r=[("ps", b3)],
                       w=[("EnbT", slot)])
                    cs = slice(n * 128, (n + 1) * 128)
                    op("dve", lambda e: e.tensor_tensor(out=qinT[slot], in0=qT[:, :, cs], in1=EbT[slot], op=ALU.mult),
                       r=[("EbT", slot), "qT"], w=[("qinT", slot)])
                    op("dve", lambda e: e.tensor_tensor(out=kinT[slot], in0=kT[:, :, cs], in1=EnbT[slot], op=ALU.mult),
                       r=[("EnbT", slot), "kT"], w=[("kinT", slot)])

            def state(dr, n, lat, slot):
                vt = v_tok if lat else vc_tok
                last_col = 127 if dr == 0 else 0
                Sc = Sst[dr]
                Stmp = Stmps[dr]
                for hp in range(2):
                    b = nbank()
                    op("pe", lambda e, b=b, hp=hp: e.matmul(PB(b)[:, 0:256], lhsT=kin_tok[slot][:, hp * 128:(hp + 1) * 128],
                                                            rhs=vt[:, n, hp * 256:(hp + 1) * 256], start=True, stop=True),
                       r=[("kin_tok", slot), "tokv", "ctokv"], w=[("ps", b)])
                    dec = EbT[slot][:, hp, last_col:last_col + 1]
                    if hp == 0:
                        op("dve", lambda e, hp=hp, dec=dec: e.tensor_scalar(out=Stmp[:, hp, :], in0=Sc[:, hp, :], scalar1=dec,
                                                                            scalar2=None, op0=ALU.mult),
                           r=[("S", dr), ("EbT", slot)], w=[("Stmp", dr, hp)])
                    else:
                        op("act", lambda e, hp=hp, dec=dec: e.activation(out=Stmp[:, hp, :], in_=Sc[:, hp, :], func=AF.Copy,
                                                                         scale=dec),
                           r=[("S", dr), ("EbT", slot)], w=[("Stmp", dr, hp)])
                    def fs(e, b=b, hp=hp, dec=dec):
                        e.scalar_tensor_tensor(out=Sc[0:64, hp, :], in0=PB(b)[0:64, 0:128], scalar=dec[0:64],
                                               in1=Stmp[0:64, hp, :], op0=ALU.mult, op1=ALU.add)
                        return e.scalar_tensor_tensor(out=Sc[64:128, hp, :], in0=PB(b)[64:128, 128:256], scalar=dec[64:128],
                                                      in1=Stmp[64:128, hp, :], op0=ALU.mult, op1=ALU.add)
                    op("dve", fs, r=[("ps", b), ("Stmp", dr, hp), ("EbT", slot)], w=[("S", dr)])
                if lat:
                    op("pool", lambda e: e.tensor_copy(out=S_bfs[dr], in_=Sc), r=[("S", dr)], w=[("S_bf", dr)])

            def ogproj(n, dr):
                cs = slice(n * 128, (n + 1) * 128)
                hcb = hc[dr]
                dma_in(hcb, hTv[:, :, cs], [("hc", dr)])
                bg = nbank()
                def f3(e):
                    last = None
                    for h in range(4):
                        for k in range(8):
                            last = e.matmul(PB(bg)[:, h * 128:(h + 1) * 128], lhsT=wog[:, k, h * 128:(h + 1) * 128],
                                            rhs=hcb[:, k, :], start=(k == 0), stop=(k == 7))
                    return last
                op("pe", f3, r=[("hc", dr), "wog"], w=[("ps", bg)])
                sg = sgs[dr]
                op("act", lambda e: e.activation(out=sg.rearrange("p a b -> p (a b)"), in_=PB(bg), func=AF.Silu),
                   r=[("ps", bg)], w=[("sg", dr)])

            def output(dr, n, slot):
                S_bf = S_bfs[dr]
                sbk = ("S_bf", dr)
                attm = attms[dr]
                atk = ("attm", dr)
                mask = maskF if dr == 0 else maskB
                mk = "maskF" if dr == 0 else "maskB"
                ba = [nbank(), nbank()]
                def f(e):
                    last = None
                    for h in (0, 2, 1, 3):
                        hp, base = h // 2, 64 * (h % 2)
                        last = e.matmul(PB(ba[h % 2])[:, hp * 128:(hp + 1) * 128], lhsT=kinT[slot][base:base + 64, hp, :],
                                        rhs=qinT[slot][base:base + 64, hp, :], start=True, stop=True)
                    return last
                op("pe", f, r=[("kinT", slot), ("qinT", slot)], w=[("ps", ba[0]), ("ps", ba[1])])
                for par in range(2):
                    av = attm.rearrange("p (hp par) c -> p par hp c", par=2)[:, par, :, :]
                    op("dve", lambda e, par=par, av=av: e.tensor_tensor(
                        out=av, in0=PB(ba[par])[:, 0:256].rearrange("p (a b) -> p a b", a=2),
                        in1=mask.unsqueeze(1).to_broadcast([128, 2, 128]), op=ALU.mult),
                       r=[("ps", ba[par]), mk], w=[atk])
                bo = nbank()
                def f2(e):
                    last = None
                    for h in range(4):
                        hp, base = h // 2, 64 * (h % 2)
                        e.matmul(PB(bo)[:, h * 128:(h + 1) * 128], lhsT=v_tok[:, n, h * 128:(h + 1) * 128],
                                 rhs=attm[:, h, :], start=True, stop=False)
                        last = e.matmul(PB(bo)[:, h * 128:(h + 1) * 128], lhsT=S_bf[base:base + 64, hp, :],
                                        rhs=qinT[slot][base:base + 64, hp, :], start=False, stop=True)
                    return last
                op("pe", f2, r=[atk, sbk, ("qinT", slot), "tokv"], w=[("ps", bo)])
                return bo

            def epilogue(dr, n, bo):
                bov = PB(bo).rearrange("p (a b) -> p a b", a=4)
                cs = slice(n * 128, (n + 1) * 128)
                first = (n < 16) if dr == 0 else (n >= 16)
                if first:
                    op("act", lambda e: e.activation(out=o_f[:, :, cs], in_=bov, func=AF.Copy), r=[("ps", bo)],
                       w=[("o_f", n)])
                    return
                op("dve", lambda e: e.tensor_tensor(out=o_sb, in0=bov, in1=o_f[:, :, cs], op=ALU.add),
                   r=[("ps", bo), ("o_f", n)], w=["o_sb"])
                op("pool", lambda e: e.tensor_tensor(out=osq, in0=o_sb, in1=o_sb, op=ALU.mult), r=["o_sb"], w=["osq"])
                bs = nbank()
                op("pe", lambda e: e.matmul(PB(bs), lhsT=ones_bf, rhs=osq.rearrange("p a b -> p (a b)"), start=True,
                                            stop=True), r=["osq", "ones_bf"], w=[("ps", bs)])
                op("act", lambda e: e.activation(out=rr.rearrange("p a b -> p (a b)"), in_=PB(bs), func=AF.Ln,
                                                 scale=1.0 / 128.0, bias=epsc[:, 0:1]), r=[("ps", bs)], w=["rr"])
                op("act", lambda e: e.activation(out=rr, in_=rr, func=AF.Exp, scale=-0.5), r=["rr"], w=["rr"])
                sg = sgs[dr]
                op("dve", lambda e: e.tensor_tensor(out=o_sb, in0=o_sb, in1=rr, op=ALU.mult), r=["o_sb", "rr"], w=["o_sb"])
                yb = ybuf[dr]
                op("dve", lambda e: e.scalar_tensor_tensor(out=yb, in0=o_sb, scalar=gnT[:, 0:1], in1=sg, op0=ALU.mult,
                                                           op1=ALU.mult),
                   r=["o_sb", ("sg", dr), "gnT"], w=[("yb", dr)])
                yv = yT_d[512:1024, :].rearrange("(h p) t -> p h t", p=128)
                op("sp", lambda e: e.dma_start(out=yv[:, :, cs], in_=yb), r=[("yb", dr)], w=[("yT_gla", n)], dma=True)

            cnt = [0]
            def nslot():
                cnt[0] += 1
                return cnt[0] % 2
            for dr in range(2):
                order = [0, 1] if dr == 0 else [1, 0]
                for n in order:
                    s_ = nslot()
                    prep(dr, n, False, s_)
                    state(dr, n, False, s_)
            if debug == "gla2":
                S.emit(sems, block)
                return nc
            for dr in range(2):
                op("act", lambda e, dr=dr: e.activation(out=S_bfs[dr], in_=Sst[dr], func=AF.Copy), r=[("S", dr)],
                   w=[("S_bf", dr)])
            def nchunk(dr, i):
                return i if dr == 0 else 31 - i

            def prep_full(dr, i):
                n = nchunk(dr, i)
                prep(dr, n, True, dr)
                first = (n < 16) if dr == 0 else (n >= 16)
                if not first:
                    ogproj(n, dr)

            def step(dr, i):
                n = nchunk(dr, i)
                bo = output(dr, n, dr)
                state(dr, n, True, dr)
                epilogue(dr, n, bo)

            prep_full(0, 0)
            for i in range(32):
                prep_full(1, i)
                step(0, i)
                if i + 1 < 32:
                    prep_full(0, i + 1)
                step(1, i)
            S.barrier()

        if debug in (False, "full", "m1", "m2", "m3", "m4", "m5", "hy"):
            AH = Alloc(PH0)
            zT = AH([128, T], BF16); x0T = AH([128, T], BF16)
            yTg = zT
            off_sig = AH.o
            SIGd = AH([128, 128, 2, 64], BF16)
            SIG = V(off_sig, [128, 65, 128], BF16)
            off_A = AH.o
            Abuf = AH([128, 65, 128], BF16)
            Ybuf = AH([128, 65, 128], BF16)
            Gv = V(off_A, [65, 128, 128], BF16)
            TA = Alloc(off_A)
            hsbs = [TA([128, 8, 512], BF16) for _ in range(2)]
            u3 = [TA([128, 512]) for _ in range(3)]
            assert TA.o <= AH.o
            Hs = AH([128, 65, 128], BF16)
            Ebig_s = AH([128, 65, 128], BF16)
            U_s = AH([65, 64, 2, 64], BF16)
            hdn = AH([64, NFFT], BF16)
            F1_s = AH([128, 128], BF16); Cinv_s = AH([128, 128], BF16)
            woutg = AH([64, 256], BF16)
            wsl = AH([128, 8, 384], BF16)
            Tts = [AH([128, 4, 128]) for _ in range(2)]; T2s = [AH([128, 4, 128])] * 2
            win8 = [AH([128, 8, 128]) for _ in range(2)]; sq8 = [AH([128, 128, 8], BF16)] * 2
            ntt_s = AH([128, 64]); drow = AH([128, 512])
            normT = AH([128, 1]); tmpb = AH([128, 8, 64]); tmpg = AH([128, 8, 64])
            embc = [V(PH0 + i_ * 2048, [33, 512]) for i_ in range(2)]
            argb = [V(PH0 + 4096 + i_ * 2048, [64, 512]) for i_ in range(2)]
            tb_ = [V(PH0 + 8192 + i_ * 2048, [64, 512]) for i_ in range(2)]
            ha = V(off_sig, [64, NFFT]); hb_ = V(off_sig + 32768, [64, NFFT])
            for nm, t_ in [("F1", F1_s), ("Cinv", Cinv_s), ("Ebig", Ebig_s), ("U", U_s), ("ntt", ntt_s)]:
                dma_in(t_, d[nm], [nm + "_s"])
            dma_in(drow, d["delta"][0:1, :].partition_broadcast(128), ["drow"])

            fws = [fw1, fw2, fw3]
            for l in range(3):
                Kp = 33 if l == 0 else 64
                dst = [ha, hb_, hdn][l]
                prev = [None, ha, hb_][l]
                for ch in range(16):
                    cs = slice(ch * 512, (ch + 1) * 512)
                    if l == 0:
                        dma_in(embc[ch % 2], d["emb"][:, cs], [("embc", ch % 2)])
                        rhs = embc[ch % 2]
                        rk = ("embc", ch % 2)
                    else:
                        rhs = prev[:, cs]
                        rk = ("fl", l - 1, ch)
                    b = nbank()
                    op("pe", lambda e, b=b, l=l, Kp=Kp, rhs=rhs: e.matmul(PB(b)[0:64, :], lhsT=fws[l][0:Kp, :], rhs=rhs[0:Kp],
                                                                          start=True, stop=True),
                       r=[rk, "fw1", "fw2", "fw3"], w=[("ps", b)])
                    a_, t_ = argb[ch % 2], tb_[ch % 2]
                    def fr_(e, b=b, l=l, a_=a_, t_=t_):
                        e.tensor_scalar(out=a_, in0=PB(b)[0:64, :], scalar1=fcol[:, 3:4], scalar2=fsc[:, l:l + 1],
                                        op0=ALU.mult, op1=ALU.add)
                        return e
                    op("dve", lambda e, b=b, l=l, a_=a_: e.tensor_scalar(
                        out=a_, in0=PB(b)[0:64, :], scalar1=fcol[:, 3:4], scalar2=fsc[:, l:l + 1], op0=ALU.mult,
                        op1=ALU.add), r=[("ps", b), "fsc", "fcol"], w=[("arg", ch % 2)])
                    op("dve", lambda e, a_=a_, t_=t_: e.tensor_scalar(out=t_, in0=a_, scalar1=1.0 / (2 * PI), scalar2=MAGIC,
                                                                      op0=ALU.mult, op1=ALU.add),
                       r=[("arg", ch % 2)], w=[("tb", ch % 2)])
                    op("dve", lambda e, t_=t_: e.tensor_scalar(out=t_, in0=t_, scalar1=MAGIC, scalar2=None,
                                                               op0=ALU.subtract), r=[("tb", ch % 2)], w=[("tb", ch % 2)])
                    op("dve", lambda e, a_=a_, t_=t_: e.scalar_tensor_tensor(out=a_, in0=t_, scalar=-2 * PI, in1=a_,
                                                                             op0=ALU.mult, op1=ALU.add),
                       r=[("tb", ch % 2), ("arg", ch % 2)], w=[("arg", ch % 2)])
                    op("act", lambda e, a_=a_, dst=dst, cs=cs: e.activation(out=dst[:, cs], in_=a_, func=AF.Sin,
                                                                             scale=0.999999),
                       r=[("arg", ch % 2)], w=[("fl", l, ch)])
            S.barrier()

            hdn_v = hdn.rearrange("p (r w) -> p r w", w=64)

            def load_wsl(g_):
                for j in range(3):
                    for k in range(8):
                        dma_in(wsl[:, k, j * 128:(j + 1) * 128],
                               w_in_v[:, k, j * 512 + g_ * 128: j * 512 + (g_ + 1) * 128], ["wsl"], eng="pool")
            for g in range(4):
                gs = slice(g * 128, (g + 1) * 128)
                S.barrier()
                if g == 0:
                    load_wsl(0)
                for tg in range(8):
                    hsb = hsbs[tg % 2]
                    hkey = ("hsb", tg % 2)
                    dma_in(hsb, hTv[:, :, tg * 512:(tg + 1) * 512], [hkey], r=[("hT_d", tg)])
                    sl = slice(tg * 512, (tg + 1) * 512)
                    for j in range(3):
                        b = nbank()
                        def f(e, b=b, j=j, hsb=hsb):
                            last = None
                            for k in range(8):
                                last = e.matmul(PB(b), lhsT=wsl[:, k, j * 128:(j + 1) * 128], rhs=hsb[:, k, :],
                                                start=(k == 0), stop=(k == 7))
                            return last
                        op("pe", f, r=[hkey, "wsl"], w=[("ps", b)])
                        cw = convwT[:, j * 4 + g, :]
                        cb = convbT[:, j * 4 + g:j * 4 + g + 1]
                        uj = u3[j]
                        op("act", lambda e, b=b, uj=uj, cw=cw, cb=cb: e.activation(
                            out=uj, in_=PB(b), func=AF.Identity, scale=cw[:, 1:2], bias=cb), r=[("ps", b), "convwT", "convbT"],
                           w=[("u3", j)])
                        pv = PB(b).rearrange("p (r w) -> p r w", w=64)
                        uv = uj.rearrange("p (r w) -> p r w", w=64)
                        op("dve", lambda e, pv=pv, uv=uv, cw=cw: e.scalar_tensor_tensor(
                            out=uv[:, :, 1:64], in0=pv[:, :, 0:63], scalar=cw[:, 0:1], in1=uv[:, :, 1:64], op0=ALU.mult,
                            op1=ALU.add), r=[("ps", b), ("u3", j)], w=[("u3", j)])
                        op("dve", lambda e, pv=pv, uv=uv, cw=cw: e.scalar_tensor_tensor(
                            out=uv[:, :, 0:63], in0=pv[:, :, 1:64], scalar=cw[:, 2:3], in1=uv[:, :, 0:63], op0=ALU.mult,
                            op1=ALU.add), r=[("ps", b), ("u3", j)], w=[("u3", j)])
                    op("pool", lambda e, sl=sl: e.tensor_tensor(out=zT[:, sl], in0=u3[1], in1=u3[2], op=ALU.mult),
                       r=[("u3", 1), ("u3", 2)], w=["zT"])
                    op("pool", lambda e, sl=sl: e.tensor_copy(out=x0T[:, sl], in_=u3[0]), r=[("u3", 0)], w=["x0T"])

                S.barrier()
                if g + 1 < 4:
                    load_wsl(g + 1)
                op("pool", lambda e, g=g: e.tensor_copy(out=woutg[:, 0:128], in_=fwout[:, g * 128:(g + 1) * 128]),
                   r=["fwout"], w=["woutg"])
                op("pool", lambda e, g=g: e.tensor_copy(out=woutg[:, 128:256], in_=fwout[:, 512 + g * 128:512 + (g + 1) * 128]),
                   r=["fwout"], w=["woutg"])
                bq = 7
                for wb in range(8):
                    ws_ = wb % 2
                    for j in range(8):
                        w_ = wb * 8 + j
                        op("act", lambda e, w_=w_, j=j, ws_=ws_, gs=gs: e.activation(
                            out=win8[ws_][:, j, :], in_=drow[:, gs], func=AF.Exp, scale=ntt_s[:, w_:w_ + 1]),
                           r=["drow", "ntt_s"], w=[("win8", ws_)])
                    for jp in range(4):
                        w0 = wb * 8 + jp * 2
                        b = nbank()
                        while b == bq:
                            b = nbank()
                        def fm(e, b=b, w0=w0):
                            e.matmul(PB(b)[:, 0:256], lhsT=hdn_v[:, :, w0], rhs=woutg, start=True, stop=True)
                            return e.matmul(PB(b)[:, 256:512], lhsT=hdn_v[:, :, w0 + 1], rhs=woutg, start=True, stop=True)
                        op("pe", fm, r=["woutg"], w=[("ps", b)])
                        def fk(e, b=b, w0=w0, jp=jp, ws_=ws_):
                            pv = PB(b).rearrange("p (w x) -> p w x", w=2)
                            e.tensor_tensor(out=SIGd[0:64, :, 0, w0:w0 + 2].rearrange("p c w -> p w c"),
                                            in0=pv[0:64, :, 0:128], in1=win8[ws_][0:64, jp * 2:jp * 2 + 2, :], op=ALU.mult)
                            return e.tensor_tensor(out=SIGd[64:128, :, 0, w0:w0 + 2].rearrange("p c w -> p w c"),
                                                   in0=pv[64:128, :, 128:256], in1=win8[ws_][64:128, jp * 2:jp * 2 + 2, :],
                                                   op=ALU.mult)
                        op("dve", fk, r=[("ps", b), ("win8", ws_)], w=[("SIG", wb)])
                    op("act", lambda e, wb=wb, ws_=ws_: e.activation(out=sq8[ws_], in_=SIGd[:, :, 0, wb * 8:(wb + 1) * 8],
                                                                     func=AF.Square), r=[("SIG", wb)], w=[("sq8", 0)])
                    def fa(e, wb=wb, ws_=ws_):
                        last = None
                        for j in range(8):
                            w_ = wb * 8 + j
                            last = e.matmul(PB(bq)[:, 0:1], lhsT=sq8[ws_][:, :, j], rhs=ones_bf[:, 0:1], start=(w_ == 0),
                                            stop=(w_ == 63))
                        return last
                    op("pe", fa, r=[("sq8", 0), "ones_bf"], w=[("ps", bq)])
                op("act", lambda e: e.activation(out=normT, in_=PB(bq)[:, 0:1], func=AF.Ln, bias=epsc[:, 0:1]),
                   r=[("ps", bq)], w=["normT"])
                op("act", lambda e: e.activation(out=normT, in_=normT, func=AF.Exp, scale=-0.5), r=["normT"], w=["normT"])

                def S1(npart):
                    op("act", lambda e: e.activation(out=SIGd[0:npart, 0:64, 1, :], in_=SIGd[0:npart, 0:64, 0, :],
                                                     func=AF.Copy),
                       r=[("SIG", q_) for q_ in range(8)], w=[("SIGdupA", 0)])
                    op("dve", lambda e: e.tensor_copy(out=SIGd[0:npart, 64:128, 1, :], in_=SIGd[0:npart, 64:128, 0, :]),
                       r=[("SIG", q_) for q_ in range(8)], w=[("SIGdupB", 0)])
                    def z_(e):
                        e.memset(Abuf[64:128, 0:1, :], 0.0)
                        return e.memset(Abuf[64:128, 64:65, :], 0.0)
                    op("pool", z_, w=[("A", c_) for c_ in range(32)])
                    for c4 in range(32):
                        b = nbank()
                        def f(e, b=b, c4=c4):
                            last = None
                            for j in range(4):
                                c = c4 * 4 + j
                                lt = SIGd[0:npart, c, :, :].rearrange("p d w -> p (d w)")
                                last = e.matmul(PB(b)[:, j * 128:(j + 1) * 128], lhsT=lt, rhs=F1_s[0:npart, :],
                                                start=True, stop=True)
                            return last
                        op("pe", f, r=[("SIG", q_) for q_ in range(8)] + ["F1_s", ("SIGdupA", 0), ("SIGdupB", 0)],
                           w=[("ps", b)])
                        pr = PB(b)[0:64, :].rearrange("p (c k) -> p k c", c=4)
                        pi_ = PB(b)[64:128, :].rearrange("p (c k) -> p k c", c=4)
                        op("dve", lambda e, pr=pr, c4=c4: e.tensor_copy(out=Abuf[0:64, 0:65, c4 * 4:(c4 + 1) * 4],
                                                                        in_=pr[:, 0:65, :]), r=[("ps", b)], w=[("A", c4)])
                        op("act", lambda e, pi_=pi_, c4=c4: e.activation(out=Abuf[64:128, 1:64, c4 * 4:(c4 + 1) * 4],
                                                                         in_=pi_[:, 65:128, :], func=AF.Copy),
                           r=[("ps", b)], w=[("A", c4)])

                def S2(consumer):
                    for q in range(17):
                        b = nbank()
                        slots = list(range(4 * q, min(4 * q + 4, 65)))
                        def f(e, b=b, slots=slots):
                            last = None
                            for j, s_ in enumerate(slots):
                                last = e.matmul(PB(b)[:, j * 128:(j + 1) * 128], lhsT=Abuf[:, s_, :], rhs=Ebig_s[:, s_, :],
                                                start=True, stop=True)
                            return last
                        op("pe", f, r=[("A", c_) for c_ in range(32)] + ["Ebig_s"], w=[("ps", b)])
                        consumer(q, b, slots)

                S1(128)
                def cons_h(q, b, slots):
                    ns = len(slots)
                    pv = PB(b)[:, 0:ns * 128].rearrange("p (a b) -> p a b", a=ns)
                    op("act", lambda e: e.activation(out=Hs[:, slots[0]:slots[0] + ns, :], in_=pv, func=AF.Copy),
                       r=[("ps", b)], w=[("Hs", q)])
                S2(cons_h)

                zv = zT.rearrange("p (r w) -> p r w", w=64)
                for q in range(8):
                    b = nbank()
                    def f(e, b=b, q=q):
                        last = None
                        for j in range(8):
                            w_ = q * 8 + j
                            last = e.transpose(out=PB(b, BF16)[0:64, j * 128:(j + 1) * 128], in_=zv[:, :, w_], identity=ident)
                        return last
                    op("pe", f, r=["zT", "ident"], w=[("ps", b)])
                    src = PB(b, BF16)[0:64, :].rearrange("p (a b) -> p a b", a=8)
                    dstv = SIGd[0:64, :, 0, q * 8:(q + 1) * 8].rearrange("p c w -> p w c")
                    if q % 2 == 0:
                        op("act", lambda e, src=src, dstv=dstv: e.activation(out=dstv, in_=src, func=AF.Copy),
                           r=[("ps", b)], w=[("SIG", q)])
                    else:
                        op("dve", lambda e, src=src, dstv=dstv: e.tensor_copy(out=dstv, in_=src),
                           r=[("ps", b)], w=[("SIG", q)])
                S1(64)
                def cons_z(q, b, slots):
                    ns = len(slots)
                    s0 = slots[0]
                    Tt, T2 = Tts[q % 2], T2s[q % 2]
                    kT, kT2 = ("Tt", q % 2), ("T2", 0)
                    pv = PB(b)[:, 0:ns * 128].rearrange("p (a b) -> p a b", a=ns)
                    op("dve", lambda e: e.tensor_tensor(out=Tt[:, 0:ns, :], in0=pv, in1=Hs[:, s0:s0 + ns, :], op=ALU.mult),
                       r=[("ps", b), ("Hs", q)], w=[kT])
                    op("pool", lambda e: e.tensor_tensor(out=Ybuf[:, s0:s0 + ns, 0:64], in0=Tt[:, 0:ns, 0:64],
                                                         in1=Tt[:, 0:ns, 64:128], op=ALU.subtract), r=[kT], w=[("Y", q)])
                    def f(e):
                        e.tensor_tensor(out=T2[:, 0:ns, 0:64], in0=pv[:, :, 0:64], in1=Hs[:, s0:s0 + ns, 64:128], op=ALU.mult)
                        return e.tensor_tensor(out=T2[:, 0:ns, 64:128], in0=pv[:, :, 64:128], in1=Hs[:, s0:s0 + ns, 0:64],
                                               op=ALU.mult)
                    op("dve", f, r=[("ps", b), ("Hs", q)], w=[kT2])
                    op("pool", lambda e: e.tensor_tensor(out=Ybuf[:, s0:s0 + ns, 64:128], in0=T2[:, 0:ns, 0:64],
                                                         in1=T2[:, 0:ns, 64:128], op=ALU.add), r=[kT2], w=[("Y", q)])
                S2(cons_z)
                for q in range(9):
                    b = nbank()
                    slots = list(range(8 * q, min(8 * q + 8, 65)))
                    def f(e, b=b, slots=slots):
                        last = None
                        for j, s_ in enumerate(slots):
                            last = e.transpose(out=PB(b, BF16)[:, j * 128:(j + 1) * 128], in_=Ybuf[:, s_, :], identity=ident)
                        return last
                    op("pe", f, r=[("Y", q_) for q_ in range(17)] + ["ident"], w=[("ps", b)])
                    ns = len(slots)
                    src = PB(b, BF16)[:, 0:ns * 128].rearrange("p (a b) -> p a b", a=ns)
                    if q % 2 == 0:
                        op("act", lambda e, src=src, slots=slots, ns=ns: e.activation(
                            out=SIG[:, slots[0]:slots[0] + ns, :], in_=src, func=AF.Copy), r=[("ps", b)],
                           w=[("SIG", q_) for q_ in range(8)])
                    else:
                        op("dve", lambda e, src=src, slots=slots, ns=ns: e.tensor_copy(
                            out=SIG[:, slots[0]:slots[0] + ns, :], in_=src), r=[("ps", b)],
                           w=[("SIG", q_) for q_ in range(8)])
                allAY = [("A", c_) for c_ in range(32)] + [("Y", q_) for q_ in range(17)]
                for c4 in range(32):
                    b = nbank()
                    def f(e, b=b, c4=c4):
                        last = None
                        for j in range(4):
                            c = c4 * 4 + j
                            last = e.matmul(PB(b)[0:65, j * 128:(j + 1) * 128], lhsT=SIG[:, 0:65, c], rhs=Cinv_s, start=True,
                                            stop=True)
                        return last
                    op("pe", f, r=[("SIG", q_) for q_ in range(8)] + ["Cinv_s"], w=[("ps", b)])
                    src = PB(b)[0:65, :].rearrange("p (c k) -> p k c", c=4)
                    wk_ = allAY if c4 == 0 else []
                    if c4 % 2 == 0:
                        op("act", lambda e, src=src, c4=c4: e.activation(out=Gv[:, :, c4 * 4:(c4 + 1) * 4], in_=src,
                                                                         func=AF.Copy), r=[("ps", b)], w=[("G", c4)] + wk_)
                    else:
                        op("dve", lambda e, src=src, c4=c4: e.tensor_copy(out=Gv[:, :, c4 * 4:(c4 + 1) * 4], in_=src),
                           r=[("ps", b)] + ([("G", 0)] if True else []), w=[("G", c4)])
                def tv(ap, q):
                    return ap.rearrange("p (mb ma) -> p ma mb", ma=64)[:, q * 8:(q + 1) * 8, :]
                for q in range(8):
                    b = nbank()
                    def f(e, b=b, q=q):
                        last = None
                        for j in range(8):
                            ma = q * 8 + j
                            e.matmul(PB(b)[:, j * 64:(j + 1) * 64], lhsT=Gv[:, ma, :], rhs=U_s[:, ma, 0, :], start=True,
                                     stop=False)
                            last = e.matmul(PB(b)[:, j * 64:(j + 1) * 64], lhsT=Gv[:, 64 + ma, :], rhs=U_s[:, ma, 1, :],
                                            start=False, stop=True)
                        return last
                    op("pe", f, r=[("G", c_) for c_ in range(32)] + ["U_s"] + allAY, w=[("ps", b)])
                    op("act", lambda e, q=q, g=g: e.activation(out=tmpb, in_=tv(zT, q), func=AF.Copy,
                                                               scale=hbiasT[:, g:g + 1]), r=["zT", "hbiasT"], w=["tmpb"])
                    op("dve", lambda e, b=b: e.scalar_tensor_tensor(
                        out=tmpg, in0=PB(b).rearrange("p (a b) -> p a b", a=8), scalar=normT[:, 0:1], in1=tmpb, op0=ALU.mult,
                        op1=ALU.add), r=[("ps", b), "normT", "tmpb"], w=["tmpg"])
                    op("pool", lambda e, q=q: e.tensor_tensor(out=tv(yTg, q), in0=tmpg, in1=tv(x0T, q), op=ALU.mult),
                       r=["tmpg", "x0T"], w=["yTg"])
                op("sp", lambda e, gs=gs: e.dma_start(out=yT_d[gs, :], in_=yTg), r=["yTg"], w=[("yT_hy", g)], dma=True)
            S.barrier()

        if debug in ("gla", "hy"):
            S.emit(sems, block)
            return nc

        AM = Alloc(16384)
        woutp = AM([128, 8, D], BF16)
        w1s = AM([128, 8, 4 * D], BF16)
        w2s = AM([128, 32, D], BF16)
        off_hid = AM.o
        hidT = AM([128, 32, 256], BF16)
        off_xtm = AM.o
        stgs = [V(off_hid, [128, 4, D]), V(off_xtm, [128, 4, D])]
        xtm = [AM([128, D]) for _ in range(4)]
        yg = AM([128, 8, 256], BF16)
        h2Ts = [AM([128, 8, 256], BF16) for _ in range(2)]
        rl = [AM([128, 512], BF16) for _ in range(2)]
        ssq2 = AM([128, 64]); rs2 = AM([128, 64])
        xnm = V(PA_g1_off, [128, 2, D], BF16)
        junkB = V(PA_g1_off + 4096, [128, D], BF16)
        w1_v = d["w_mlp1"].rearrange("(k p) n -> p k n", p=128)
        w2_v = d["w_mlp2"].rearrange("(k p) n -> p k n", p=128)
        wo_v = d["w_out"].rearrange("(k p) n -> p k n", p=128)
        op("pool", lambda e: e.memset(ssq2, 0.0), w=["ssq2"])
        for k in range(8):
            dma_in(w1s[:, k, :], w1_v[:, k, :], [("w1s", k)], eng="pool")
        for rd in range(10):
            stg = stgs[rd % 2]
            sk = ("stg", rd % 2)
            if rd < 2:
                dma_in(stg, wo_v[:, rd * 4:(rd + 1) * 4, :], [sk])
            else:
                dma_in(stg, w2_v[:, (rd - 2) * 4:(rd - 1) * 4, :], [sk])
            for kk in range(4):
                eng = "dve" if kk % 2 == 0 else "pool"
                if rd < 2:
                    op(eng, lambda e, rd=rd, kk=kk, stg=stg: e.tensor_tensor(out=woutp[:, rd * 4 + kk, :], in0=stg[:, kk, :],
                                                                             in1=g1row, op=ALU.mult), r=[sk], w=["woutp"])
                else:
                    op(eng, lambda e, rd=rd, kk=kk, stg=stg: e.tensor_tensor(out=w2s[:, (rd - 2) * 4 + kk, :], in0=stg[:, kk, :],
                                                                             in1=g2row, op=ALU.mult), r=[sk], w=["w2s"])
        S.barrier()

        def rms_stats(src, col, sq_key, junk_ap, junk_key):
            op("act", lambda e: e.activation(out=junk_ap, in_=src, func=AF.Square, accum_out=ssq2[:, col:col + 1]),
               r=[sq_key, "ssq2"], w=[junk_key, ("ssq2", col)])
            op("act", lambda e: e.activation(out=rs2[:, col:col + 1], in_=ssq2[:, col:col + 1], func=AF.Ln, scale=1.0 / D,
                                             bias=epsc[:, 0:1]), r=[("ssq2", col)], w=[("rs2", col)])
            op("act", lambda e: e.activation(out=rs2[:, col:col + 1], in_=rs2[:, col:col + 1], func=AF.Exp, scale=-0.5),
               r=[("rs2", col)], w=[("rs2", col)])

        tbs = {}

        def stageA1(gq):
            dma_in(yg, yT_v[:, :, gq * 256:(gq + 1) * 256], ["yg"])
            for i in range(2):
                ti = gq * 2 + i
                slot = ti % 4
                xb = xtm[slot]
                dma_in(xb, d["x"][ti * 128:(ti + 1) * 128, :], [("xtm", slot)])
                bh = [nbank(), nbank()]
                def f(e, i=i, bh=bh):
                    last = None
                    for h in range(2):
                        for k in range(8):
                            last = e.matmul(PB(bh[h]), lhsT=yg[:, k, i * 128:(i + 1) * 128],
                                            rhs=woutp[:, k, h * 512:(h + 1) * 512], start=(k == 0), stop=(k == 7))
                    return last
                op("pe", f, r=["yg", "woutp"], w=[("ps", bh[0]), ("ps", bh[1])])
                for h in range(2):
                    op("dve", lambda e, h=h, bh=bh, xb=xb: e.tensor_tensor(
                        out=xb[:, h * 512:(h + 1) * 512], in0=PB(bh[h]), in1=xb[:, h * 512:(h + 1) * 512], op=ALU.add),
                       r=[("ps", bh[h]), ("xtm", slot)], w=[("xtm", slot)])
                rms_stats(xb, ti, ("xtm", slot), xnm[:, i, :], ("xnm", i))
                op("dve", lambda e, i=i, xb=xb, ti=ti: e.tensor_scalar(out=xnm[:, i, :], in0=xb, scalar1=rs2[:, ti:ti + 1],
                                                                       scalar2=None, op0=ALU.mult),
                   r=[("xtm", slot), ("rs2", ti)], w=[("xnm", i)])

        def stageA2(gq):
            tb = [nbank(), nbank()]
            hb = h2Ts[gq % 2]
            hk = ("h2T", gq % 2)
            for i in range(2):
                def ft(e, i=i, tb=tb):
                    last = None
                    for k in range(8):
                        last = e.transpose(out=PB(tb[k // 4], BF16)[:, (k % 4) * 256 + i * 128:(k % 4) * 256 + (i + 1) * 128],
                                           in_=xnm[:, i, k * 128:(k + 1) * 128], identity=ident)
                    return last
                op("pe", ft, r=[("xnm", i), "ident"], w=[("ps", tb[0]), ("ps", tb[1])])
            for k in range(8):
                pv = PB(tb[k // 4], BF16)[:, (k % 4) * 256:(k % 4 + 1) * 256]
                if k // 4 == 0:
                    op("act", lambda e, k=k, pv=pv, hb=hb: e.activation(out=hb[:, k, :], in_=pv, func=AF.Identity,
                                                                        scale=msc2[:, k:k + 1], bias=msh2[:, k:k + 1]),
                       r=[("ps", tb[k // 4])], w=[(hk, k)])
                else:
                    op("dve", lambda e, k=k, pv=pv, hb=hb: e.tensor_scalar(out=hb[:, k, :], in0=pv, scalar1=msc2[:, k:k + 1],
                                                                           scalar2=msh2[:, k:k + 1], op0=ALU.mult, op1=ALU.add),
                       r=[("ps", tb[k // 4])], w=[(hk, k)])

        def stageB1(gq):
            hb = h2Ts[gq % 2]
            hk = ("h2T", gq % 2)
            for hc2 in range(16):
                b = nbank()
                def f(e, b=b, hc2=hc2):
                    last = None
                    for j in range(2):
                        hcx = hc2 * 2 + j
                        for k in range(8):
                            last = e.matmul(PB(b)[:, j * 256:(j + 1) * 256], lhsT=w1s[:, k, hcx * 128:(hcx + 1) * 128],
                                            rhs=hb[:, k, :], start=(k == 0), stop=(k == 7))
                    return last
                op("pe", f, r=[(hk, k) for k in range(8)] + [("w1s", k) for k in range(8)], w=[("ps", b)])
                rb = rl[hc2 % 2]
                op("act", lambda e, b=b, rb=rb: e.activation(out=rb, in_=PB(b), func=AF.Relu), r=[("ps", b)],
                   w=[("rl", hc2 % 2)])
                op("pool" if hc2 % 2 else "dve", lambda e, rb=rb, hc2=hc2: e.tensor_tensor(
                    out=hidT[:, hc2 * 2:(hc2 + 1) * 2, :], in0=rb.rearrange("p (a b) -> p a b", a=2),
                    in1=rb.rearrange("p (a b) -> p a b", a=2), op=ALU.mult), r=[("rl", hc2 % 2)], w=[("hidT", hc2)])

        def stageB2(gq):
            for i in range(2):
                ti = gq * 2 + i
                slot = ti % 4
                xb = xtm[slot]
                bh = [nbank(), nbank()]
                def f(e, i=i, bh=bh):
                    last = None
                    for h in range(2):
                        for k in range(32):
                            last = e.matmul(PB(bh[h]), lhsT=hidT[:, k, i * 128:(i + 1) * 128],
                                            rhs=w2s[:, k, h * 512:(h + 1) * 512], start=(k == 0), stop=(k == 31))
                    return last
                op("pe", f, r=[("hidT", q_) for q_ in range(16)] + ["w2s"], w=[("ps", bh[0]), ("ps", bh[1])])
                for h in range(2):
                    op("dve", lambda e, h=h, bh=bh, xb=xb: e.tensor_tensor(
                        out=xb[:, h * 512:(h + 1) * 512], in0=PB(bh[h]), in1=xb[:, h * 512:(h + 1) * 512], op=ALU.add),
                       r=[("ps", bh[h]), ("xtm", slot)], w=[("xtm", slot)])
                rms_stats(xb, 32 + ti, ("xtm", slot), junkB, "junkB")
                op("dve", lambda e, xb=xb, ti=ti: e.tensor_scalar(out=xb, in0=xb, scalar1=rs2[:, 32 + ti:33 + ti],
                                                                   scalar2=None, op0=ALU.mult),
                   r=[("xtm", slot), ("rs2", 32 + ti)], w=[("xtm", slot)])
                op("pool", lambda e, xb=xb: e.tensor_tensor(out=xb, in0=xb, in1=nfrow, op=ALU.mult),
                   r=[("xtm", slot), "nfrow"], w=[("xtm", slot)])
                op("sp", lambda e, xb=xb, ti=ti: e.dma_start(out=out_d[ti * 128:(ti + 1) * 128, :], in_=xb),
                   r=[("xtm", slot)], w=[("out", ti)], dma=True)

        stageA1(0)
        stageA2(0)
        for gq in range(16):
            stageB1(gq)
            if gq + 1 < 16:
                stageA1(gq + 1)
            stageB2(gq)
            if gq + 1 < 16:
                stageA2(gq + 1)
        if debug == "m5":
            S.barrier()
            op("sp", lambda e: e.dma_start(out=out_d[0:128, :], in_=nfrow), dma=True)
            op("sp", lambda e: e.dma_start(out=out_d[128:256, :], in_=g1row), dma=True)
        if debug == "m4":
            S.barrier()
            op("sp", lambda e: e.dma_start(out=out_d[3968:4096, 0:64], in_=ssq2), dma=True)
            op("sp", lambda e: e.dma_start(out=out_d[3968:4096, 64:128], in_=rs2), dma=True)
        S.emit(sems, block)
    return nc


def make_in_maps(inputs):
    f = lambda a: np.ascontiguousarray(np.asarray(a, dtype=np.float32))
    c = host_constants()
    x = f(inputs["x"]); cvec = f(inputs["c"]); ctx = f(inputs["ctx"]); c_ctx = f(inputs["c_ctx"])
    common = {}
    common["w_ada"] = f(inputs["w_ada"][0])
    b_ada = f(inputs["b_ada"][0])
    common["b_adaT"] = np.ascontiguousarray(b_ada.reshape(48, 128).T)
    common["b_ada"] = b_ada.reshape(1, -1)
    common["nmixT"] = np.ascontiguousarray(f(inputs["norm_mix"][0]).reshape(8, 128).T)
    common["nmlpT"] = np.ascontiguousarray(f(inputs["norm_mlp"][0]).reshape(8, 128).T)
    common["nfin"] = f(inputs["norm_final"]).reshape(1, -1)
    common["w_in"] = f(inputs["w_in"][0])
    cw = f(inputs["conv_w"][0])
    common["convwT"] = np.ascontiguousarray(cw.reshape(3, 12, 128).transpose(2, 1, 0))
    common["convbT"] = np.ascontiguousarray(f(inputs["conv_b"][0]).reshape(12, 128).T)
    common["fw1"] = f(inputs["filt_w1"][0]); common["fw2"] = f(inputs["filt_w2"][0]); common["fw3"] = f(inputs["filt_w3"][0])
    common["fwout"] = f(inputs["filt_wout"][0])
    common["fcol"] = np.ascontiguousarray(np.stack([f(inputs["filt_b1"][0]), f(inputs["filt_b2"][0]),
                                                    f(inputs["filt_b3"][0]), f(inputs["filt_freq"][0])], axis=1))
    common["hbiasT"] = np.ascontiguousarray(f(inputs["hyena_bias"][0]).reshape(4, 128).T)
    gkw = np.zeros((48, 256), np.float32)
    gkw[0:16] = f(inputs["gk_w_fwd"][0]); gkw[32:48] = f(inputs["gk_w_bwd"][0])
    common["gkw"] = gkw
    common["gkb"] = np.concatenate([f(inputs["gk_b_fwd"][0]), f(inputs["gk_b_bwd"][0])]).reshape(1, 512)
    common["gnT"] = f(inputs["gla_norm"][0]).reshape(128, 1)
    common["w_out"] = f(inputs["w_out"][0]); common["w_mlp1"] = f(inputs["w_mlp1"][0]); common["w_mlp2"] = f(inputs["w_mlp2"][0])
    for k in ["F1", "Ebig", "Cinv", "U", "emb", "ntt", "delta", "triF", "triB", "triFb", "triBb", "maskF", "maskB", "ident"]:
        common[k] = c[k]
    maps = []
    for b in range(8):
        m = dict(common)
        m["x"] = x[b]
        m["ctx"] = ctx[b]
        cc = np.stack([cvec[b], c_ctx], axis=1)
        m["cc"] = np.ascontiguousarray(cc.reshape(8, 128, 2).transpose(1, 0, 2))
        maps.append(m)
    return maps


_NC_CACHE = {}


def kernel(**inputs):
    if "nc" not in _NC_CACHE:
        _NC_CACHE["nc"] = build(debug=DEBUG)
    nc = _NC_CACHE["nc"]
    maps = make_in_maps(inputs)
    res = run_bass_kernel_spmd(nc, maps, core_ids=list(range(8)))
    out = np.stack([np.asarray(r["out"], dtype=np.float32) for r in res.results], axis=0)
    return out
```

```python
import math
import os
GSTOP = int(os.environ.get('GLA_STOP', '99'))
SAME_ENGINE_RELAX = os.environ.get('RELAX_SAME') is not None
from contextlib import ExitStack
import numpy as np
import ml_dtypes
import concourse.bass as bass
import concourse.mybir as mybir
from concourse.bass_utils import run_bass_kernel_spmd

F32 = mybir.dt.float32
BF16 = mybir.dt.bfloat16
AF = mybir.ActivationFunctionType
ALU = mybir.AluOpType
NPBF = ml_dtypes.bfloat16

T = 4096
D = 1024
EPS = 1e-6
NFFT = 8192
PI = math.pi
MAGIC = 12582912.0
DEBUG = False


class Sched:
    ENGS = ("pe", "act", "dve", "pool", "sp")
    NDMA = {"sp": 16, "pool": 8, "act": 2, "dve": 1, "pe": 1}

    def __init__(self):
        self.ops = []
        self.last_w = {}
        self.readers = {}
        self.dcnt = {}

    @classmethod
    def sem_names(cls):
        names = []
        for e in cls.ENGS:
            names.append(e)
            for j in range(cls.NDMA[e]):
                names.append(f"{e}_dma{j}")
        return names

    def add(self, eng, fn, r=(), w=(), dma=False, n=1, name=""):
        assert n == 1
        deps = set()
        raw = set()
        for k in r:
            if k in self.last_w:
                deps.add(self.last_w[k])
                raw.add(self.last_w[k])
        for k in w:
            if k in self.last_w:
                deps.add(self.last_w[k])
            deps |= self.readers.get(k, set())
        idx = len(self.ops)
        if dma:
            c = self.dcnt.get(eng, 0)
            self.dcnt[eng] = c + 1
            sem = f"{eng}_dma{c % self.NDMA[eng]}"
        else:
            sem = eng
        self.ops.append(dict(eng=eng, fn=fn, deps=sorted(deps), raw=raw, dma=dma, n=n, name=name, sem=sem))
        for k in r:
            self.readers.setdefault(k, set()).add(idx)
        for k in w:
            self.last_w[k] = idx
            self.readers[k] = set()
        return idx

    def barrier(self):
        last = {}
        for i, op in enumerate(self.ops):
            if op["fn"] is not None:
                last[op["sem"]] = i
        deps = sorted(last.values())
        for e in self.ENGS:
            self.ops.append(dict(eng=e, fn=None, deps=deps, raw=set(deps), dma=False, n=1, name="barrier", sem=e))
        self.last_w = {}
        self.readers = {}

    class _Fake:
        class _I:
            def then_inc(self, *a, **k):
                return self

        def __init__(self):
            self.calls = []

        def __getattr__(self, name):
            def f(*args, **kw):
                out = kw.get("out", args[0] if args else None)
                n = 1
                try:
                    for d_ in out.shape[1:]:
                        n *= int(d_)
                except Exception:
                    n = 256
                self.calls.append((name, n))
                return Sched._Fake._I()
            return f

    def _cost(self, op):
        fk = Sched._Fake()
        try:
            op["fn"](fk)
        except Exception:
            return 1.0, 0.0
        eng = op["eng"]
        c = 0.0
        lat = 0.0
        for name, n in fk.calls:
            if name == "dma_start":
                c += 0.15
                lat = max(lat, 2.5 + n * 3.0 / 150e3)
            elif eng == "pe":
                c += max(64, n) / 1400.0 + 0.04
            elif eng == "act":
                c += n / 1100.0 + 0.25
            elif eng == "dve":
                c += n / 900.0 + 0.12
            else:
                c += n / 450.0 + 0.25
        return max(c, 0.05), lat

    def reschedule(self):
        ops = self.ops
        n = len(ops)
        new_order = []
        i = 0
        SLACK = float(os.environ.get('SLACK', '1.0'))
        XLAT = float(os.environ.get('XLAT', '2.5'))
        while i < n:
            if ops[i]["fn"] is None:
                new_order.append(i)
                i += 1
                continue
            j = i
            while j < n and ops[j]["fn"] is not None:
                j += 1
            seg = list(range(i, j))
            segset = set(seg)
            cost = {}
            lat = {}
            for k in seg:
                cost[k], lat[k] = self._cost(ops[k])
            succ = {k: [] for k in seg}
            indeg = {k: 0 for k in seg}
            for k in seg:
                for d_ in ops[k]["deps"]:
                    if d_ in segset:
                        succ[d_].append(k)
                        indeg[k] += 1
            prio = {}
            for k in reversed(seg):
                prio[k] = cost[k] + lat[k] + max([prio[x_] for x_ in succ[k]], default=0.0)
            eng_free = {}
            fin = {}
            ready = [k for k in seg if indeg[k] == 0]
            sched = []
            while ready:
                cands = []
                for k in ready:
                    e_ = ops[k]["eng"]
                    t = eng_free.get(e_, 0.0)
                    for d_ in ops[k]["deps"]:
                        if d_ in segset:
                            t = max(t, fin[d_] + (0.0 if ops[d_]["eng"] == e_ and not ops[d_]["dma"] else XLAT))
                    cands.append((t, -prio[k], k))
                tmin = min(c_[0] for c_ in cands)
                sel = min((c_ for c_ in cands if c_[0] <= tmin + SLACK), key=lambda c_: (c_[1], c_[0], c_[2]))
                t, _, k = sel
                ready.remove(k)
                e_ = ops[k]["eng"]
                eng_free[e_] = t + cost[k]
                fin[k] = t + cost[k] + lat[k]
                sched.append((t, k))
                for x_ in succ[k]:
                    indeg[x_] -= 1
                    if indeg[x_] == 0:
                        ready.append(x_)
            assert len(sched) == len(seg)
            sched.sort()
            new_order.extend(k for _, k in sched)
            i = j
        remap = {old: new for new, old in enumerate(new_order)}
        new_ops = []
        for old in new_order:
            op = ops[old]
            op["deps"] = sorted(remap[d_] for d_ in op["deps"])
            op["raw"] = set(remap[d_] for d_ in op["raw"])
            new_ops.append(op)
        last = {}
        for idx, op in enumerate(new_ops):
            if op["fn"] is None:
                op["deps"] = sorted(last.values())
                op["raw"] = set(op["deps"])
            else:
                last[op["sem"]] = idx
        self.ops = new_ops

    def emit(self, sems, block):
        if os.environ.get("NO_RESCHED") is None:
            self.reschedule()
        run = {}
        for op in self.ops:
            s = op["sem"]
            if op["fn"] is None:
                continue
            op["prev"] = run.get(s, 0)
            run[s] = run.get(s, 0) + (16 if op["dma"] else 1)
            op["val"] = run[s]
        final = dict(run)
        handles = dict(pe="tensor", act="scalar", dve="vector", pool="gpsimd", sp="sync")

        def make(engname):
            def body(e):
                waited = {}
                for op in self.ops:
                    if op["eng"] != engname:
                        continue
                    for d in op["deps"]:
                        dop = self.ops[d]
                        if dop["fn"] is None:
                            continue
                        if dop["eng"] == "pe" and engname == "pe" and not dop["dma"]:
                            continue
                        if (SAME_ENGINE_RELAX and dop["eng"] == engname and not dop["dma"] and not op["dma"]
                                and d not in op["raw"]):
                            continue
                        s, v = dop["sem"], dop["val"]
                        if waited.get(s, 0) >= v:
                            continue
                        e.wait_ge(sems[s], v)
                        waited[s] = v
                    if op["fn"] is None:
                        continue
                    if op["dma"] and op["prev"] > waited.get(op["sem"], 0):
                        e.wait_ge(sems[op["sem"]], op["prev"])
                        waited[op["sem"]] = op["prev"]
                    res = op["fn"](e)
                    if isinstance(res, (list, tuple)):
                        res = res[-1]
                    res.then_inc(sems[op["sem"]], 16 if op["dma"] else 1)
                if engname == "sp":
                    for s, v in final.items():
                        if v > 0:
                            e.wait_ge(sems[s], v)
            return body

        for engname in self.ENGS:
            getattr(block, handles[engname])(make(engname))


def _bf(a):
    return np.ascontiguousarray(a.astype(np.float32)).astype(NPBF)


_CONST_CACHE = {}


def host_constants():
    if _CONST_CACHE:
        return _CONST_CACHE
    c = {}
    r128 = np.arange(128, dtype=np.float64)[:, None]
    F1 = np.zeros((128, 128))
    F1[:, :65] = np.cos(2 * np.pi * r128 * np.arange(65)[None, :] / 128)
    F1[:, 65:] = -np.sin(2 * np.pi * r128 * np.arange(1, 64)[None, :] / 128)
    c["F1"] = _bf(F1)
    w = np.arange(64, dtype=np.float64)[:, None]
    k2 = np.arange(64, dtype=np.float64)[None, :]
    Eb = np.zeros((128, 65, 128))
    for s in range(65):
        ang = 2 * np.pi * w * (s + 128 * k2) / NFFT
        Mr, Mi = np.cos(ang), -np.sin(ang)
        Eb[:64, s, :64] = Mr
        Eb[:64, s, 64:] = Mi
        Eb[64:, s, :64] = -Mi
        Eb[64:, s, 64:] = Mr
    c["Ebig"] = _bf(Eb)
    ang = 2 * np.pi * np.arange(64)[None, :] * np.arange(64, dtype=np.float64)[:, None] / 64
    Vr, Vi = np.cos(ang), np.sin(ang)
    Ci = np.zeros((128, 128))
    Ci[:64, :64] = Vr
    Ci[:64, 64:] = Vi
    Ci[64:, :64] = -Vi
    Ci[64:, 64:] = Vr
    c["Cinv"] = _bf(Ci)
    U = np.zeros((65, 64, 2, 64))
    sl = np.arange(65, dtype=np.float64)[:, None]
    mb = np.arange(64, dtype=np.float64)[None, :]
    wt = np.where((sl == 0) | (sl == 64), 1.0, 2.0) / NFFT
    for a in range(64):
        ang = 2 * np.pi * (mb * sl / 128 + a * sl / NFFT)
        U[:, a, 0, :] = wt * np.cos(ang)
        U[:, a, 1, :] = -wt * np.sin(ang)
    c["U"] = _bf(U)
    L = T
    f32 = np.float32
    tlin = np.linspace(0.0, 1.0, L, dtype=f32)
    wv = (f32(2.0 * math.pi) * np.arange(L, dtype=f32) / f32(L)).astype(f32)
    fb = np.linspace(1e-4, 15, 16, dtype=f32)
    angm = (wv[:, None] * fb[None, :]).astype(f32)
    emb = np.concatenate([tlin[:, None], np.cos(angm), -np.sin(angm)], axis=-1).astype(f32)
    j = np.zeros(NFFT, dtype=np.int64)
    j[:L] = np.arange(L)
    j[L] = 0
    j[L + 1:] = L - np.arange(1, L)
    emb_all = emb[j].T.copy()
    c["emb"] = np.ascontiguousarray(emb_all.astype(f32))
    tt = tlin[j].astype(np.float64)
    tt[L] = 1.0e4
    c["ntt"] = np.ascontiguousarray((-tt).reshape(128, 64).astype(f32))
    min_decay = math.log(1e-2) / 1.5
    max_decay = math.log(1e-2) / 0.3
    deltas = np.abs(np.linspace(min_decay, max_decay, 512, dtype=f32))
    c["delta"] = np.ascontiguousarray(deltas.reshape(1, 512).astype(f32))
    s_ = np.arange(128)[:, None]
    t_ = np.arange(128)[None, :]
    c["triF"] = np.where(s_ <= t_, -1.0 / 16.0, 0.0).astype(f32)
    c["triB"] = np.where(s_ >= t_, -1.0 / 16.0, 0.0).astype(f32)
    c["triFb"] = _bf(c["triF"])
    c["triBb"] = _bf(c["triB"])
    c["maskF"] = np.where(s_ <= t_, 1.0, 0.0).astype(f32)
    c["maskB"] = np.where(s_ >= t_, 1.0, 0.0).astype(f32)
    c["ident"] = _bf(np.eye(128))
    _CONST_CACHE.update(c)
    return c


IN_SPECS = [
    ("x", [T, D], F32), ("ctx", [256, D], F32), ("cc", [128, 8, 2], F32),
    ("w_ada", [D, 6 * D], F32), ("b_adaT", [128, 48], F32), ("b_ada", [1, 6 * D], F32),
    ("nmixT", [128, 8], F32), ("nmlpT", [128, 8], F32), ("nfin", [1, D], F32),
    ("w_in", [D, 3104], F32), ("convwT", [128, 12, 3], F32), ("convbT", [128, 12], F32),
    ("fw1", [33, 64], F32), ("fw2", [64, 64], F32), ("fw3", [64, 64], F32), ("fwout", [64, 1024], F32),
    ("fcol", [64, 4], F32), ("hbiasT", [128, 4], F32),
    ("gkw", [48, 256], F32), ("gkb", [1, 512], F32), ("gnT", [128, 1], F32),
    ("w_out", [D, D], F32), ("w_mlp1", [D, 4 * D], F32), ("w_mlp2", [4 * D, D], F32),
    ("F1", [128, 128], BF16), ("Ebig", [128, 65, 128], BF16), ("Cinv", [128, 128], BF16),
    ("U", [65, 64, 2, 64], BF16), ("emb", [33, NFFT], F32), ("ntt", [128, 64], F32),
    ("delta", [1, 512], F32), ("triF", [128, 128], F32), ("triB", [128, 128], F32), ("triFb", [128, 128], BF16), ("triBb", [128, 128], BF16),
    ("maskF", [128, 128], F32), ("maskB", [128, 128], F32), ("ident", [128, 128], BF16),
]


def build(debug=False):
    nc = bass.Bass("TRN2", target_bir_lowering=False)
    d = {}
    for name, shape, dt in IN_SPECS:
        d[name] = nc.dram_tensor(name, list(shape), dt, kind="ExternalInput").ap()
    out_d = nc.dram_tensor("out", [T, D], F32, kind="ExternalOutput").ap()
    skind = "ExternalOutput" if debug else "Internal"
    hT_d = nc.dram_tensor("hT_d", [D, T], BF16, kind=skind).ap()
    yT_d = nc.dram_tensor("yT_d", [D, T], BF16, kind=skind).ap()

    S = Sched()
    st = ExitStack()
    with st:
        ARENA_F = 53000
        arena_t = st.enter_context(nc.sbuf_tensor("arena", [128, ARENA_F], F32))
        banks = [st.enter_context(nc.psum_tensor(f"bank{i}", [128, 512], F32)) for i in range(8)]
        sems = {}
        for nm_ in Sched.sem_names():
            sems[nm_] = st.enter_context(nc.semaphore("s_" + nm_))
        block = st.enter_context(nc.Block())

        def V(off, shape, dt=F32):
            esz = 4 if dt == F32 else 2
            nfree = int(np.prod(shape[1:]))
            nb = nfree * esz
            assert off % 4 == 0 and nb % 4 == 0, (off, shape)
            assert off + nb <= ARENA_F * 4, (off, shape)
            ap = arena_t[0:shape[0], off // 4:(off + nb) // 4]
            if dt != F32:
                ap = ap.bitcast(dt)
            if len(shape) == 3:
                ap = ap.rearrange("p (a b) -> p a b", a=shape[1])
            elif len(shape) == 4:
                ap = ap.rearrange("p (a b c) -> p a b c", a=shape[1], b=shape[2])
            return ap

        class Alloc:
            def __init__(self, base, limit=ARENA_F * 4):
                self.o = base
                self.limit = limit

            def __call__(self, shape, dt=F32):
                esz = 4 if dt == F32 else 2
                nb = int(np.prod(shape[1:])) * esz
                nb = (nb + 31) // 32 * 32
                v = V(self.o, shape, dt)
                self.o += nb
                assert self.o <= self.limit, ("arena overflow", self.o)
                return v

        pbank = [0]
        reserved = set()

        def nbank():
            while True:
                b = pbank[0]
                pbank[0] = (b + 1) % 8
                if b not in reserved:
                    return b

        def PB(b, dt=F32):
            ap = banks[b][:]
            return ap if dt == F32 else ap.bitcast(dt)

        def op(eng, fn, r=(), w=(), dma=False, n=1, name=""):
            return S.add(eng, fn, r=r, w=w, dma=dma, n=n, name=name)

        def dma_in(out_ap, in_ap, w, r=(), eng="sp"):
            op(eng, lambda e, o=out_ap, i=in_ap: e.dma_start(out=o, in_=i), r=r, w=w, dma=True)

        PA = Alloc(0, 16384)
        ident = PA([128, 128], BF16)
        ones_bf = PA([128, 128], BF16)
        onesf = PA([128, 128], F32)
        nfrow = PA([128, D], F32)
        PA_g1_off = PA.o
        g1row = PA([128, D], F32)
        g2row = PA([128, D], F32)
        msc1 = PA([128, 8]); msh1 = PA([128, 8]); csc1 = PA([128, 8]); csh1 = PA([128, 8])
        msc2 = PA([128, 8]); msh2 = PA([128, 8])
        epsc = PA([128, 1])
        PBA = Alloc(16384, 32768)
        hctxT = PBA([128, 8, 256], BF16)
        convwT = PBA([128, 12, 3]); convbT = PBA([128, 12]); hbiasT = PBA([128, 4]); gnT = PBA([128, 1])
        gkw = PBA([48, 256]); gkb = PBA([1, 512])
        fw1 = PBA([33, 64]); fw2 = PBA([64, 64]); fw3 = PBA([64, 64]); fwout = PBA([64, 1024])
        fcol = PBA([64, 4]); fsc = PBA([64, 8])
        nmixT = PBA([128, 8]); nmlpT = PBA([128, 8])
        modT = PBA([128, 48, 2])
        Sst = [PBA([128, 2, 128]) for _ in range(2)]
        PH0 = 32768

        dma_in(ident, d["ident"], ["ident"])
        op("pool", lambda e: e.memset(ones_bf, 1.0), w=["ones_bf"])
        op("pool", lambda e: e.memset(onesf, 1.0), w=["onesf"])
        op("pool", lambda e: e.memset(epsc, EPS), w=["epsc"])
        for nm, t_ in [("convwT", convwT), ("convbT", convbT), ("hbiasT", hbiasT), ("gnT", gnT), ("gkw", gkw),
                       ("gkb", gkb), ("fw1", fw1), ("fw2", fw2), ("fw3", fw3), ("fwout", fwout), ("fcol", fcol),
                       ("nmixT", nmixT), ("nmlpT", nmlpT)]:
            dma_in(t_, d[nm], [nm])
        dma_in(nfrow, d["nfin"][0:1, :].partition_broadcast(128), ["nfrow"])
        op("dve", lambda e: e.tensor_scalar(out=fsc[:, 0:3], in0=fcol[:, 0:3], scalar1=fcol[:, 3:4], scalar2=None,
                                            op0=ALU.mult), r=["fcol"], w=["fsc"])

        A0 = Alloc(PH0)
        ccT = A0([128, 8, 2]); scT = A0([128, 8, 2]); scTb = A0([128, 8, 2], BF16); screpb = A0([128, 8, 128], BF16)
        badaT = A0([128, 48]); brow = A0([128, 2, D])
        wab = [A0([128, 8, 1024]) for _ in range(2)]
        wabb = [A0([128, 8, 1024], BF16) for _ in range(2)]
        A1 = Alloc(A0.o)
        xt4 = [A1([128, 4, D]) for _ in range(2)]
        junk = A1([128, D], BF16)
        xn = A1([128, 4, D], BF16)
        hTg = [A1([128, 8, 512], BF16) for _ in range(2)]
        ssq = A1([128, 40]); rs = A1([128, 40])
        dma_in(ccT, d["cc"], ["ccT"])
        dma_in(badaT, d["b_adaT"], ["badaT"])
        dma_in(brow[:, 0, :], d["b_ada"][0:1, 2048:3072].partition_broadcast(128), ["brow0"])
        dma_in(brow[:, 1, :], d["b_ada"][0:1, 5120:6144].partition_broadcast(128), ["brow1"])
        op("act", lambda e: e.activation(out=scT, in_=ccT, func=AF.Silu), r=["ccT"], w=["scT"])
        op("dve", lambda e: e.tensor_copy(out=scTb, in_=scT), r=["scT"], w=["scTb"])
        op("dve", lambda e: e.tensor_copy(out=screpb, in_=scT[:, :, 0:1].to_broadcast([128, 8, 128])),
           r=["scT"], w=["screpb"])
        op("pool", lambda e: e.memset(ssq, 0.0), w=["ssq"])
        wada_v = d["w_ada"].rearrange("(k p) n -> p k n", p=128)
        bmod = 7
        reserved.add(bmod)
        mcnt = [0]

        def mod_block(blk):
            i_ = mcnt[0] % 2
            mcnt[0] += 1
            buf, bufb = wab[i_], wabb[i_]
            dma_in(buf, wada_v[:, :, blk * 1024:(blk + 1) * 1024], [("wab", i_)])
            for k in range(8):
                if k % 2 == 0:
                    op("act", lambda e, k=k: e.activation(out=bufb[:, k, :], in_=buf[:, k, :], func=AF.Copy),
                       r=[("wab", i_)], w=[("wabb", i_, k)])
                else:
                    op("dve", lambda e, k=k: e.tensor_copy(out=bufb[:, k, :], in_=buf[:, k, :]), r=[("wab", i_)],
                       w=[("wabb", i_, k)])
            rk = [("wabb", i_, k) for k in range(8)]
            if blk in (0, 1, 3, 4):
                def f(e):
                    last = None
                    for j in range(8):
                        col = (blk * 8 + j) * 2
                        for k in range(8):
                            last = e.matmul(PB(bmod)[:, col:col + 2], lhsT=bufb[:, k, j * 128:(j + 1) * 128],
                                            rhs=scTb[:, k, :], start=(k == 0), stop=(k == 7))
                    return last
                op("pe", f, r=rk + ["scTb"], w=[("psmod", blk), ("ps", bmod)])
            else:
                gi = 0 if blk == 2 else 1
                bb = [nbank(), nbank()]
                def f(e):
                    last = None
                    for h in range(2):
                        for k in range(8):
                            last = e.matmul(PB(bb[h]), lhsT=screpb[:, k, :], rhs=bufb[:, k, h * 512:(h + 1) * 512],
                                            start=(k == 0), stop=(k == 7))
                    return last
                op("pe", f, r=rk + ["screpb"], w=[("ps", bb[0]), ("ps", bb[1])])
                grow = g1row if gi == 0 else g2row
                for h in range(2):
                    op("dve", lambda e, h=h: e.tensor_tensor(
                        out=grow[:, h * 512:(h + 1) * 512], in0=PB(bb[h]), in1=brow[:, gi, h * 512:(h + 1) * 512],
                        op=ALU.add), r=[("ps", bb[h]), f"brow{gi}"], w=[("grow", gi, h)])

        def mod_finalize(c0, c1, blks, derived):
            op("dve", lambda e: e.tensor_tensor(
                out=modT[:, c0:c1, :], in0=PB(bmod)[:, 2 * c0:2 * c1].rearrange("p (a b) -> p a b", b=2),
                in1=badaT[:, c0:c1].unsqueeze(2).to_broadcast([128, c1 - c0, 2]), op=ALU.add),
               r=[("psmod", b_) for b_ in blks] + ["badaT", ("ps", bmod)], w=[("modT", c0)])
            for kind, dst, nrm, cc0, col, nm in derived:
                if kind == "scale":
                    op("dve", lambda e, dst=dst, cc0=cc0, col=col: e.tensor_scalar(
                        out=dst, in0=modT[:, cc0:cc0 + 8, col], scalar1=1.0, scalar2=None, op0=ALU.add),
                       r=[("modT", c0)], w=[nm])
                    op("dve", lambda e, dst=dst, nrm=nrm: e.tensor_tensor(out=dst, in0=dst, in1=nrm, op=ALU.mult),
                       r=[nm, "nmixT", "nmlpT"], w=[nm])
                else:
                    op("dve", lambda e, dst=dst, cc0=cc0, col=col: e.tensor_copy(out=dst, in_=modT[:, cc0:cc0 + 8, col]),
                       r=[("modT", c0)], w=[nm])

        mod_block(0)
        mod_block(1)
        mod_finalize(0, 16, (0, 1), [("scale", msc1, nmixT, 8, 0, "msc1"), ("scale", csc1, nmixT, 8, 1, "csc1"),
                                     ("shift", msh1, None, 0, 0, "msh1"), ("shift", csh1, None, 0, 1, "csh1")])
        hT_v = hT_d.rearrange("(k p) t -> p k t", p=128)

        def norm_tile(xb, xkey, ti, xn_dst, sq_scale=1.0 / D):
            op("act", lambda e: e.activation(out=junk, in_=xb, func=AF.Square, accum_out=ssq[:, ti:ti + 1]),
               r=[xkey, "ssq"], w=["junk", ("ssq", ti)])
            op("act", lambda e: e.activation(out=rs[:, ti:ti + 1], in_=ssq[:, ti:ti + 1], func=AF.Ln,
                                             scale=sq_scale, bias=epsc[:, 0:1]), r=[("ssq", ti), "epsc"], w=[("rs", ti)])
            op("act", lambda e: e.activation(out=rs[:, ti:ti + 1], in_=rs[:, ti:ti + 1], func=AF.Exp, scale=-0.5),
               r=[("rs", ti)], w=[("rs", ti)])
            op("dve", lambda e: e.tensor_scalar(out=xn_dst, in0=xb, scalar1=rs[:, ti:ti + 1], scalar2=None,
                                                op0=ALU.mult), r=[xkey, ("rs", ti)], w=[("xn", ti % 4)])

        later_blocks = {0: 3, 2: 4, 4: 2, 6: 5}
        for tg in range(9):
            is_ctx = tg == 8
            ntile = 2 if is_ctx else 4
            tb = [nbank() for _ in range(4)]
            W = ntile * 128
            xb4 = xt4[tg % 2]
            xk4 = ("xt4", tg % 2)
            if is_ctx:
                dma_in(xb4[:, 0:2, :], d["ctx"].rearrange("(i p) f -> p i f", p=128), [xk4])
            else:
                dma_in(xb4, d["x"][tg * 512:(tg + 1) * 512, :].rearrange("(i p) f -> p i f", p=128), [xk4])
            for i in range(ntile):
                ti = tg * 4 + i
                norm_tile(xb4[:, i, :], xk4, ti, xn[:, i, :])

                def f(e, i=i, tb=tb, W=W):
                    last = None
                    for k in range(8):
                        pv = PB(tb[k // 2], BF16)[:, (k % 2) * 512:(k % 2) * 512 + 512]
                        last = e.transpose(out=pv[:, i * 128:(i + 1) * 128], in_=xn[:, i, k * 128:(k + 1) * 128],
                                           identity=ident)
                    return last
                op("pe", f, r=[("xn", i), "ident"], w=[("ps", b_) for b_ in tb])
            dst = hctxT if is_ctx else hTg[tg % 2]
            dkeys = [("hctxT", k) if is_ctx else ("hTg", tg % 2, k) for k in range(8)]
            sc, sh = (csc1, csh1) if is_ctx else (msc1, msh1)
            sck = ["csc1", "csh1"] if is_ctx else ["msc1", "msh1"]
            for k in range(8):
                pv = PB(tb[k // 2], BF16)[:, (k % 2) * 512:(k % 2) * 512 + W]
                if (k // 2) % 2 == 0:
                    op("act", lambda e, k=k, pv=pv, dst=dst, sc=sc, sh=sh: e.activation(
                        out=dst[:, k, :], in_=pv, func=AF.Identity, scale=sc[:, k:k + 1], bias=sh[:, k:k + 1]),
                       r=[("ps", tb[k // 2])] + sck, w=[dkeys[k]])
                else:
                    op("dve", lambda e, k=k, pv=pv, dst=dst, sc=sc, sh=sh: e.tensor_scalar(
                        out=dst[:, k, :], in0=pv, scalar1=sc[:, k:k + 1], scalar2=sh[:, k:k + 1], op0=ALU.mult,
                        op1=ALU.add), r=[("ps", tb[k // 2])] + sck, w=[dkeys[k]])
            if not is_ctx:
                op("sp", lambda e, tg=tg: e.dma_start(out=hT_v[:, :, tg * 512:(tg + 1) * 512], in_=hTg[tg % 2]),
                   r=dkeys, w=[("hT_d", tg)], dma=True)
            if tg in later_blocks:
                mod_block(later_blocks[tg])
        mod_finalize(24, 40, (3, 4), [("scale", msc2, nmlpT, 32, 0, "msc2"), ("shift", msh2, None, 24, 0, "msh2")])
        reserved.discard(bmod)
        S.barrier()

        if debug == "p1":
            op("sp", lambda e: e.dma_start(out=out_d[0:128, :], in_=nfrow), dma=True)
            op("sp", lambda e: e.dma_start(out=out_d[128:256, :], in_=g1row), dma=True)
            op("sp", lambda e: e.dma_start(out=out_d[256:384, :], in_=g2row), dma=True)
            S.emit(sems, block)
            return nc

        hTv = hT_v
        w_in_v = d["w_in"].rearrange("(k p) n -> p k n", p=128)
        yT_v = yT_d.rearrange("(k p) t -> p k t", p=128)

        if debug in (False, "full", "m1", "m2", "m3", "m4", "m5", "gla", "gla1", "gla2", "gla3"):
            AG = Alloc(PH0)
            qT = AG([128, 2, T], BF16)
            kT = AG([128, 2, T], BF16)
            k_tok = AG([128, 32, 256], BF16)
            v_tok = AG([128, 32, 512], BF16)
            gT = AG([64, T])
            o_f = AG([128, 4, T], BF16)
            gkwb = AG([16, 256]); gsh = [AG([16, 128]) for _ in range(2)]
            triF = AG([128, 128], BF16); triB = AG([128, 128], BF16); maskF = AG([128, 128]); maskB = AG([128, 128])
            kc_tok = AG([128, 2, 256], BF16); vc_tok = AG([128, 2, 512], BF16); gTc = AG([64, 256])
            for nm, t_, dn in [("triF", triF, "triFb"), ("triB", triB, "triBb"), ("maskF", maskF, "maskF"),
                               ("maskB", maskB, "maskB")]:
                dma_in(t_, d[dn], [nm])
            dma_in(gkwb, d["gkw"][32:48, :], ["gkwb"])
            G1 = Alloc(AG.o)
            wq = G1([128, 8, 256], BF16); wk = G1([128, 8, 256], BF16); wv = G1([128, 8, 512], BF16)
            wg = G1([128, 8, 64], BF16)
            hs = [G1([128, 8, 512], BF16) for _ in range(2)]
            op("dve", lambda e: e.memset(wg, 0.0), w=["wg"])
            for k in range(8):
                for t_, c0, c1, nm in [(wq, 1536, 1792, "wq"), (wk, 1792, 2048, "wk"), (wv, 2048, 2560, "wv")]:
                    dma_in(t_[:, k, :], w_in_v[:, k, c0:c1], [nm], eng="pool")
                dma_in(wg[:, k, 0:16], w_in_v[:, k, 3072:3088], ["wg"], eng="pool")
                dma_in(wg[:, k, 32:48], w_in_v[:, k, 3088:3104], ["wg"], eng="pool")

            def proj_fm(hb, hk, wt, wkey, M0, M1, dst_ap, dkey, scale, npart=128, eng="act"):
                b = nbank()
                def f(e):
                    last = None
                    for k in range(8):
                        last = e.matmul(PB(b)[0:npart, 0:hb.shape[2]], lhsT=wt[:, k, M0:M1], rhs=hb[:, k, :],
                                        start=(k == 0), stop=(k == 7))
                    return last
                op("pe", f, r=[hk, wkey], w=[("ps", b)])
                src = PB(b)[0:npart, 0:hb.shape[2]]
                if eng == "act":
                    op("act", lambda e: e.activation(out=dst_ap, in_=src, func=AF.Copy, scale=scale),
                       r=[("ps", b)], w=[dkey])
                else:
                    op("dve", lambda e: e.tensor_copy(out=dst_ap, in_=src), r=[("ps", b)], w=[dkey])

            def proj_tm(hb, hk, i, kdst, vdst, dkey):
                bk_, bv_ = nbank(), nbank()
                def f(e):
                    last = None
                    for k in range(8):
                        last = e.matmul(PB(bk_)[:, 0:256], lhsT=hb[:, k, i * 128:(i + 1) * 128], rhs=wk[:, k, :],
                                        start=(k == 0), stop=(k == 7))
                    for k in range(8):
                        last = e.matmul(PB(bv_), lhsT=hb[:, k, i * 128:(i + 1) * 128], rhs=wv[:, k, :],
                                        start=(k == 0), stop=(k == 7))
                    return last
                op("pe", f, r=[hk, "wk", "wv"], w=[("ps", bk_), ("ps", bv_)])
                op("act", lambda e: e.activation(out=kdst, in_=PB(bk_)[:, 0:256], func=AF.Copy), r=[("ps", bk_)],
                   w=[dkey + "k"])
                op("dve", lambda e: e.tensor_copy(out=vdst, in_=PB(bv_)), r=[("ps", bv_)], w=[dkey + "v"])

            for tg in range(8):
                hb = hs[tg % 2]; hk = ("hs", tg % 2)
                dma_in(hb, hTv[:, :, tg * 512:(tg + 1) * 512], [hk], r=[("hT_d", tg)])
                sl = slice(tg * 512, (tg + 1) * 512)
                for hp in range(2):
                    proj_fm(hb, hk, wq, "wq", hp * 128, (hp + 1) * 128, qT[:, hp, sl], "qT", 0.125)
                    proj_fm(hb, hk, wk, "wk", hp * 128, (hp + 1) * 128, kT[:, hp, sl], "kT", 1.0, eng="dve")
                proj_fm(hb, hk, wg, "wg", 0, 64, gT[:, sl], "gT", 1.0, npart=64)
                for i in range(4):
                    n = tg * 4 + i
                    proj_tm(hb, hk, i, k_tok[:, n, :], v_tok[:, n, :], "tok")
            proj_fm(hctxT, "hctxT", wg, "wg", 0, 64, gTc, "gTc", 1.0, npart=64)
            for i in range(2):
                proj_tm(hctxT, "hctxT", i, kc_tok[:, i, :], vc_tok[:, i, :], "ctok")
            S.barrier()
            if debug == "gla1":
                S.emit(sems, block)
                return nc

            G2 = Alloc(AG.o)
            wog = G2([128, 8, 512], BF16)
            for k in range(8):
                dma_in(wog[:, k, :], w_in_v[:, k, 2560:3072], ["wog"], eng="pool")
            lbuf = [G2([128, 256]) for _ in range(2)]
            lbf = [G2([128, 256], BF16) for _ in range(2)]
            enb_tok = [G2([128, 256], BF16) for _ in range(2)]
            kin_tok = [G2([128, 256], BF16) for _ in range(2)]
            EbT = [G2([128, 2, 128]) for _ in range(2)]
            EnbT = [G2([128, 2, 128]) for _ in range(2)]
            qinT = [G2([128, 2, 128], BF16) for _ in range(2)]
            kinT = [G2([128, 2, 128], BF16) for _ in range(2)]
            attms = [G2([128, 4, 128], BF16) for _ in range(2)]
            Stmps = [G2([128, 2, 128]) for _ in range(2)]; S_bfs = [G2([128, 2, 128], BF16) for _ in range(2)]
            o_sb = G2([128, 4, 128]); osq = G2([128, 4, 128], BF16); rr = G2([128, 4, 128])
            sgs = [G2([128, 4, 128], BF16) for _ in range(2)]
            sbc = [0]
            ybuf = [G2([128, 4, 128], BF16) for _ in range(2)]
            hc = [G2([128, 8, 128], BF16) for _ in range(2)]
            for dr in range(2):
                op("pool", lambda e, dr=dr: e.memset(Sst[dr], 0.0), w=[("S", dr)])

            def prep(dr, n, lat, slot):
                base = 0 if dr == 0 else 32
                g_src = gT if lat else gTc
                tri = triF if dr == 0 else triB
                trik = "triF" if dr == 0 else "triB"
                kt = k_tok if lat else kc_tok
                b1 = nbank()
                if dr == 0:
                    glhs, grhs = g_src[0:16, n * 128:(n + 1) * 128], gkw[0:16, :]
                else:
                    gs_ = gsh[slot]
                    op("dve", lambda e: e.tensor_copy(out=gs_, in_=g_src[32:48, n * 128:(n + 1) * 128]),
                       r=["gT", "gTc"], w=[("gsh", slot)])
                    glhs, grhs = gs_, gkwb
                def f(e):
                    e.matmul(PB(b1)[:, 0:256], lhsT=glhs, rhs=grhs, start=True, stop=False)
                    return e.matmul(PB(b1)[:, 0:256], lhsT=onesf[0:1, 0:128], rhs=gkb[0:1, dr * 256:(dr + 1) * 256],
                                    start=False, stop=True)
                op("pe", f, r=["gT", "gTc", "gkw", "gkwb", "gkb", "onesf", ("gsh", slot)], w=[("ps", b1)])
                lb = lbuf[slot]
                op("act", lambda e: e.activation(out=lb, in_=PB(b1)[:, 0:256], func=AF.Exp, scale=-1.0),
                   r=[("ps", b1)], w=[("lb", slot)])
                lb16 = lbf[slot]
                op("act", lambda e: e.activation(out=lb16, in_=lb, func=AF.Ln, bias=onesf[:, 0:1]), r=[("lb", slot), "onesf"],
                   w=[("lb16", slot)])
                b2, b3 = nbank(), nbank()
                def f2(e):
                    e.matmul(PB(b2)[:, 0:256], lhsT=tri, rhs=lb16, start=True, stop=True)
                    last = None
                    for hp in range(2):
                        last = e.matmul(PB(b3)[:, hp * 128:(hp + 1) * 128], lhsT=lb16[:, hp * 128:(hp + 1) * 128],
                                        rhs=tri, start=True, stop=True)
                    return last
                op("pe", f2, r=[("lb16", slot), trik], w=[("ps", b2), ("ps", b3)])
                op("act", lambda e: e.activation(out=enb_tok[slot], in_=PB(b2)[:, 0:256], func=AF.Exp, scale=-1.0),
                   r=[("ps", b2)], w=[("enb", slot)])
                op("dve", lambda e: e.tensor_tensor(out=kin_tok[slot], in0=kt[:, n, :], in1=enb_tok[slot], op=ALU.mult),
                   r=[("enb", slot), "tokk", "ctokk"], w=[("kin_tok", slot)])
                b3v = PB(b3)[:, 0:256].rearrange("p (a b) -> p a b", a=2)
                op("act", lambda e: e.activation(out=EbT[slot], in_=b3v, func=AF.Exp), r=[("ps", b3)], w=[("EbT", slot)])
                if lat:
                    op("act", lambda e: e.activation(out=EnbT[slot], in_=b3v, func=AF.Exp, scale=-1.0), r=[("ps", b3)],
                       w=[("EnbT", slot)])
                    cs = slice(n * 128, (n + 1) * 128)
                    op("dve", lambda e: e.tensor_tensor(out=qinT[slot], in0=qT[:, :, cs], in1=EbT[slot], op=ALU.mult),
                       r=[("EbT", slot), "qT"], w=[("qinT", slot)])
                    op("dve", lambda e: e.tensor_tensor(out=kinT[slot], in0=kT[:, :, cs], in1=EnbT[slot], op=ALU.mult),
                       r=[("EnbT", slot), "kT"], w=[("kinT", slot)])

            def state(dr, n, lat, slot):
                vt = v_tok if lat else vc_tok
                last_col = 127 if dr == 0 else 0
                Sc = Sst[dr]
                Stmp = Stmps[dr]
                for hp in range(2):
                    b = nbank()
                    op("pe", lambda e, b=b, hp=hp: e.matmul(PB(b)[:, 0:256], lhsT=kin_tok[slot][:, hp * 128:(hp + 1) * 128],
                                                            rhs=vt[:, n, hp * 256:(hp + 1) * 256], start=True, stop=True),
                       r=[("kin_tok", slot), "tokv", "ctokv"], w=[("ps", b)])
                    dec = EbT[slot][:, hp, last_col:last_col + 1]
                    if hp == 0:
                        op("dve", lambda e, hp=hp, dec=dec: e.tensor_scalar(out=Stmp[:, hp, :], in0=Sc[:, hp, :], scalar1=dec,
                                                                            scalar2=None, op0=ALU.mult),
                           r=[("S", dr), ("EbT", slot)], w=[("Stmp", dr, hp)])
                    else:
                        op("act", lambda e, hp=hp, dec=dec: e.activation(out=Stmp[:, hp, :], in_=Sc[:, hp, :], func=AF.Copy,
                                                                         scale=dec),
                           r=[("S", dr), ("EbT", slot)], w=[("Stmp", dr, hp)])
                    def fs(e, b=b, hp=hp, dec=dec):
                        e.scalar_tensor_tensor(out=Sc[0:64, hp, :], in0=PB(b)[0:64, 0:128], scalar=dec[0:64],
                                               in1=Stmp[0:64, hp, :], op0=ALU.mult, op1=ALU.add)
                        return e.scalar_tensor_tensor(out=Sc[64:128, hp, :], in0=PB(b)[64:128, 128:256], scalar=dec[64:128],
                                                      in1=Stmp[64:128, hp, :], op0=ALU.mult, op1=ALU.add)
                    op("dve", fs, r=[("ps", b), ("Stmp", dr, hp), ("EbT", slot)], w=[("S", dr)])
                if lat:
                    op("pool", lambda e: e.tensor_copy(out=S_bfs[dr], in_=Sc), r=[("S", dr)], w=[("S_bf", dr)])

            def ogproj(n, dr):
                cs = slice(n * 128, (n + 1) * 128)
                hcb = hc[dr]
                dma_in(hcb, hTv[:, :, cs], [("hc", dr)])
                bg = nbank()
                def f3(e):
                    last = None
                    for h in range(4):
                        for k in range(8):
                            last = e.matmul(PB(bg)[:, h * 128:(h + 1) * 128], lhsT=wog[:, k, h * 128:(h + 1) * 128],
                                            rhs=hcb[:, k, :], start=(k == 0), stop=(k == 7))
                    return last
                op("pe", f3, r=[("hc", dr), "wog"], w=[("ps", bg)])
                sg = sgs[dr]
                op("act", lambda e: e.activation(out=sg.rearrange("p a b -> p (a b)"), in_=PB(bg), func=AF.Silu),
                   r=[("ps", bg)], w=[("sg", dr)])

            def output(dr, n, slot):
                S_bf = S_bfs[dr]
                sbk = ("S_bf", dr)
                attm = attms[dr]
                atk = ("attm", dr)
                mask = maskF if dr == 0 else maskB
                mk = "maskF" if dr == 0 else "maskB"
                ba = [nbank(), nbank()]
                def f(e):
                    last = None
                    for h in (0, 2, 1, 3):
                        hp, base = h // 2, 64 * (h % 2)
                        last = e.matmul(PB(ba[h % 2])[:, hp * 128:(hp + 1) * 128], lhsT=kinT[slot][base:base + 64, hp, :],
                                        rhs=qinT[slot][base:base + 64, hp, :], start=True, stop=True)
                    return last
                op("pe", f, r=[("kinT", slot), ("qinT", slot)], w=[("ps", ba[0]), ("ps", ba[1])])
                for par in range(2):
                    av = attm.rearrange("p (hp par) c -> p par hp c", par=2)[:, par, :, :]
                    op("dve", lambda e, par=par, av=av: e.tensor_tensor(
                        out=av, in0=PB(ba[par])[:, 0:256].rearrange("p (a b) -> p a b", a=2),
                        in1=mask.unsqueeze(1).to_broadcast([128, 2, 128]), op=ALU.mult),
                       r=[("ps", ba[par]), mk], w=[atk])
                bo = nbank()
                def f2(e):
                    last = None
                    for h in range(4):
                        hp, base = h // 2, 64 * (h % 2)
                        e.matmul(PB(bo)[:, h * 128:(h + 1) * 128], lhsT=v_tok[:, n, h * 128:(h + 1) * 128],
                                 rhs=attm[:, h, :], start=True, stop=False)
                        last = e.matmul(PB(bo)[:, h * 128:(h + 1) * 128], lhsT=S_bf[base:base + 64, hp, :],
                                        rhs=qinT[slot][base:base + 64, hp, :], start=False, stop=True)
                    return last
                op("pe", f2, r=[atk, sbk, ("qinT", slot), "tokv"], w=[("ps", bo)])
                return bo

            def epilogue(dr, n, bo):
                bov = PB(bo).rearrange("p (a b) -> p a b", a=4)
                cs = slice(n * 128, (n + 1) * 128)
                first = (n < 16) if dr == 0 else (n >= 16)
                if first:
                    op("act", lambda e: e.activation(out=o_f[:, :, cs], in_=bov, func=AF.Copy), r=[("ps", bo)],
                       w=[("o_f", n)])
                    return
                op("dve", lambda e: e.tensor_tensor(out=o_sb, in0=bov, in1=o_f[:, :, cs], op=ALU.add),
                   r=[("ps", bo), ("o_f", n)], w=["o_sb"])
                op("pool", lambda e: e.tensor_tensor(out=osq, in0=o_sb, in1=o_sb, op=ALU.mult), r=["o_sb"], w=["osq"])
                bs = nbank()
                op("pe", lambda e: e.matmul(PB(bs), lhsT=ones_bf, rhs=osq.rearrange("p a b -> p (a b)"), start=True,
                                            stop=True), r=["osq", "ones_bf"], w=[("ps", bs)])
                op("act", lambda e: e.activation(out=rr.rearrange("p a b -> p (a b)"), in_=PB(bs), func=AF.Ln,
                                                 scale=1.0 / 128.0, bias=epsc[:, 0:1]), r=[("ps", bs)], w=["rr"])
                op("act", lambda e: e.activation(out=rr, in_=rr, func=AF.Exp, scale=-0.5), r=["rr"], w=["rr"])
                sg = sgs[dr]
                op("dve", lambda e: e.tensor_tensor(out=o_sb, in0=o_sb, in1=rr, op=ALU.mult), r=["o_sb", "rr"], w=["o_sb"])
                yb = ybuf[dr]
                op("dve", lambda e: e.scalar_tensor_tensor(out=yb, in0=o_sb, scalar=gnT[:, 0:1], in1=sg, op0=ALU.mult,
                                                           op1=ALU.mult),
                   r=["o_sb", ("sg", dr), "gnT"], w=[("yb", dr)])
                yv = yT_d[512:1024, :].rearrange("(h p) t -> p h t", p=128)
                op("sp", lambda e: e.dma_start(out=yv[:, :, cs], in_=yb), r=[("yb", dr)], w=[("yT_gla", n)], dma=True)

            cnt = [0]
            def nslot():
                cnt[0] += 1
                return cnt[0] % 2
            for dr in range(2):
                order = [0, 1] if dr == 0 else [1, 0]
                for n in order:
                    s_ = nslot()
                    prep(dr, n, False, s_)
                    state(dr, n, False, s_)
            if debug == "gla2":
                S.emit(sems, block)
                return nc
            for dr in range(2):
                op("act", lambda e, dr=dr: e.activation(out=S_bfs[dr], in_=Sst[dr], func=AF.Copy), r=[("S", dr)],
                   w=[("S_bf", dr)])
            def nchunk(dr, i):
                return i if dr == 0 else 31 - i

            def prep_full(dr, i):
                n = nchunk(dr, i)
                prep(dr, n, True, dr)
                first = (n < 16) if dr == 0 else (n >= 16)
                if not first:
                    ogproj(n, dr)

            def step(dr, i):
                n = nchunk(dr, i)
                bo = output(dr, n, dr)
                state(dr, n, True, dr)
                epilogue(dr, n, bo)

            prep_full(0, 0)
            for i in range(32):
                prep_full(1, i)
                step(0, i)
                if i + 1 < 32:
                    prep_full(0, i + 1)
                step(1, i)
            S.barrier()

        if debug in (False, "full", "m1", "m2", "m3", "m4", "m5", "hy"):
            AH = Alloc(PH0)
            zT = AH([128, T], BF16); x0T = AH([128, T], BF16)
            yTg = zT
            off_sig = AH.o
            SIGd = AH([128, 128, 2, 64], BF16)
            SIG = V(off_sig, [128, 65, 128], BF16)
            off_A = AH.o
            Abuf = AH([128, 65, 128], BF16)
            Ybuf = AH([128, 65, 128], BF16)
            Gv = V(off_A, [65, 128, 128], BF16)
            TA = Alloc(off_A)
            hsbs = [TA([128, 8, 512], BF16) for _ in range(2)]
            u3 = [TA([128, 512]) for _ in range(3)]
            assert TA.o <= AH.o
            Hs = AH([128, 65, 128], BF16)
            Ebig_s = AH([128, 65, 128], BF16)
            U_s = AH([65, 64, 2, 64], BF16)
            hdn = AH([64, NFFT], BF16)
            F1_s = AH([128, 128], BF16); Cinv_s = AH([128, 128], BF16)
            woutg = AH([64, 256], BF16)
            wsl = AH([128, 8, 384], BF16)
            Tts = [AH([128, 4, 128]) for _ in range(2)]; T2s = [AH([128, 4, 128])] * 2
            win8 = [AH([128, 8, 128], BF16) for _ in range(2)]; sq8 = [AH([128, 128, 8], BF16)] * 2
            kts = [AH([128, 2, 128], BF16) for _ in range(4)]
            ktc = [0]
            ntt_s = AH([128, 64]); drow = AH([128, 512])
            normT = AH([128, 1]); tmpb = AH([128, 8, 64]); tmpg = AH([128, 8, 64])
            embc = [V(PH0 + i_ * 2048, [33, 512]) for i_ in range(2)]
            argb = [V(PH0 + 4096 + i_ * 2048, [64, 512]) for i_ in range(2)]
            tb_ = [V(PH0 + 8192 + i_ * 2048, [64, 512]) for i_ in range(2)]
            ha = V(off_sig, [64, NFFT]); hb_ = V(off_sig + 32768, [64, NFFT])
            for nm, t_ in [("F1", F1_s), ("Cinv", Cinv_s), ("Ebig", Ebig_s), ("U", U_s), ("ntt", ntt_s)]:
                dma_in(t_, d[nm], [nm + "_s"])
            dma_in(drow, d["delta"][0:1, :].partition_broadcast(128), ["drow"])

            fws = [fw1, fw2, fw3]
            for l in range(3):
                Kp = 33 if l == 0 else 64
                dst = [ha, hb_, hdn][l]
                prev = [None, ha, hb_][l]
                for ch in range(16):
                    cs = slice(ch * 512, (ch + 1) * 512)
                    if l == 0:
                        dma_in(embc[ch % 2], d["emb"][:, cs], [("embc", ch % 2)])
                        rhs = embc[ch % 2]
                        rk = ("embc", ch % 2)
                    else:
                        rhs = prev[:, cs]
                        rk = ("fl", l - 1, ch)
                    b = nbank()
                    op("pe", lambda e, b=b, l=l, Kp=Kp, rhs=rhs: e.matmul(PB(b)[0:64, :], lhsT=fws[l][0:Kp, :], rhs=rhs[0:Kp],
                                                                          start=True, stop=True),
                       r=[rk, "fw1", "fw2", "fw3"], w=[("ps", b)])
                    a_, t_ = argb[ch % 2], tb_[ch % 2]
                    def fr_(e, b=b, l=l, a_=a_, t_=t_):
                        e.tensor_scalar(out=a_, in0=PB(b)[0:64, :], scalar1=fcol[:, 3:4], scalar2=fsc[:, l:l + 1],
                                        op0=ALU.mult, op1=ALU.add)
                        return e
                    op("dve", lambda e, b=b, l=l, a_=a_: e.tensor_scalar(
                        out=a_, in0=PB(b)[0:64, :], scalar1=fcol[:, 3:4], scalar2=fsc[:, l:l + 1], op0=ALU.mult,
                        op1=ALU.add), r=[("ps", b), "fsc", "fcol"], w=[("arg", ch % 2)])
                    op("dve", lambda e, a_=a_, t_=t_: e.tensor_scalar(out=t_, in0=a_, scalar1=1.0 / (2 * PI), scalar2=MAGIC,
                                                                      op0=ALU.mult, op1=ALU.add),
                       r=[("arg", ch % 2)], w=[("tb", ch % 2)])
                    op("dve", lambda e, t_=t_: e.tensor_scalar(out=t_, in0=t_, scalar1=MAGIC, scalar2=None,
                                                               op0=ALU.subtract), r=[("tb", ch % 2)], w=[("tb", ch % 2)])
                    op("dve", lambda e, a_=a_, t_=t_: e.scalar_tensor_tensor(out=a_, in0=t_, scalar=-2 * PI, in1=a_,
                                                                             op0=ALU.mult, op1=ALU.add),
                       r=[("tb", ch % 2), ("arg", ch % 2)], w=[("arg", ch % 2)])
                    op("act", lambda e, a_=a_, dst=dst, cs=cs: e.activation(out=dst[:, cs], in_=a_, func=AF.Sin,
                                                                             scale=0.999999),
                       r=[("arg", ch % 2)], w=[("fl", l, ch)])
            S.barrier()

            hdn_v = hdn.rearrange("p (r w) -> p r w", w=64)

            def load_wsl(g_):
                for j in range(3):
                    for k in range(8):
                        dma_in(wsl[:, k, j * 128:(j + 1) * 128],
                               w_in_v[:, k, j * 512 + g_ * 128: j * 512 + (g_ + 1) * 128], ["wsl"], eng="pool")
            for g in range(4):
                gs = slice(g * 128, (g + 1) * 128)
                S.barrier()
                if g == 0:
                    load_wsl(0)
                for tg in range(8):
                    hsb = hsbs[tg % 2]
                    hkey = ("hsb", tg % 2)
                    dma_in(hsb, hTv[:, :, tg * 512:(tg + 1) * 512], [hkey], r=[("hT_d", tg)])
                    sl = slice(tg * 512, (tg + 1) * 512)
                    for j in range(3):
                        b = nbank()
                        def f(e, b=b, j=j, hsb=hsb):
                            last = None
                            for k in range(8):
                                last = e.matmul(PB(b), lhsT=wsl[:, k, j * 128:(j + 1) * 128], rhs=hsb[:, k, :],
                                                start=(k == 0), stop=(k == 7))
                            return last
                        op("pe", f, r=[hkey, "wsl"], w=[("ps", b)])
                        cw = convwT[:, j * 4 + g, :]
                        cb = convbT[:, j * 4 + g:j * 4 + g + 1]
                        uj = u3[j]
                        op("act", lambda e, b=b, uj=uj, cw=cw, cb=cb: e.activation(
                            out=uj, in_=PB(b), func=AF.Identity, scale=cw[:, 1:2], bias=cb), r=[("ps", b), "convwT", "convbT"],
                           w=[("u3", j)])
                        pv = PB(b).rearrange("p (r w) -> p r w", w=64)
                        uv = uj.rearrange("p (r w) -> p r w", w=64)
                        op("dve", lambda e, pv=pv, uv=uv, cw=cw: e.scalar_tensor_tensor(
                            out=uv[:, :, 1:64], in0=pv[:, :, 0:63], scalar=cw[:, 0:1], in1=uv[:, :, 1:64], op0=ALU.mult,
                            op1=ALU.add), r=[("ps", b), ("u3", j)], w=[("u3", j)])
                        op("dve", lambda e, pv=pv, uv=uv, cw=cw: e.scalar_tensor_tensor(
                            out=uv[:, :, 0:63], in0=pv[:, :, 1:64], scalar=cw[:, 2:3], in1=uv[:, :, 0:63], op0=ALU.mult,
                            op1=ALU.add), r=[("ps", b), ("u3", j)], w=[("u3", j)])
                    op("pool", lambda e, sl=sl: e.tensor_tensor(out=zT[:, sl], in0=u3[1], in1=u3[2], op=ALU.mult),
                       r=[("u3", 1), ("u3", 2)], w=["zT"])
                    op("pool", lambda e, sl=sl: e.tensor_copy(out=x0T[:, sl], in_=u3[0]), r=[("u3", 0)], w=["x0T"])

                S.barrier()
                if g + 1 < 4:
                    load_wsl(g + 1)
                op("pool", lambda e, g=g: e.tensor_copy(out=woutg[:, 0:128], in_=fwout[:, g * 128:(g + 1) * 128]),
                   r=["fwout"], w=["woutg"])
                op("pool", lambda e, g=g: e.tensor_copy(out=woutg[:, 128:256], in_=fwout[:, 512 + g * 128:512 + (g + 1) * 128]),
                   r=["fwout"], w=["woutg"])
                bq = 7
                for wb in range(8):
                    ws_ = wb % 2
                    for j in range(8):
                        w_ = wb * 8 + j
                        op("act", lambda e, w_=w_, j=j, ws_=ws_, gs=gs: e.activation(
                            out=win8[ws_][:, j, :], in_=drow[:, gs], func=AF.Exp, scale=ntt_s[:, w_:w_ + 1]),
                           r=["drow", "ntt_s"], w=[("win8", ws_)])
                    for jp in range(4):
                        w0 = wb * 8 + jp * 2
                        b = nbank()
                        while b == bq:
                            b = nbank()
                        def fm(e, b=b, w0=w0):
                            e.matmul(PB(b)[:, 0:256], lhsT=hdn_v[:, :, w0], rhs=woutg, start=True, stop=True)
                            return e.matmul(PB(b)[:, 256:512], lhsT=hdn_v[:, :, w0 + 1], rhs=woutg, start=True, stop=True)
                        op("pe", fm, r=["woutg"], w=[("ps", b)])
                        ki = ktc[0] % 4
                        ktc[0] += 1
                        kt = kts[ki]
                        def fk(e, b=b, w0=w0, jp=jp, ws_=ws_, kt=kt):
                            pv = PB(b).rearrange("p (w x) -> p w x", w=2)
                            e.tensor_tensor(out=kt[0:64], in0=pv[0:64, :, 0:128], in1=win8[ws_][0:64, jp * 2:jp * 2 + 2, :],
                                            op=ALU.mult)
                            return e.tensor_tensor(out=kt[64:128], in0=pv[64:128, :, 128:256],
                                                   in1=win8[ws_][64:128, jp * 2:jp * 2 + 2, :], op=ALU.mult)
                        op("dve", fk, r=[("ps", b), ("win8", ws_)], w=[("kt", ki)])
                        dstv = SIGd[:, :, 0, w0:w0 + 2].rearrange("p c w -> p w c")
                        if jp % 2 == 0:
                            op("pool", lambda e, dstv=dstv, kt=kt: e.tensor_copy(out=dstv, in_=kt), r=[("kt", ki)],
                               w=[("SIG", wb, jp)])
                        else:
                            op("act", lambda e, dstv=dstv, kt=kt: e.activation(out=dstv, in_=kt, func=AF.Copy),
                               r=[("kt", ki)], w=[("SIG", wb, jp)])
                    op("act", lambda e, wb=wb, ws_=ws_: e.activation(out=sq8[ws_], in_=SIGd[:, :, 0, wb * 8:(wb + 1) * 8],
                                                                     func=AF.Square),
                       r=[("SIG", wb, jp_) for jp_ in range(4)], w=[("sq8", 0), ("SIG", wb)])
                    def fa(e, wb=wb, ws_=ws_):
                        last = None
                        for j in range(8):
                            w_ = wb * 8 + j
                            last = e.matmul(PB(bq)[:, 0:1], lhsT=sq8[ws_][:, :, j], rhs=ones_bf[:, 0:1], start=(w_ == 0),
                                            stop=(w_ == 63))
                        return last
                    op("pe", fa, r=[("sq8", 0), "ones_bf"], w=[("ps", bq)])
                op("act", lambda e: e.activation(out=normT, in_=PB(bq)[:, 0:1], func=AF.Ln, bias=epsc[:, 0:1]),
                   r=[("ps", bq)], w=["normT"])
                op("act", lambda e: e.activation(out=normT, in_=normT, func=AF.Exp, scale=-0.5), r=["normT"], w=["normT"])

                def S1(npart):
                    op("act", lambda e: e.activation(out=SIGd[0:npart, 0:64, 1, :], in_=SIGd[0:npart, 0:64, 0, :],
                                                     func=AF.Copy),
                       r=[("SIG", q_) for q_ in range(8)], w=[("SIGdupA", 0)])
                    op("dve", lambda e: e.tensor_copy(out=SIGd[0:npart, 64:128, 1, :], in_=SIGd[0:npart, 64:128, 0, :]),
                       r=[("SIG", q_) for q_ in range(8)], w=[("SIGdupB", 0)])
                    def z_(e):
                        e.memset(Abuf[64:128, 0:1, :], 0.0)
                        return e.memset(Abuf[64:128, 64:65, :], 0.0)
                    op("pool", z_, w=[("A", c_) for c_ in range(32)])
                    for c4 in range(32):
                        b = nbank()
                        def f(e, b=b, c4=c4):
                            last = None
                            for j in range(4):
                                c = c4 * 4 + j
                                lt = SIGd[0:npart, c, :, :].rearrange("p d w -> p (d w)")
                                last = e.matmul(PB(b)[:, j * 128:(j + 1) * 128], lhsT=lt, rhs=F1_s[0:npart, :],
                                                start=True, stop=True)
                            return last
                        op("pe", f, r=[("SIG", q_) for q_ in range(8)] + ["F1_s", ("SIGdupA", 0), ("SIGdupB", 0)],
                           w=[("ps", b)])
                        pr = PB(b)[0:64, :].rearrange("p (c k) -> p k c", c=4)
                        pi_ = PB(b)[64:128, :].rearrange("p (c k) -> p k c", c=4)
                        op("dve", lambda e, pr=pr, c4=c4: e.tensor_copy(out=Abuf[0:64, 0:65, c4 * 4:(c4 + 1) * 4],
                                                                        in_=pr[:, 0:65, :]), r=[("ps", b)], w=[("A", c4)])
                        op("act", lambda e, pi_=pi_, c4=c4: e.activation(out=Abuf[64:128, 1:64, c4 * 4:(c4 + 1) * 4],
                                                                         in_=pi_[:, 65:128, :], func=AF.Copy),
                           r=[("ps", b)], w=[("A", c4)])

                def S2(consumer):
                    for q in range(17):
                        b = nbank()
                        slots = list(range(4 * q, min(4 * q + 4, 65)))
                        def f(e, b=b, slots=slots):
                            last = None
                            for j, s_ in enumerate(slots):
                                last = e.matmul(PB(b)[:, j * 128:(j + 1) * 128], lhsT=Abuf[:, s_, :], rhs=Ebig_s[:, s_, :],
                                                start=True, stop=True)
                            return last
                        op("pe", f, r=[("A", c_) for c_ in range(32)] + ["Ebig_s"], w=[("ps", b)])
                        consumer(q, b, slots)

                S1(128)
                def cons_h(q, b, slots):
                    ns = len(slots)
                    pv = PB(b)[:, 0:ns * 128].rearrange("p (a b) -> p a b", a=ns)
                    op("act", lambda e: e.activation(out=Hs[:, slots[0]:slots[0] + ns, :], in_=pv, func=AF.Copy),
                       r=[("ps", b)], w=[("Hs", q)])
                S2(cons_h)

                zv = zT.rearrange("p (r w) -> p r w", w=64)
                for q in range(8):
                    b = nbank()
                    def f(e, b=b, q=q):
                        last = None
                        for j in range(8):
                            w_ = q * 8 + j
                            last = e.transpose(out=PB(b, BF16)[0:64, j * 128:(j + 1) * 128], in_=zv[:, :, w_], identity=ident)
                        return last
                    op("pe", f, r=["zT", "ident"], w=[("ps", b)])
                    src = PB(b, BF16)[0:64, :].rearrange("p (a b) -> p a b", a=8)
                    dstv = SIGd[0:64, :, 0, q * 8:(q + 1) * 8].rearrange("p c w -> p w c")
                    if q % 2 == 0:
                        op("act", lambda e, src=src, dstv=dstv: e.activation(out=dstv, in_=src, func=AF.Copy),
                           r=[("ps", b)], w=[("SIG", q)])
                    else:
                        op("dve", lambda e, src=src, dstv=dstv: e.tensor_copy(out=dstv, in_=src),
                           r=[("ps", b)], w=[("SIG", q)])
                S1(64)
                def cons_z(q, b, slots):
                    ns = len(slots)
                    s0 = slots[0]
                    Tt, T2 = Tts[q % 2], T2s[q % 2]
                    kT, kT2 = ("Tt", q % 2), ("T2", 0)
                    pv = PB(b)[:, 0:ns * 128].rearrange("p (a b) -> p a b", a=ns)
                    op("dve", lambda e: e.tensor_tensor(out=Tt[:, 0:ns, :], in0=pv, in1=Hs[:, s0:s0 + ns, :], op=ALU.mult),
                       r=[("ps", b), ("Hs", q)], w=[kT])
                    op("pool", lambda e: e.tensor_tensor(out=Ybuf[:, s0:s0 + ns, 0:64], in0=Tt[:, 0:ns, 0:64],
                                                         in1=Tt[:, 0:ns, 64:128], op=ALU.subtract), r=[kT], w=[("Y", q)])
                    def f(e):
                        e.tensor_tensor(out=T2[:, 0:ns, 0:64], in0=pv[:, :, 0:64], in1=Hs[:, s0:s0 + ns, 64:128], op=ALU.mult)
                        return e.tensor_tensor(out=T2[:, 0:ns, 64:128], in0=pv[:, :, 64:128], in1=Hs[:, s0:s0 + ns, 0:64],
                                               op=ALU.mult)
                    op("dve", f, r=[("ps", b), ("Hs", q)], w=[kT2])
                    op("pool", lambda e: e.tensor_tensor(out=Ybuf[:, s0:s0 + ns, 64:128], in0=T2[:, 0:ns, 0:64],
                                                         in1=T2[:, 0:ns, 64:128], op=ALU.add), r=[kT2], w=[("Y", q)])
                S2(cons_z)
                for q in range(9):
                    b = nbank()
                    slots = list(range(8 * q, min(8 * q + 8, 65)))
                    def f(e, b=b, slots=slots):
                        last = None
                        for j, s_ in enumerate(slots):
                            last = e.transpose(out=PB(b, BF16)[:, j * 128:(j + 1) * 128], in_=Ybuf[:, s_, :], identity=ident)
                        return last
                    op("pe", f, r=[("Y", q_) for q_ in range(17)] + ["ident"], w=[("ps", b)])
                    ns = len(slots)
                    src = PB(b, BF16)[:, 0:ns * 128].rearrange("p (a b) -> p a b", a=ns)
                    if q % 2 == 0:
                        op("act", lambda e, src=src, slots=slots, ns=ns: e.activation(
                            out=SIG[:, slots[0]:slots[0] + ns, :], in_=src, func=AF.Copy), r=[("ps", b)],
                           w=[("SIG", q_) for q_ in range(8)])
                    else:
                        op("dve", lambda e, src=src, slots=slots, ns=ns: e.tensor_copy(
                            out=SIG[:, slots[0]:slots[0] + ns, :], in_=src), r=[("ps", b)],
                           w=[("SIG", q_) for q_ in range(8)])
                allAY = [("A", c_) for c_ in range(32)] + [("Y", q_) for q_ in range(17)]
                for c4 in range(32):
                    b = nbank()
                    def f(e, b=b, c4=c4):
                        last = None
                        for j in range(4):
                            c = c4 * 4 + j
                            last = e.matmul(PB(b)[0:65, j * 128:(j + 1) * 128], lhsT=SIG[:, 0:65, c], rhs=Cinv_s, start=True,
                                            stop=True)
                        return last
                    op("pe", f, r=[("SIG", q_) for q_ in range(8)] + ["Cinv_s"], w=[("ps", b)])
                    src = PB(b)[0:65, :].rearrange("p (c k) -> p k c", c=4)
                    wk_ = allAY if c4 == 0 else []
                    if c4 % 2 == 0:
                        op("act", lambda e, src=src, c4=c4: e.activation(out=Gv[:, :, c4 * 4:(c4 + 1) * 4], in_=src,
                                                                         func=AF.Copy), r=[("ps", b)], w=[("G", c4)] + wk_)
                    else:
                        op("dve", lambda e, src=src, c4=c4: e.tensor_copy(out=Gv[:, :, c4 * 4:(c4 + 1) * 4], in_=src),
                           r=[("ps", b)] + ([("G", 0)] if True else []), w=[("G", c4)])
                def tv(ap, q):
                    return ap.rearrange("p (mb ma) -> p ma mb", ma=64)[:, q * 8:(q + 1) * 8, :]
                for q in range(8):
                    b = nbank()
                    def f(e, b=b, q=q):
                        last = None
                        for j in range(8):
                            ma = q * 8 + j
                            e.matmul(PB(b)[:, j * 64:(j + 1) * 64], lhsT=Gv[:, ma, :], rhs=U_s[:, ma, 0, :], start=True,
                                     stop=False)
                            last = e.matmul(PB(b)[:, j * 64:(j + 1) * 64], lhsT=Gv[:, 64 + ma, :], rhs=U_s[:, ma, 1, :],
                                            start=False, stop=True)
                        return last
                    op("pe", f, r=[("G", c_) for c_ in range(32)] + ["U_s"] + allAY, w=[("ps", b)])
                    op("act", lambda e, q=q, g=g: e.activation(out=tmpb, in_=tv(zT, q), func=AF.Copy,
                                                               scale=hbiasT[:, g:g + 1]), r=["zT", "hbiasT"], w=["tmpb"])
                    op("dve", lambda e, b=b: e.scalar_tensor_tensor(
                        out=tmpg, in0=PB(b).rearrange("p (a b) -> p a b", a=8), scalar=normT[:, 0:1], in1=tmpb, op0=ALU.mult,
                        op1=ALU.add), r=[("ps", b), "normT", "tmpb"], w=["tmpg"])
                    op("pool", lambda e, q=q: e.tensor_tensor(out=tv(yTg, q), in0=tmpg, in1=tv(x0T, q), op=ALU.mult),
                       r=["tmpg", "x0T"], w=["yTg"])
                op("sp", lambda e, gs=gs: e.dma_start(out=yT_d[gs, :], in_=yTg), r=["yTg"], w=[("yT_hy", g)], dma=True)
            S.barrier()

        if debug in ("gla", "hy"):
            S.emit(sems, block)
            return nc

        AM = Alloc(16384)
        woutp = AM([128, 8, D], BF16)
        w1s = AM([128, 8, 4 * D], BF16)
        w2s = AM([128, 32, D], BF16)
        off_hid = AM.o
        hidT = AM([128, 32, 256], BF16)
        off_xtm = AM.o
        stgs = [V(off_hid, [128, 4, D]), V(off_xtm, [128, 4, D])]
        xtm = [AM([128, D]) for _ in range(4)]
        yg = AM([128, 8, 256], BF16)
        h2Ts = [AM([128, 8, 256], BF16) for _ in range(2)]
        rl = [AM([128, 512], BF16) for _ in range(2)]
        ssq2 = AM([128, 64]); rs2 = AM([128, 64])
        xnm = V(PA_g1_off, [128, 2, D], BF16)
        junkB = V(PA_g1_off + 4096, [128, D], BF16)
        w1_v = d["w_mlp1"].rearrange("(k p) n -> p k n", p=128)
        w2_v = d["w_mlp2"].rearrange("(k p) n -> p k n", p=128)
        wo_v = d["w_out"].rearrange("(k p) n -> p k n", p=128)
        op("pool", lambda e: e.memset(ssq2, 0.0), w=["ssq2"])
        for k in range(8):
            dma_in(w1s[:, k, :], w1_v[:, k, :], [("w1s", k)], eng="pool")
        for rd in range(10):
            stg = stgs[rd % 2]
            sk = ("stg", rd % 2)
            if rd < 2:
                dma_in(stg, wo_v[:, rd * 4:(rd + 1) * 4, :], [sk])
            else:
                dma_in(stg, w2_v[:, (rd - 2) * 4:(rd - 1) * 4, :], [sk])
            for kk in range(4):
                eng = "dve" if kk % 2 == 0 else "pool"
                if rd < 2:
                    op(eng, lambda e, rd=rd, kk=kk, stg=stg: e.tensor_tensor(out=woutp[:, rd * 4 + kk, :], in0=stg[:, kk, :],
                                                                             in1=g1row, op=ALU.mult), r=[sk], w=["woutp"])
                else:
                    op(eng, lambda e, rd=rd, kk=kk, stg=stg: e.tensor_tensor(out=w2s[:, (rd - 2) * 4 + kk, :], in0=stg[:, kk, :],
                                                                             in1=g2row, op=ALU.mult), r=[sk], w=["w2s"])
        S.barrier()

        def rms_stats(src, col, sq_key, junk_ap, junk_key):
            op("act", lambda e: e.activation(out=junk_ap, in_=src, func=AF.Square, accum_out=ssq2[:, col:col + 1]),
               r=[sq_key, "ssq2"], w=[junk_key, ("ssq2", col)])
            op("act", lambda e: e.activation(out=rs2[:, col:col + 1], in_=ssq2[:, col:col + 1], func=AF.Ln, scale=1.0 / D,
                                             bias=epsc[:, 0:1]), r=[("ssq2", col)], w=[("rs2", col)])
            op("act", lambda e: e.activation(out=rs2[:, col:col + 1], in_=rs2[:, col:col + 1], func=AF.Exp, scale=-0.5),
               r=[("rs2", col)], w=[("rs2", col)])

        tbs = {}

        def stageA1(gq):
            dma_in(yg, yT_v[:, :, gq * 256:(gq + 1) * 256], ["yg"])
            for i in range(2):
                ti = gq * 2 + i
                slot = ti % 4
                xb = xtm[slot]
                dma_in(xb, d["x"][ti * 128:(ti + 1) * 128, :], [("xtm", slot)])
                bh = [nbank(), nbank()]
                def f(e, i=i, bh=bh):
                    last = None
                    for h in range(2):
                        for k in range(8):
                            last = e.matmul(PB(bh[h]), lhsT=yg[:, k, i * 128:(i + 1) * 128],
                                            rhs=woutp[:, k, h * 512:(h + 1) * 512], start=(k == 0), stop=(k == 7))
                    return last
                op("pe", f, r=["yg", "woutp"], w=[("ps", bh[0]), ("ps", bh[1])])
                for h in range(2):
                    op("dve", lambda e, h=h, bh=bh, xb=xb: e.tensor_tensor(
                        out=xb[:, h * 512:(h + 1) * 512], in0=PB(bh[h]), in1=xb[:, h * 512:(h + 1) * 512], op=ALU.add),
                       r=[("ps", bh[h]), ("xtm", slot)], w=[("xtm", slot)])
                rms_stats(xb, ti, ("xtm", slot), xnm[:, i, :], ("xnm", i))
                op("dve", lambda e, i=i, xb=xb, ti=ti: e.tensor_scalar(out=xnm[:, i, :], in0=xb, scalar1=rs2[:, ti:ti + 1],
                                                                       scalar2=None, op0=ALU.mult),
                   r=[("xtm", slot), ("rs2", ti)], w=[("xnm", i)])

        def stageA2(gq):
            tb = [nbank(), nbank()]
            hb = h2Ts[gq % 2]
            hk = ("h2T", gq % 2)
            for i in range(2):
                def ft(e, i=i, tb=tb):
                    last = None
                    for k in range(8):
                        last = e.transpose(out=PB(tb[k // 4], BF16)[:, (k % 4) * 256 + i * 128:(k % 4) * 256 + (i + 1) * 128],
                                           in_=xnm[:, i, k * 128:(k + 1) * 128], identity=ident)
                    return last
                op("pe", ft, r=[("xnm", i), "ident"], w=[("ps", tb[0]), ("ps", tb[1])])
            for k in range(8):
                pv = PB(tb[k // 4], BF16)[:, (k % 4) * 256:(k % 4 + 1) * 256]
                if k // 4 == 0:
                    op("act", lambda e, k=k, pv=pv, hb=hb: e.activation(out=hb[:, k, :], in_=pv, func=AF.Identity,
                                                                        scale=msc2[:, k:k + 1], bias=msh2[:, k:k + 1]),
                       r=[("ps", tb[k // 4])], w=[(hk, k)])
                else:
                    op("dve", lambda e, k=k, pv=pv, hb=hb: e.tensor_scalar(out=hb[:, k, :], in0=pv, scalar1=msc2[:, k:k + 1],
                                                                           scalar2=msh2[:, k:k + 1], op0=ALU.mult, op1=ALU.add),
                       r=[("ps", tb[k // 4])], w=[(hk, k)])

        def stageB1(gq):
            hb = h2Ts[gq % 2]
            hk = ("h2T", gq % 2)
            for hc2 in range(16):
                b = nbank()
                def f(e, b=b, hc2=hc2):
                    last = None
                    for j in range(2):
                        hcx = hc2 * 2 + j
                        for k in range(8):
                            last = e.matmul(PB(b)[:, j * 256:(j + 1) * 256], lhsT=w1s[:, k, hcx * 128:(hcx + 1) * 128],
                                            rhs=hb[:, k, :], start=(k == 0), stop=(k == 7))
                    return last
                op("pe", f, r=[(hk, k) for k in range(8)] + [("w1s", k) for k in range(8)], w=[("ps", b)])
                rb = rl[hc2 % 2]
                op("act", lambda e, b=b, rb=rb: e.activation(out=rb, in_=PB(b), func=AF.Relu), r=[("ps", b)],
                   w=[("rl", hc2 % 2)])
                op("pool" if hc2 % 2 else "dve", lambda e, rb=rb, hc2=hc2: e.tensor_tensor(
                    out=hidT[:, hc2 * 2:(hc2 + 1) * 2, :], in0=rb.rearrange("p (a b) -> p a b", a=2),
                    in1=rb.rearrange("p (a b) -> p a b", a=2), op=ALU.mult), r=[("rl", hc2 % 2)], w=[("hidT", hc2)])

        def stageB2(gq):
            for i in range(2):
                ti = gq * 2 + i
                slot = ti % 4
                xb = xtm[slot]
                bh = [nbank(), nbank()]
                def f(e, i=i, bh=bh):
                    last = None
                    for h in range(2):
                        for k in range(32):
                            last = e.matmul(PB(bh[h]), lhsT=hidT[:, k, i * 128:(i + 1) * 128],
                                            rhs=w2s[:, k, h * 512:(h + 1) * 512], start=(k == 0), stop=(k == 31))
                    return last
                op("pe", f, r=[("hidT", q_) for q_ in range(16)] + ["w2s"], w=[("ps", bh[0]), ("ps", bh[1])])
                for h in range(2):
                    op("dve", lambda e, h=h, bh=bh, xb=xb: e.tensor_tensor(
                        out=xb[:, h * 512:(h + 1) * 512], in0=PB(bh[h]), in1=xb[:, h * 512:(h + 1) * 512], op=ALU.add),
                       r=[("ps", bh[h]), ("xtm", slot)], w=[("xtm", slot)])
                rms_stats(xb, 32 + ti, ("xtm", slot), junkB, "junkB")
                op("dve", lambda e, xb=xb, ti=ti: e.tensor_scalar(out=xb, in0=xb, scalar1=rs2[:, 32 + ti:33 + ti],
                                                                   scalar2=None, op0=ALU.mult),
                   r=[("xtm", slot), ("rs2", 32 + ti)], w=[("xtm", slot)])
                op("pool", lambda e, xb=xb: e.tensor_tensor(out=xb, in0=xb, in1=nfrow, op=ALU.mult),
                   r=[("xtm", slot), "nfrow"], w=[("xtm", slot)])
                op("sp", lambda e, xb=xb, ti=ti: e.dma_start(out=out_d[ti * 128:(ti + 1) * 128, :], in_=xb),
                   r=[("xtm", slot)], w=[("out", ti)], dma=True)

        stageA1(0)
        stageA2(0)
        for gq in range(16):
            stageB1(gq)
            if gq + 1 < 16:
                stageA1(gq + 1)
            stageB2(gq)
            if gq + 1 < 16:
                stageA2(gq + 1)
        if debug == "m5":
            S.barrier()
            op("sp", lambda e: e.dma_start(out=out_d[0:128, :], in_=nfrow), dma=True)
            op("sp", lambda e: e.dma_start(out=out_d[128:256, :], in_=g1row), dma=True)
        if debug == "m4":
            S.barrier()
            op("sp", lambda e: e.dma_start(out=out_d[3968:4096, 0:64], in_=ssq2), dma=True)
            op("sp", lambda e: e.dma_start(out=out_d[3968:4096, 64:128], in_=rs2), dma=True)
        S.emit(sems, block)
    return nc


def make_in_maps(inputs):
    f = lambda a: np.ascontiguousarray(np.asarray(a, dtype=np.float32))
    c = host_constants()
    x = f(inputs["x"]); cvec = f(inputs["c"]); ctx = f(inputs["ctx"]); c_ctx = f(inputs["c_ctx"])
    common = {}
    common["w_ada"] = f(inputs["w_ada"][0])
    b_ada = f(inputs["b_ada"][0])
    common["b_adaT"] = np.ascontiguousarray(b_ada.reshape(48, 128).T)
    common["b_ada"] = b_ada.reshape(1, -1)
    common["nmixT"] = np.ascontiguousarray(f(inputs["norm_mix"][0]).reshape(8, 128).T)
    common["nmlpT"] = np.ascontiguousarray(f(inputs["norm_mlp"][0]).reshape(8, 128).T)
    common["nfin"] = f(inputs["norm_final"]).reshape(1, -1)
    common["w_in"] = f(inputs["w_in"][0])
    cw = f(inputs["conv_w"][0])
    common["convwT"] = np.ascontiguousarray(cw.reshape(3, 12, 128).transpose(2, 1, 0))
    common["convbT"] = np.ascontiguousarray(f(inputs["conv_b"][0]).reshape(12, 128).T)
    common["fw1"] = f(inputs["filt_w1"][0]); common["fw2"] = f(inputs["filt_w2"][0]); common["fw3"] = f(inputs["filt_w3"][0])
    common["fwout"] = f(inputs["filt_wout"][0])
    common["fcol"] = np.ascontiguousarray(np.stack([f(inputs["filt_b1"][0]), f(inputs["filt_b2"][0]),
                                                    f(inputs["filt_b3"][0]), f(inputs["filt_freq"][0])], axis=1))
    common["hbiasT"] = np.ascontiguousarray(f(inputs["hyena_bias"][0]).reshape(4, 128).T)
    gkw = np.zeros((48, 256), np.float32)
    gkw[0:16] = f(inputs["gk_w_fwd"][0]); gkw[32:48] = f(inputs["gk_w_bwd"][0])
    common["gkw"] = gkw
    common["gkb"] = np.concatenate([f(inputs["gk_b_fwd"][0]), f(inputs["gk_b_bwd"][0])]).reshape(1, 512)
    common["gnT"] = f(inputs["gla_norm"][0]).reshape(128, 1)
    common["w_out"] = f(inputs["w_out"][0]); common["w_mlp1"] = f(inputs["w_mlp1"][0]); common["w_mlp2"] = f(inputs["w_mlp2"][0])
    for k in ["F1", "Ebig", "Cinv", "U", "emb", "ntt", "delta", "triF", "triB", "triFb", "triBb", "maskF", "maskB", "ident"]:
        common[k] = c[k]
    maps = []
    for b in range(8):
        m = dict(common)
        m["x"] = x[b]
        m["ctx"] = ctx[b]
        cc = np.stack([cvec[b], c_ctx], axis=1)
        m["cc"] = np.ascontiguousarray(cc.reshape(8, 128, 2).transpose(1, 0, 2))
        maps.append(m)
    return maps


_NC_CACHE = {}


def kernel(**inputs):
    if "nc" not in _NC_CACHE:
        _NC_CACHE["nc"] = build(debug=DEBUG)
    nc = _NC_CACHE["nc"]
    maps = make_in_maps(inputs)
    res = run_bass_kernel_spmd(nc, maps, core_ids=list(range(8)))
    out = np.stack([np.asarray(r["out"], dtype=np.float32) for r in res.results], axis=0)
    return out
```
